# Optimizing a Trainium2 kernel written in Bass

```python
import jax, jax.numpy as jnp
from jax import lax
import numpy as np

D_MODEL = 1024
BATCH = 8
SEQ = 4096
DEPTH = 2
DEC_BATCH = 4
DEC_SEQ = 8192
PAST_LEN = 128

HEAD_DIM = 64
ROPE_THETA = 10000.0
EPS = 1e-6
NEG = -1e30
A_Q_HEADS = 8
A_KV_HEADS = 2
A_HALF_WINDOW = 128
B_HEADS = 4
B_CONFIGS = ((128, 1), (512, 4), (2048, 16))
N_BRANCH = len(B_CONFIGS)
A_Q_DIM = A_Q_HEADS * HEAD_DIM
A_KV_DIM = A_KV_HEADS * HEAD_DIM
B_DIM = B_HEADS * HEAD_DIM
IN0_DIM = A_Q_DIM + 2 * A_KV_DIM + N_BRANCH * 3 * B_DIM
OUT0_DIM = A_Q_DIM + B_DIM
C_HEADS = 16
C_NOPE = 64
C_ROPE = 32
C_QK = C_NOPE + C_ROPE
C_V = 64
C_Q_LORA = 256
C_KV_LORA = 256
IN1_DIM = C_Q_LORA + C_KV_LORA + C_ROPE
Q_BLOCK = 128
D_FF = 2816
CONV_W = 3
N_EVEN = (DEPTH + 1) // 2
N_ODD = DEPTH // 2

kernel_name = "hybrid_window_dilated_mla_encoder"


def rms_norm(x, g):
    xf = x.astype(jnp.float32)
    y = xf * lax.rsqrt(jnp.mean(xf * xf, axis=-1, keepdims=True) + EPS)
    return (y * g.astype(jnp.float32)).astype(x.dtype)


def rope(x):
    S, Dr = x.shape[1], x.shape[-1]
    inv = jnp.power(ROPE_THETA, -jnp.arange(0, Dr, 2, dtype=jnp.float32) / Dr)
    ang = jnp.arange(S, dtype=jnp.float32)[:, None] * inv[None, :]
    cos = jnp.cos(ang)[None, :, None, :]
    sin = jnp.sin(ang)[None, :, None, :]
    xf = x.astype(jnp.float32)
    x1, x2 = xf[..., :Dr // 2], xf[..., Dr // 2:]
    return jnp.concatenate([x1 * cos - x2 * sin, x2 * cos + x1 * sin], axis=-1).astype(x.dtype)


def local_attention(q, k, v, half, sink=None):
    Bt, L, Hq, Dh = q.shape
    Hkv = k.shape[2]
    G = Hq // Hkv
    blk = half
    nb = -(-L // blk)
    Lp = nb * blk
    qp = jnp.pad(q, ((0, 0), (0, Lp - L), (0, 0), (0, 0))).reshape(Bt, nb, blk, Hkv, G, Dh)
    kv_pad = ((0, 0), (blk, Lp - L + blk), (0, 0), (0, 0))
    kp = jnp.pad(k, kv_pad).reshape(Bt, nb + 2, blk, Hkv, Dh)
    vp = jnp.pad(v, kv_pad).reshape(Bt, nb + 2, blk, Hkv, Dh)
    kw = jnp.concatenate([kp[:, :-2], kp[:, 1:-1], kp[:, 2:]], axis=2)
    vw = jnp.concatenate([vp[:, :-2], vp[:, 1:-1], vp[:, 2:]], axis=2)
    qpos = jnp.arange(nb)[:, None] * blk + jnp.arange(blk)[None, :]
    kpos = (jnp.arange(nb)[:, None] - 1) * blk + jnp.arange(3 * blk)[None, :]
    kpos_b = kpos[:, None, :]
    mask = (jnp.abs(qpos[:, :, None] - kpos_b) <= half) & (kpos_b >= 0) & (kpos_b < L)
    s = jnp.einsum('bnqhgd,bnkhd->bnhgqk', qp, kw, preferred_element_type=jnp.float32) * (Dh ** -0.5)
    s = jnp.where(mask[None, :, None, None], s, NEG)
    m = jnp.max(s, axis=-1)
    if sink is not None:
        sk = sink.astype(jnp.float32).reshape(1, 1, Hkv, G, 1)
        m = jnp.maximum(m, sk)
    p = jnp.exp(s - m[..., None])
    den = jnp.sum(p, axis=-1)
    if sink is not None:
        den = den + jnp.exp(sk - m)
    o = jnp.einsum('bnhgqk,bnkhd->bnqhgd', p.astype(v.dtype), vw, preferred_element_type=jnp.float32)
    o = o / jnp.transpose(den, (0, 1, 4, 2, 3))[..., None]
    lse = jnp.transpose(m + jnp.log(den), (0, 1, 4, 2, 3)).reshape(Bt, Lp, Hq)[:, :L]
    o = o.reshape(Bt, Lp, Hq, Dh)[:, :L]
    return o.astype(q.dtype), lse


def dilated_attention(q, k, v, window, r):
    Bt, S, H, Dh = q.shape
    Ls = S // r

    def fold(t):
        return jnp.transpose(t.reshape(Bt, Ls, r, H, Dh), (0, 2, 1, 3, 4)).reshape(Bt * r, Ls, H, Dh)

    o, lse = local_attention(fold(q), fold(k), fold(v), (window // 2) // r)
    o = jnp.transpose(o.reshape(Bt, r, Ls, H, Dh), (0, 2, 1, 3, 4)).reshape(Bt, S, H, Dh)
    lse = jnp.transpose(lse.reshape(Bt, r, Ls, H), (0, 2, 1, 3)).reshape(Bt, S, H)
    return o, lse


def dense_attention(q, k, v):
    Bt, S, H, Dq = q.shape
    Dv = v.shape[-1]
    nq = S // Q_BLOCK
    qb = jnp.transpose(q.reshape(Bt, nq, Q_BLOCK, H, Dq), (1, 0, 2, 3, 4))
    scale = Dq ** -0.5

    def one_block(qi):
        s = jnp.einsum('bqhd,bkhd->bhqk', qi, k, preferred_element_type=jnp.float32) * scale
        p = jax.nn.softmax(s, axis=-1)
        o = jnp.einsum('bhqk,bkhd->bqhd', p.astype(v.dtype), v, preferred_element_type=jnp.float32)
        return o.astype(q.dtype)

    o = lax.map(one_block, qb)
    return jnp.transpose(o, (1, 0, 2, 3, 4)).reshape(Bt, S, H, Dv)


def hybrid_ab(h, w_in, a_q_gain, a_k_gain, a_sink, b_q_gain, b_k_gain, w_out):
    Bt, S, _ = h.shape
    proj = h @ w_in
    aq, ak, av, bqkv = jnp.split(proj, [A_Q_DIM, A_Q_DIM + A_KV_DIM, A_Q_DIM + 2 * A_KV_DIM], axis=-1)
    aq = rope(rms_norm(aq.reshape(Bt, S, A_Q_HEADS, HEAD_DIM), a_q_gain))
    ak = rope(rms_norm(ak.reshape(Bt, S, A_KV_HEADS, HEAD_DIM), a_k_gain))
    av = av.reshape(Bt, S, A_KV_HEADS, HEAD_DIM)
    a_out, _ = local_attention(aq, ak, av, A_HALF_WINDOW, a_sink)
    bqkv = bqkv.reshape(Bt, S, N_BRANCH, 3, B_HEADS, HEAD_DIM)
    outs, lses = [], []
    for g, (window, r) in enumerate(B_CONFIGS):
        bq = rope(rms_norm(bqkv[:, :, g, 0], b_q_gain[g]))
        bk = rope(rms_norm(bqkv[:, :, g, 1], b_k_gain[g]))
        o, lse = dilated_attention(bq, bk, bqkv[:, :, g, 2], window, r)
        outs.append(o)
        lses.append(lse)
    wts = jax.nn.softmax(jnp.stack(lses, axis=0), axis=0)
    b_out = jnp.sum(wts[..., None] * jnp.stack(outs, axis=0).astype(jnp.float32), axis=0).astype(h.dtype)
    cat = jnp.concatenate([a_out.reshape(Bt, S, A_Q_DIM), b_out.reshape(Bt, S, B_DIM)], axis=-1)
    return cat @ w_out


def mla(h, w_in, q_lora_gain, w_uq, kv_gain, w_ukv, q_gain, k_gain, w_out):
    Bt, S, _ = h.shape
    proj = h @ w_in
    cq, ckv, k_rope = jnp.split(proj, [C_Q_LORA, C_Q_LORA + C_KV_LORA], axis=-1)
    q = (rms_norm(cq, q_lora_gain) @ w_uq).reshape(Bt, S, C_HEADS, C_QK)
    kv = (rms_norm(ckv, kv_gain) @ w_ukv).reshape(Bt, S, C_HEADS, C_NOPE + C_V)
    k_nope, v = kv[..., :C_NOPE], kv[..., C_NOPE:]
    k = jnp.concatenate([k_nope, jnp.broadcast_to(k_rope[:, :, None, :], (Bt, S, C_HEADS, C_ROPE))], axis=-1)
    q = rms_norm(q, q_gain)
    k = rms_norm(k, k_gain)
    q = jnp.concatenate([q[..., :C_NOPE], rope(q[..., C_NOPE:])], axis=-1)
    k = jnp.concatenate([k[..., :C_NOPE], rope(k[..., C_NOPE:])], axis=-1)
    o = dense_attention(q, k, v)
    return o.reshape(Bt, S, C_HEADS * C_V) @ w_out


def conv_ffn(h, w_up, conv_w, conv_b, w_down):
    gate, val = jnp.split(h @ w_up, 2, axis=-1)
    gate = lax.conv_general_dilated(gate, conv_w[:, None, :].astype(gate.dtype), window_strides=(1,),
                                    padding='SAME', dimension_numbers=('NWC', 'WIO', 'NWC'),
                                    feature_group_count=D_FF) + conv_b
    return (jax.nn.gelu(gate) * val) @ w_down


def trunk(x, e_norm, e_w_in, e_a_q_gain, e_a_k_gain, e_a_sink, e_b_q_gain, e_b_k_gain, e_w_out,
          o_norm, o_w_in, o_q_lora_gain, o_w_uq, o_kv_gain, o_w_ukv, o_q_gain, o_k_gain, o_w_out,
          f_norm, f_w_up, f_conv_w, f_conv_b, f_w_down):
    for layer in range(DEPTH):
        i = layer // 2
        if layer % 2 == 0:
            x = x + hybrid_ab(rms_norm(x, e_norm[i]), e_w_in[i], e_a_q_gain[i], e_a_k_gain[i], e_a_sink[i],
                              e_b_q_gain[i], e_b_k_gain[i], e_w_out[i])
        else:
            x = x + mla(rms_norm(x, o_norm[i]), o_w_in[i], o_q_lora_gain[i], o_w_uq[i], o_kv_gain[i],
                        o_w_ukv[i], o_q_gain[i], o_k_gain[i], o_w_out[i])
        x = x + conv_ffn(rms_norm(x, f_norm[layer]), f_w_up[layer], f_conv_w[layer], f_conv_b[layer],
                         f_w_down[layer])
    return x


def setup_inputs(seed: int = 0) -> dict:
    key = jax.random.key(seed)
    ks = jax.random.split(key, 32)
    f32 = jnp.float32

    def w(k, shape, fan_in):
        return jax.random.normal(k, shape, f32) * (fan_in ** -0.5)

    def g(k, shape):
        return 1.0 + 0.02 * jax.random.normal(k, shape, f32)

    return {
        "x_prompt": jax.random.normal(ks[0], (BATCH, SEQ, D_MODEL), f32),
        "x_sample": jax.random.normal(ks[1], (DEC_BATCH, DEC_SEQ, D_MODEL), f32),
        "e_norm": g(ks[2], (N_EVEN, D_MODEL)),
        "e_w_in": w(ks[3], (N_EVEN, D_MODEL, IN0_DIM), D_MODEL),
        "e_a_q_gain": g(ks[4], (N_EVEN, HEAD_DIM)),
        "e_a_k_gain": g(ks[5], (N_EVEN, HEAD_DIM)),
        "e_a_sink": 0.5 * jax.random.normal(ks[6], (N_EVEN, A_Q_HEADS), f32),
        "e_b_q_gain": g(ks[7], (N_EVEN, N_BRANCH, HEAD_DIM)),
        "e_b_k_gain": g(ks[8], (N_EVEN, N_BRANCH, HEAD_DIM)),
        "e_w_out": w(ks[9], (N_EVEN, OUT0_DIM, D_MODEL), OUT0_DIM),
        "o_norm": g(ks[10], (N_ODD, D_MODEL)),
        "o_w_in": w(ks[11], (N_ODD, D_MODEL, IN1_DIM), D_MODEL),
        "o_q_lora_gain": g(ks[12], (N_ODD, C_Q_LORA)),
        "o_w_uq": w(ks[13], (N_ODD, C_Q_LORA, C_HEADS * C_QK), C_Q_LORA),
        "o_kv_gain": g(ks[14], (N_ODD, C_KV_LORA)),
        "o_w_ukv": w(ks[15], (N_ODD, C_KV_LORA, C_HEADS * (C_NOPE + C_V)), C_KV_LORA),
        "o_q_gain": g(ks[16], (N_ODD, C_QK)),
        "o_k_gain": g(ks[17], (N_ODD, C_QK)),
        "o_w_out": w(ks[18], (N_ODD, C_HEADS * C_V, D_MODEL), C_HEADS * C_V),
        "f_norm": g(ks[19], (DEPTH, D_MODEL)),
        "f_w_up": w(ks[20], (DEPTH, D_MODEL, 2 * D_FF), D_MODEL),
        "f_conv_w": w(ks[21], (DEPTH, CONV_W, D_FF), CONV_W),
        "f_conv_b": 0.02 * jax.random.normal(ks[22], (DEPTH, D_FF), f32),
        "f_w_down": w(ks[23], (DEPTH, D_FF, D_MODEL), D_FF),
    }


def reference(x_prompt, x_sample, e_norm, e_w_in, e_a_q_gain, e_a_k_gain, e_a_sink, e_b_q_gain, e_b_k_gain,
              e_w_out, o_norm, o_w_in, o_q_lora_gain, o_w_uq, o_kv_gain, o_w_ukv, o_q_gain, o_k_gain, o_w_out,
              f_norm, f_w_up, f_conv_w, f_conv_b, f_w_down):
    y_prompt = trunk(x_prompt, e_norm, e_w_in, e_a_q_gain, e_a_k_gain, e_a_sink, e_b_q_gain, e_b_k_gain, e_w_out,
                     o_norm, o_w_in, o_q_lora_gain, o_w_uq, o_kv_gain, o_w_ukv, o_q_gain, o_k_gain, o_w_out,
                     f_norm, f_w_up, f_conv_w, f_conv_b, f_w_down)
    y_sample = trunk(x_sample, e_norm, e_w_in, e_a_q_gain, e_a_k_gain, e_a_sink, e_b_q_gain, e_b_k_gain, e_w_out,
                     o_norm, o_w_in, o_q_lora_gain, o_w_uq, o_kv_gain, o_w_ukv, o_q_gain, o_k_gain, o_w_out,
                     f_norm, f_w_up, f_conv_w, f_conv_b, f_w_down)
    return (y_prompt, y_sample)
```

```python
import os
import numpy as np
from contextlib import ExitStack
import concourse.bass as bass
import concourse.mybir as mybir
from concourse.bass_utils import run_bass_kernel_spmd

F32 = mybir.dt.float32
BF16 = mybir.dt.bfloat16
AF = mybir.ActivationFunctionType
ALU = mybir.AluOpType

D = 1024
EPS = 1e-6
DFF = 2816
NFC = DFF // 128
PAD = 1024
NQK = 18
NEGB = -30000.0

ENGS = ("pe", "act", "dve", "pool", "sp")
STRICT = not os.environ.get("NOSTRICT")


class Sched:
    def __init__(self, nc, es):
        self.nc = nc
        self.es = es
        self.sem = {e: es.enter_context(nc.semaphore("sem_" + e)) for e in ("pe", "act", "dve", "pool")}
        self.sigcnt = {e: 0 for e in self.sem}
        self.chans = {}
        self.waited = {e: {} for e in ENGS}
        self.ops = []
        self.last_w = {}
        self.readers = {}
        self.bar_val = 0
        self.n_inst = 0

    def chan(self, name):
        if name not in self.chans:
            self.chans[name] = dict(sem=self.es.enter_context(self.nc.semaphore("ch_" + name)), issued=0, name=name)
        return self.chans[name]

    def _add(self, o, reads, writes):
        idx = len(self.ops)
        deps = []
        seen = set()

        def add_dep(d, raw):
            if d in seen:
                return
            seen.add(d)
            deps.append((d, raw))

        for k in reads:
            if k in self.last_w:
                add_dep(self.last_w[k], True)
        for k in writes:
            if k in self.last_w:
                lw = self.ops[self.last_w[k]]
                if not (o["dma"] and lw["dma"] and lw["chan"] is o["chan"] and lw["eng"] == o["eng"]):
                    add_dep(self.last_w[k], False)
            for r in self.readers.get(k, ()):
                add_dep(r, False)
        for k in reads:
            self.readers.setdefault(k, []).append(idx)
        for k in writes:
            self.last_w[k] = idx
            self.readers[k] = []
        need = []
        for d, raw in deps:
            od = self.ops[d]
            if od["dma"]:
                need.append(("ch", od["chan"], od["chan"]["issued"]))
            else:
                if od["eng"] == o["eng"] and not o["dma"]:
                    if o["eng"] == "pe" or (not raw and not STRICT):
                        continue
                od["sig"] = True
                need.append(("op", d))
        o["need"] = need
        self.ops.append(o)
        return idx

    def op(self, eng, fn, reads=(), writes=()):
        o = dict(eng=eng, fn=fn, dma=False, sig=False)
        return self._add(o, reads, writes)

    def dma(self, eng, fn, reads=(), writes=(), chan=None):
        ch = self.chan(chan)
        o = dict(eng=eng, fn=fn, dma=True, sig=False, chan=ch)
        idx = self._add(o, reads, writes)
        ch["issued"] += 1
        return idx

    def alias(self, key, idx):
        self.last_w[key] = idx
        self.readers[key] = []

    def emit_pass(self):
        nc = self.nc
        ops = self.ops
        last = {}
        for i, o in enumerate(ops):
            if not o["dma"]:
                last[o["eng"]] = i
        for e, i in last.items():
            ops[i]["sig"] = True
        for o in ops:
            if not o["dma"] and o["sig"]:
                self.sigcnt[o["eng"]] += 1
                o["sigval"] = self.sigcnt[o["eng"]]
        per = {e: [o for o in ops if o["eng"] == e] for e in ENGS}
        bar_prev = self.bar_val
        self.sigcnt["dve"] += 1
        bar_new = self.sigcnt["dve"]
        sched = self

        def wait(e, eng, sem, key, val):
            w = sched.waited[e]
            if w.get(key, 0) >= val:
                return
            w[key] = val
            eng.wait_ge(sem, val)
            sched.n_inst += 1

        def run_engine(e, eng):
            if bar_prev > 0:
                wait(e, eng, sched.sem["dve"], "dve", bar_prev)
            for o in per[e]:
                for n in o["need"]:
                    if n[0] == "ch":
                        wait(e, eng, n[1]["sem"], "ch_" + n[1]["name"], 16 * n[2])
                    else:
                        od = ops[n[1]]
                        wait(e, eng, sched.sem[od["eng"]], od["eng"], od["sigval"])
                ins = o["fn"](eng)
                sched.n_inst += 1
                if o["dma"]:
                    ins.then_inc(o["chan"]["sem"], 16)
                elif o["sig"]:
                    ins.then_inc(sched.sem[e], 1)
            if e == "dve":
                for e2 in ("pe", "act", "pool"):
                    if sched.sigcnt[e2] > 0:
                        wait(e, eng, sched.sem[e2], e2, sched.sigcnt[e2])
                for ch in sched.chans.values():
                    if ch["issued"] > 0:
                        wait(e, eng, ch["sem"], "ch_" + ch["name"], 16 * ch["issued"])
                eng.memset(sched.bar_tile[0:1, 0:1], 0.0).then_inc(sched.sem["dve"], 1)
            else:
                pass

        with nc.Block() as block:
            @block.tensor
            def _(eng):
                run_engine("pe", eng)

            @block.scalar
            def _(eng):
                run_engine("act", eng)

            @block.vector
            def _(eng):
                run_engine("dve", eng)

            @block.gpsimd
            def _(eng):
                run_engine("pool", eng)

            @block.sync
            def _(eng):
                run_engine("sp", eng)

        self.bar_val = bar_new
        self.ops = []
        self.last_w = {}
        self.readers = {}

    def final_wait(self):
        nc = self.nc
        sched = self
        with nc.Block() as block:
            @block.sync
            def _(eng):
                eng.wait_ge(sched.sem["dve"], sched.bar_val)

            @block.tensor
            def _(eng):
                eng.wait_ge(sched.sem["dve"], sched.bar_val)

            @block.scalar
            def _(eng):
                eng.wait_ge(sched.sem["dve"], sched.bar_val)

            @block.gpsimd
            def _(eng):
                eng.wait_ge(sched.sem["dve"], sched.bar_val)


CB_IDENT, CB_B64, CB_R64, CB_ONES, CB_R96, CB_E, CB_GE, CB_LE, CB_GEF, CB_LEL, CB_GEB, CB_LEB, CB_GEAB, CB_LEAB = range(14)
NCB = 14


def build_program(T, debug=False, passes=None):
    nc = bass.Bass("TRN2", target_bir_lowering=False)
    G = T // 512
    NT = T // 128
    HB = T // 2
    SEG = 2048
    NSEG = T // SEG
    allp = ("a1", "a2", "a3", "b", "c1", "c2", "c3", "d")
    passes = allp if passes is None else passes

    def din(name, shape):
        return nc.dram_tensor(name, list(shape), F32, kind="ExternalInput").ap()

    def dscr(name, shape, dt):
        kind = "ExternalOutput" if (debug and name in debug) else "Internal"
        return nc.dram_tensor(name, list(shape), dt, kind=kind).ap()

    x_in = din("x", [T, D])
    cs64 = din("cs64", [2, 128, T])
    cs96 = din("cs96", [2, 96, T])
    consts = din("consts", [128, NCB * 128])
    percore = din("percore", [128, 4])
    e_norm = din("e_norm", [D])
    e_w_in = din("e_w_in", [D, 3072])
    e_a_q_gain = din("e_a_q_gain", [64])
    e_a_k_gain = din("e_a_k_gain", [64])
    e_a_sink = din("e_a_sink", [8])
    e_b_q_gain = din("e_b_q_gain", [3, 64])
    e_b_k_gain = din("e_b_k_gain", [3, 64])
    e_w_out = din("e_w_out", [768, D])
    o_norm = din("o_norm", [D])
    o_w_in = din("o_w_in", [D, 544])
    o_q_lora_gain = din("o_q_lora_gain", [256])
    o_w_uq = din("o_w_uq", [256, 1536])
    o_kv_gain = din("o_kv_gain", [256])
    o_w_ukv = din("o_w_ukv", [256, 2048])
    o_q_gain = din("o_q_gain", [96])
    o_k_gain = din("o_k_gain", [96])
    o_w_out = din("o_w_out", [D, D])
    f_norm = din("f_norm", [2, D])
    f_w_up = din("f_w_up", [2, D, 2 * DFF])
    f_conv_w = din("f_conv_w", [2, 3, DFF])
    f_conv_b = din("f_conv_b", [2, DFF])
    f_w_down = din("f_w_down", [2, DFF, D])

    y_out = nc.dram_tensor("y", [T, D], F32, kind="ExternalOutput").ap()

    TP = T + 2 * PAD
    qk0 = dscr("qk0", [NQK * 128, TP], BF16)
    v0 = dscr("v0", [TP, 896], BF16)
    cat0 = dscr("cat0", [768, T], BF16)
    x1 = dscr("x1", [T, D], F32)
    x2 = dscr("x2", [T, D], F32)
    q1 = dscr("q1", [16 * 96, T], BF16)
    k1 = dscr("k1", [16 * 96, T], BF16)
    v1 = dscr("v1", [T, 16 * 128], BF16)
    cat1 = dscr("cat1", [D, T], BF16)
    x3 = dscr("x3", [T, D], F32)

    es_glob = ExitStack()
    with es_glob:
        S = Sched(nc, es_glob)
        S.bar_tile = es_glob.enter_context(nc.sbuf_tensor("bar_tile", [1, 8], F32))
        cb = es_glob.enter_context(nc.sbuf_tensor("cb", [128, NCB, 128], BF16))
        pc = es_glob.enter_context(nc.sbuf_tensor("pc", [128, 4], F32))
        epsc = es_glob.enter_context(nc.sbuf_tensor("epsc", [128, 8], F32))
        ncd = nc.allow_non_contiguous_dma(reason="tiny parameter loads")
        es_glob.enter_context(ncd)

        def cbm(i, p=128, f=128):
            return cb[0:p, i, 0:f]

        uniq = [0]

        def sbt(es, name, shape, dt):
            uniq[0] += 1
            return es.enter_context(nc.sbuf_tensor("%s_%d" % (name, uniq[0]), list(shape), dt))

        def pst(es, name, dt=F32, cols=512):
            uniq[0] += 1
            return es.enter_context(nc.psum_tensor("%s_%d" % (name, uniq[0]), [128, cols], dt))

        with ExitStack() as es:
            zt = sbt(es, "zt", [128, 8 * 896], BF16)
            S.dma("pool", lambda e: e.dma_start(out=cb[:].rearrange("p a b -> p (a b)"), in_=consts[:, :]),
                  writes=["cb"], chan="cb")
            S.dma("sp", lambda e: e.dma_start(out=pc[:], in_=percore[:, :]), writes=["pc"], chan="pc")
            S.op("pool", lambda e: e.memset(zt[:], 0.0), writes=["zt"])
            for i, v in enumerate((D * EPS, 64 * EPS, 256 * EPS, 96 * EPS, 0.0)):
                S.op("pool", lambda e, i=i, v=v: e.memset(epsc[:, i:i + 1], float(v)), writes=[("epsc", i)])
            if "a1" in passes:
                for side in range(2):
                    c0 = 0 if side == 0 else PAD + T
                    for kc in range(NQK):
                        S.dma("sp", lambda e, kc=kc, c0=c0: e.dma_start(
                            out=qk0[kc * 128:(kc + 1) * 128, c0:c0 + PAD], in_=zt[:, 0:PAD]),
                            reads=["zt"], chan="zpad")
                    S.dma("sp", lambda e, c0=c0: e.dma_start(
                        out=v0[c0:c0 + PAD, :].rearrange("(p a) c -> p (a c)", p=128), in_=zt[:, 0:8 * 896]),
                        reads=["zt"], chan="zpad")
            S.emit_pass()

        def load_gain_cols(es, name, src_vec, n, chunks, mul, chan):
            t = sbt(es, name, [128, chunks], F32)
            S.dma("sp", lambda e: e.dma_start(out=t[:], in_=src_vec.rearrange("(c p) -> p c", p=128)),
                  writes=[name], chan=chan)
            if mul != 1.0:
                S.op("pool", lambda e: e.tensor_scalar(out=t[:], in0=t[:], scalar1=float(mul), scalar2=None, op0=ALU.mult),
                     reads=[name], writes=[name])
            return t

        def norm_transpose(es_unused, bufs, gi, src_dram, t0, ntok_tiles, gcol, xnT, xnT_key, col0, keep_x=None):
            pass

        def pass_a1():
            with ExitStack() as es:
                wqk = sbt(es, "wqk", [128, 8, NQK * 128], BF16)
                wv = sbt(es, "wv", [128, 8, 896], BF16)
                colmap = []
                for c in range(4):
                    colmap.append((c, 0, c * 128, 128))
                colmap += [(4, 0, 512, 64), (4, 64, 512, 64), (5, 0, 576, 64), (5, 64, 576, 64)]
                for g in range(3):
                    base = 768 + g * 768
                    for c in range(2):
                        colmap.append((6 + g * 4 + c, 0, base + c * 128, 128))
                        colmap.append((6 + g * 4 + 2 + c, 0, base + 256 + c * 128, 128))
                w_in_v = e_w_in.rearrange("(c p) n -> p c n", p=128)
                for (dc, do, sc, n) in colmap:
                    S.dma("pool", lambda e, dc=dc, do=do, sc=sc, n=n: e.dma_start(
                        out=wqk[:, :, dc * 128 + do: dc * 128 + do + n], in_=w_in_v[:, :, sc:sc + n]),
                        writes=[("wqk", dc, do)], chan="wqk")
                S.alias("wqk", len(S.ops) - 1)
                vmap = [(0, 640, 128)] + [(128 + g * 256, 768 + g * 768 + 512, 256) for g in range(3)]
                for (do, sc, n) in vmap:
                    S.dma("pool", lambda e, do=do, sc=sc, n=n: e.dma_start(
                        out=wv[:, :, do:do + n], in_=w_in_v[:, :, sc:sc + n]), writes=[("wv", do)], chan="wv")
                S.alias("wv", len(S.ops) - 1)
                gn = load_gain_cols(es, "gn", e_norm, D, 8, 32.0, "gn")
                gq = sbt(es, "gq", [128, NQK], F32)
                gsrc = [e_a_q_gain] * 4 + [e_a_k_gain] * 2
                for g in range(3):
                    gsrc += [e_b_q_gain[g], e_b_q_gain[g], e_b_k_gain[g], e_b_k_gain[g]]
                for c, gs in enumerate(gsrc):
                    for h in range(2):
                        S.dma("sp", lambda e, c=c, gs=gs, h=h: e.dma_start(
                            out=gq[h * 64:(h + 1) * 64, c:c + 1], in_=gs.rearrange("(p o) -> p o", o=1)),
                            writes=[("gq", c, h)], chan="gq")
                S.alias("gq", len(S.ops) - 1)
                xt = [sbt(es, "xt%d" % i, [128, 4, D], F32) for i in range(2)]
                xn = sbt(es, "xn", [128, 4, D], BF16)
                junk = [sbt(es, "junk%d" % i, [128, D], BF16) for i in range(4)]
                ssq = [sbt(es, "ssq%d" % i, [128, 4], F32) for i in range(2)]
                rst = [sbt(es, "rst%d" % i, [128, 4], F32) for i in range(2)]
                xnT = [sbt(es, "xnT%d" % i, [128, 8, 512], BF16) for i in range(2)]
                cst = [sbt(es, "cst%d" % i, [128, 2, 512], F32) for i in range(2)]
                sq = [sbt(es, "sq%d" % i, [128, 512], BF16) for i in range(2)]
                qg = [sbt(es, "qg%d" % i, [128, 512], BF16) for i in range(2)]
                sd = [sbt(es, "sd%d" % i, [128, 512], F32) for i in range(2)]
                t1 = [sbt(es, "t1%d" % i, [128, 512], F32) for i in range(2)]
                t2 = [sbt(es, "t2%d" % i, [128, 512], F32) for i in range(2)]
                t3 = [sbt(es, "t3%d" % i, [128, 512], F32) for i in range(2)]
                qf = [sbt(es, "qf%d" % i, [128, 512], BF16) for i in range(3)]
                vst = [sbt(es, "vst%d" % i, [128, 896], BF16) for i in range(2)]
                ptr = [pst(es, "ptr%d" % i, BF16, 1024) for i in range(2)]
                qps = [pst(es, "qps%d" % i) for i in range(2)]
                sps = pst(es, "sps")
                rps = pst(es, "rps")
                vps = [pst(es, "vps%d" % i) for i in range(2)]
                ci = 0
                vi = 0
                for g in range(G):
                    t0 = g * 512
                    b = g % 2
                    S.dma("sp", lambda e, b=b, t0=t0: e.dma_start(
                        out=xt[b][:], in_=x_in[t0:t0 + 512, :].rearrange("(j p) d -> p j d", p=128)),
                        writes=[("xt", b)], chan="xt%d" % b)
                    S.dma("sp", lambda e, b=b, t0=t0: e.dma_start(
                        out=cst[b][:], in_=cs64[:, :, t0:t0 + 512].rearrange("a p t -> p a t")),
                        writes=[("cst", b)], chan="cst%d" % b)
                    S.op("pool", lambda e, b=b: e.memset(ssq[b][:], 0.0), writes=[("ssq", b, j) for j in range(4)])
                    for j in range(4):
                        S.op("act", lambda e, b=b, j=j: e.activation(
                            out=junk[j][:], in_=xt[b][:, j, :], func=AF.Square, accum_out=ssq[b][:, j:j + 1]),
                            reads=[("xt", b), ("ssq", b, j)], writes=[("ssq", b, j), ("junk", j)])
                    S.op("act", lambda e, b=b: e.activation(
                        out=rst[b][:], in_=ssq[b][:], func=AF.Sqrt, bias=epsc[:, 0:1], scale=1.0),
                        reads=[("ssq", b, j) for j in range(4)] + [("epsc", 0)], writes=[("rst", b)])
                    S.op("dve", lambda e, b=b: e.reciprocal(out=rst[b][:], in_=rst[b][:]),
                         reads=[("rst", b)], writes=[("rst", b)])
                    for j in range(4):
                        S.op("dve" if j % 2 else "pool", lambda e, b=b, j=j: e.tensor_scalar(
                            out=xn[:, j, :], in0=xt[b][:, j, :], scalar1=rst[b][:, j:j + 1], scalar2=None, op0=ALU.mult),
                            reads=[("xt", b), ("rst", b)], writes=[("xn", j)])
                    for c in range(8):
                        pb = c % 2
                        for j in range(4):
                            S.op("pe", lambda e, pb=pb, c=c, j=j: e.transpose(
                                ptr[pb][:, j * 128:(j + 1) * 128], xn[:, j, c * 128:(c + 1) * 128], cbm(CB_IDENT)),
                                reads=[("xn", j), "cb"], writes=[("ptr", pb)])
                        S.op("act" if c % 2 else "dve", (lambda e, pb=pb, c=c, b=b: e.activation(
                            out=xnT[b][:, c, :], in_=ptr[pb][:, 0:512], func=AF.Copy, scale=gn[:, c:c + 1])) if c % 2 else
                            (lambda e, pb=pb, c=c, b=b: e.tensor_scalar(
                                out=xnT[b][:, c, :], in0=ptr[pb][:, 0:512], scalar1=gn[:, c:c + 1], scalar2=None, op0=ALU.mult)),
                            reads=[("ptr", pb), "gn"], writes=[("xnT", b, c)])
                    xk = [("xnT", b, c) for c in range(8)]
                    pend = []
                    for fc in range(NQK):
                        qb = ci % 2
                        ci += 1
                        for c in range(8):
                            S.op("pe", lambda e, qb=qb, fc=fc, c=c, b=b: e.matmul(
                                qps[qb][:], lhsT=wqk[:, c, fc * 128:(fc + 1) * 128], rhs=xnT[b][:, c, :],
                                start=(c == 0), stop=(c == 7)),
                                reads=[("xnT", b, c), "wqk"], writes=[("qps", qb)])
                        def post(qb=qb, fc=fc, b=b, g=g, t0=t0):
                            S.op("act", lambda e, qb=qb: e.activation(out=sq[qb][:], in_=qps[qb][:], func=AF.Square),
                                 reads=[("qps", qb)], writes=[("sq", qb)])
                            S.op("act", lambda e, qb=qb, fc=fc: e.activation(
                                out=qg[qb][:], in_=qps[qb][:], func=AF.Copy, scale=gq[:, fc:fc + 1]),
                                reads=[("qps", qb), "gq"], writes=[("qg", qb)])
                            S.op("pe", lambda e, qb=qb: e.matmul(sps[:], lhsT=cbm(CB_B64), rhs=sq[qb][:], start=True, stop=True),
                                 reads=[("sq", qb), "cb"], writes=["sps"])
                            S.op("pe", lambda e, qb=qb: e.matmul(rps[:], lhsT=cbm(CB_R64), rhs=qg[qb][:], start=True, stop=True),
                                 reads=[("qg", qb), "cb"], writes=["rps"])
                            S.op("act", lambda e, qb=qb: e.activation(
                                out=sd[qb][:], in_=sps[:], func=AF.Sqrt, bias=epsc[:, 1:2], scale=1.0),
                                reads=["sps", ("epsc", 1)], writes=[("sd", qb)])
                            S.op("pool", lambda e, qb=qb, b=b: e.tensor_tensor(
                                out=t1[qb][:], in0=qg[qb][:], in1=cst[b][:, 0, :], op=ALU.mult),
                                reads=[("qg", qb), ("cst", b)], writes=[("t1", qb)])
                            S.op("dve", lambda e, qb=qb, b=b: e.tensor_tensor(
                                out=t2[qb][:], in0=rps[:], in1=cst[b][:, 1, :], op=ALU.mult),
                                reads=["rps", ("cst", b)], writes=[("t2", qb)])
                            S.op("pool", lambda e, qb=qb: e.tensor_tensor(
                                out=t3[qb][:], in0=t1[qb][:], in1=t2[qb][:], op=ALU.add),
                                reads=[("t1", qb), ("t2", qb)], writes=[("t3", qb)])
                            fb = (g * NQK + fc) % 3
                            S.op("dve", lambda e, qb=qb: e.reciprocal(out=sd[qb][:], in_=sd[qb][:]),
                                 reads=[("sd", qb)], writes=[("sd", qb)])
                            S.op("dve", lambda e, qb=qb, fb=fb: e.tensor_tensor(
                                out=qf[fb][:], in0=t3[qb][:], in1=sd[qb][:], op=ALU.mult),
                                reads=[("t3", qb), ("sd", qb)], writes=[("qf", fb)])
                            S.dma("sp", lambda e, fb=fb, fc=fc, t0=t0: e.dma_start(
                                out=qk0[fc * 128:(fc + 1) * 128, PAD + t0:PAD + t0 + 512], in_=qf[fb][:]),
                                reads=[("qf", fb)], chan="qf%d" % fb)
                        if pend:
                            pend.pop()()
                        pend.append(post)
                    pend.pop()()
                    for j in range(4):
                        vb = vi % 2
                        vi += 1
                        for (c0, c1, pi) in ((0, 512, 0), (512, 896, 1)):
                            for c in range(8):
                                S.op("pe", lambda e, pi=pi, c0=c0, c1=c1, c=c, b=b, j=j: e.matmul(
                                    vps[pi][:, 0:c1 - c0], lhsT=xnT[b][:, c, j * 128:(j + 1) * 128], rhs=wv[:, c, c0:c1],
                                    start=(c == 0), stop=(c == 7)),
                                    reads=[("xnT", b, c), "wv"], writes=[("vps", pi)])
                            S.op("act" if pi else "dve", (lambda e, pi=pi, c0=c0, c1=c1, vb=vb: e.activation(
                                out=vst[vb][:, c0:c1], in_=vps[pi][:, 0:c1 - c0], func=AF.Copy)) if pi else
                                (lambda e, pi=pi, c0=c0, c1=c1, vb=vb: e.tensor_copy(out=vst[vb][:, c0:c1], in_=vps[pi][:, 0:c1 - c0])),
                                reads=[("vps", pi)], writes=[("vst", vb, pi)])
                        S.dma("sp", lambda e, vb=vb, t0=t0, j=j: e.dma_start(
                            out=v0[PAD + t0 + j * 128:PAD + t0 + (j + 1) * 128, :], in_=vst[vb][:]),
                            reads=[("vst", vb, 0), ("vst", vb, 1)], chan="vst%d" % vb)
                S.emit_pass()

        def pass_a2():
            with ExitStack() as es:
                qTA = sbt(es, "qTA", [128, 4, SEG], BF16)
                kTA = sbt(es, "kTA", [128, 2, SEG + 2 * PAD], BF16)
                vtA = sbt(es, "vtA", [128, 18, 2, 128], BF16)
                qTB = sbt(es, "qTB", [128, 2, SEG], BF16)
                kTB = sbt(es, "kTB", [128, 2, SEG + 2 * PAD], BF16)
                vtB = sbt(es, "vtB", [128, 32, 4, 128], BF16)
                acc = sbt(es, "acc", [128, 4, SEG], F32)
                recB = sbt(es, "recB", [128, SEG], F32)
                ostA = sbt(es, "ostA", [128, 4, SEG], BF16)
                ostB = sbt(es, "ostB", [128, 2, SEG], BF16)
                pt = [sbt(es, "pt%d" % i, [128, 512], BF16) for i in range(3)]
                dn = [sbt(es, "dn%d" % i, [128, 512], F32) for i in range(2)]
                snk = sbt(es, "snk", [128, 8], F32)
                spsX = [pst(es, "s2psx%d" % i) for i in range(2)]
                spsY = [pst(es, "s2psy%d" % i) for i in range(2)]
                ops_ = [pst(es, "o2ps%d" % i) for i in range(2)]
                S.op("pool", lambda e: e.memset(vtA[:, :, :, 64:128], 1.0), writes=["vtA1"])
                S.op("pool", lambda e: e.memset(vtB[:, :, :, 64:128], 1.0), writes=["vtB1"])
                S.dma("sp", lambda e: e.dma_start(out=snk[:], in_=e_a_sink.partition_broadcast(128)), writes=["snk"], chan="snk")
                S.op("act", lambda e: e.activation(out=snk[:], in_=snk[:], func=AF.Exp), reads=["snk"], writes=["snk"])
                cnt = dict(p=0, s=0, o=0, d=0)

                def attend(items, mask, vt_single, o_t, first, last):
                    order = [i for i in range(len(items)) if items[i][0] == 0] + [i for i in range(len(items)) if items[i][0] != 0]
                    nx = sum(1 for it in items if it[0] == 0)
                    n = len(items)
                    si = cnt["s"] % 2
                    cnt["s"] += 1
                    pi = cnt["p"] % 3
                    cnt["p"] += 1
                    for col, idx in enumerate(order):
                        po, l, r, rk, _ = items[idx]
                        bank = spsX[si] if po == 0 else spsY[si]
                        bkey = ("spsX", si) if po == 0 else ("spsY", si)
                        cc = col if po == 0 else col - nx
                        S.op("pe", lambda e, l=l, r=r, cc=cc, bank=bank: e.matmul(
                            bank[:, cc * 128:(cc + 1) * 128], lhsT=l, rhs=r, start=True, stop=True),
                            reads=rk, writes=[bkey])
                    if nx > 0:
                        S.op("act", lambda e, si=si, pi=pi, nx=nx: e.activation(
                            out=pt[pi][:, 0:nx * 128], in_=spsX[si][:, 0:nx * 128], func=AF.Exp, scale=8.0),
                            reads=[("spsX", si)], writes=[("pt", pi, 0)])
                    if n - nx > 0:
                        S.op("act", lambda e, si=si, pi=pi, nx=nx, n=n: e.activation(
                            out=pt[pi][:, nx * 128:n * 128], in_=spsY[si][:, 0:(n - nx) * 128], func=AF.Exp, scale=8.0),
                            reads=[("spsY", si)], writes=[("pt", pi, 1)])
                    pk = [("pt", pi, 0), ("pt", pi, 1)]
                    if mask is not None:
                        S.op("pool", lambda e, pi=pi, n=n, mask=mask: e.tensor_tensor(
                            out=pt[pi][:, 0:n * 128].rearrange("p (a b) -> p a b", a=n),
                            in0=pt[pi][:, 0:n * 128].rearrange("p (a b) -> p a b", a=n),
                            in1=cbm(mask).unsqueeze(1).broadcast_to([128, n, 128]), op=ALU.mult),
                            reads=pk + ["cb"], writes=pk)
                    if vt_single is not None:
                        vl, vk = vt_single
                        S.op("pe", lambda e, vl=vl, pi=pi, o_t=o_t, n=n: e.matmul(
                            ops_[o_t][:, 0:n * 128], lhsT=vl, rhs=pt[pi][:, 0:n * 128], start=first, stop=last),
                            reads=pk + vk, writes=[("ops", o_t)])
                    else:
                        for col, idx in enumerate(order):
                            vl, vk = items[idx][4]
                            st_ = first and col == 0
                            sp_ = last and col == n - 1
                            S.op("pe", lambda e, vl=vl, col=col, pi=pi, o_t=o_t, st_=st_, sp_=sp_: e.matmul(
                                ops_[o_t][:, col * 128:(col + 1) * 128], lhsT=vl, rhs=pt[pi][:, col * 128:(col + 1) * 128],
                                start=st_, stop=sp_),
                                reads=pk + vk, writes=[("ops", o_t)])
                    return order

                for sg in range(NSEG):
                    tseg = sg * SEG
                    S.dma("sp", lambda e, tseg=tseg: e.dma_start(
                        out=qTA[:], in_=qk0[0:512, PAD + tseg:PAD + tseg + SEG].rearrange("(c p) t -> p c t", p=128)),
                        writes=["qTA"], chan="qTA")
                    S.dma("sp", lambda e, tseg=tseg: e.dma_start(
                        out=kTA[:], in_=qk0[512:768, tseg:tseg + SEG + 2 * PAD].rearrange("(c p) t -> p c t", p=128)),
                        writes=["kTA"], chan="kTA")
                    for h in range(2):
                        S.dma("sp", lambda e, tseg=tseg, h=h: e.dma_start(
                            out=vtA[:, :, h, 0:64],
                            in_=v0[PAD + tseg - 128:PAD + tseg - 128 + 18 * 128, h * 64:(h + 1) * 64].rearrange("(m j) d -> j m d", j=128)),
                            writes=["vtA"], chan="vtA")
                    for nn in range(SEG // 128 if not os.environ.get("SKIP_A") else 0):
                        tb = tseg + nn * 128
                        tiles = []
                        if tb > 0:
                            tiles.append((-1, CB_GEAB if tb == HB else CB_GE))
                        tiles.append((0, None))
                        if tb + 128 < T:
                            tiles.append((1, CB_LEAB if tb + 128 == HB else CB_LE))
                        for hkv in range(2):
                            ot = cnt["o"] % 2
                            cnt["o"] += 1
                            for ti, (off, mask) in enumerate(tiles):
                                kc0 = PAD + 128 * (nn + off)
                                items = []
                                for g in range(4):
                                    hq = 4 * hkv + g
                                    po = (hq % 2) * 64
                                    items.append((po, kTA[po:po + 64, hkv, kc0:kc0 + 128],
                                                  qTA[po:po + 64, hq // 2, nn * 128:(nn + 1) * 128], ["kTA", "qTA"], None))
                                order = attend(items, mask, (vtA[:, nn + 1 + off, hkv, :], ["vtA", "vtA1"]),
                                               ot, ti == 0, ti == len(tiles) - 1)
                            di = cnt["d"] % 2
                            cnt["d"] += 1
                            for col, g in enumerate(order):
                                hq = 4 * hkv + g
                                S.op("dve", lambda e, di=di, ot=ot, col=col, hq=hq: e.tensor_scalar(
                                    out=dn[di][64:128, col * 128:(col + 1) * 128], in0=ops_[ot][64:128, col * 128:(col + 1) * 128],
                                    scalar1=snk[64:128, hq:hq + 1], scalar2=None, op0=ALU.add),
                                    reads=[("ops", ot), "snk"], writes=[("dn", di, col)])
                            S.op("dve", lambda e, di=di: e.reciprocal(out=dn[di][64:128, :], in_=dn[di][64:128, :]),
                                 reads=[("dn", di, g) for g in range(4)], writes=[("dn", di, g) for g in range(4)])
                            for col, g in enumerate(order):
                                hq = 4 * hkv + g
                                po = (hq % 2) * 64
                                S.op("dve", lambda e, di=di, ot=ot, col=col, hq=hq, po=po, nn=nn: e.tensor_tensor(
                                    out=ostA[po:po + 64, hq // 2, nn * 128:(nn + 1) * 128],
                                    in0=ops_[ot][0:64, col * 128:(col + 1) * 128], in1=dn[di][64:128, col * 128:(col + 1) * 128], op=ALU.mult),
                                    reads=[("ops", ot), ("dn", di, col)], writes=[("ostA", hq, nn)])
                    S.dma("sp", lambda e, tseg=tseg: e.dma_start(
                        out=cat0[0:512, tseg:tseg + SEG].rearrange("(c p) t -> p c t", p=128), in_=ostA[:]),
                        reads=[("ostA", hq, nn) for hq in range(8) for nn in range(SEG // 128)], chan="ostA")
                    for g, r in enumerate((1, 4, 16)):
                        if os.environ.get("SKIP_B") and str(g) in os.environ.get("SKIP_B"):
                            continue
                        nblk = SEG // (128 * r)
                        Fblocks = T // (128 * r)
                        nbnd = HB // (128 * r)
                        qrow = (6 + 4 * g) * 128
                        S.dma("sp", lambda e, tseg=tseg, qrow=qrow: e.dma_start(
                            out=qTB[:], in_=qk0[qrow:qrow + 256, PAD + tseg:PAD + tseg + SEG].rearrange("(c p) t -> p c t", p=128)),
                            writes=["qTB"], chan="qTB")
                        S.dma("sp", lambda e, tseg=tseg, qrow=qrow: e.dma_start(
                            out=kTB[:], in_=qk0[qrow + 256:qrow + 512, tseg:tseg + SEG + 2 * PAD].rearrange("(c p) t -> p c t", p=128)),
                            writes=["kTB"], chan="kTB")
                        vcol = 128 + 256 * g
                        for c in range(r):
                            base = PAD + tseg - 64 * r + c
                            nrow = (nblk + 1) * 128
                            for h in range(4):
                                S.dma("sp", lambda e, base=base, nrow=nrow, r=r, c=c, nblk=nblk, vcol=vcol, h=h: e.dma_start(
                                    out=vtB[:, c * (nblk + 1):(c + 1) * (nblk + 1), h, 0:64],
                                    in_=v0[base:base + (nrow - 1) * r + 1:r, vcol + h * 64:vcol + (h + 1) * 64].rearrange("(m j) d -> j m d", j=128)),
                                    writes=["vtB"], chan="vtB")
                        for nn in range(nblk):
                            n = nblk * sg + nn
                            for c in range(r):
                                ot = cnt["o"] % 2
                                cnt["o"] += 1
                                for which in range(2):
                                    mm = nn + which
                                    if which == 0:
                                        mask = CB_GEF if n == 0 else (CB_GEB if n == nbnd else CB_GE)
                                    else:
                                        mask = CB_LEL if n + 1 == Fblocks else (CB_LEB if n + 1 == nbnd else CB_LE)
                                    k0 = PAD - 64 * r + c + 128 * r * mm
                                    q0 = 128 * r * nn + c
                                    items = []
                                    for h in range(4):
                                        po = (h % 2) * 64
                                        items.append((po, kTB[po:po + 64, h // 2, k0:k0 + 127 * r + 1:r],
                                                      qTB[po:po + 64, h // 2, q0:q0 + 127 * r + 1:r], ["kTB", "qTB"],
                                                      (vtB[:, c * (nblk + 1) + mm, h, :], ["vtB", "vtB1"])))
                                    order = attend(items, mask, None, ot, which == 0, which == 1)
                                assert order == [0, 2, 1, 3]
                                for half in range(2):
                                    av = acc[:, half:4:2, 128 * r * nn:128 * r * (nn + 1)].rearrange("p h (i r) -> p h r i", r=r)[:, :, c, :]
                                    ov = ops_[ot][:, half * 256:(half + 1) * 256].rearrange("p (h i) -> p h i", h=2)
                                    if g == 0:
                                        S.op("dve", lambda e, av=av, ov=ov: e.tensor_copy(out=av, in_=ov),
                                             reads=[("ops", ot)], writes=["acc"])
                                    else:
                                        S.op("dve", lambda e, av=av, ov=ov: e.tensor_tensor(out=av, in0=ov, in1=av, op=ALU.add),
                                             reads=[("ops", ot), "acc"], writes=["acc"])
                    for h in range(4):
                        po = (h % 2) * 64
                        S.op("dve", lambda e, h=h: e.reciprocal(out=recB[0:64, :], in_=acc[64:128, h, :]),
                             reads=["acc"], writes=["recB"])
                        S.op("pool", lambda e, h=h, po=po: e.tensor_tensor(
                            out=ostB[po:po + 64, h // 2, :], in0=acc[0:64, h, :], in1=recB[0:64, :], op=ALU.mult),
                            reads=["acc", "recB"], writes=[("ostB", h)])
                    S.dma("sp", lambda e, tseg=tseg: e.dma_start(
                        out=cat0[512:768, tseg:tseg + SEG].rearrange("(c p) t -> p c t", p=128), in_=ostB[:]),
                        reads=[("ostB", h) for h in range(4)], chan="ostB")
                S.emit_pass()

        def pass_outproj(tag, xin, catT, KC, w_dram, xout):
            with ExitStack() as es:
                wo = sbt(es, "wo", [128, KC, D], BF16)
                S.dma("pool", lambda e: e.dma_start(out=wo[:], in_=w_dram.rearrange("(c p) n -> p c n", p=128)),
                      writes=["wo"], chan="wo")
                ct = [sbt(es, "ct%d" % i, [128, KC, 512], BF16) for i in range(2)]
                xt = [sbt(es, "oxt%d" % i, [128, 4, D], F32) for i in range(2)]
                yps = [pst(es, "yps%d" % i) for i in range(4)]
                k = 0
                for g in range(G):
                    t0 = g * 512
                    b = g % 2
                    S.dma("sp", lambda e, b=b, t0=t0: e.dma_start(
                        out=ct[b][:], in_=catT[:, t0:t0 + 512].rearrange("(c p) t -> p c t", p=128)),
                        writes=[("ct", b)], chan="ct%d" % b)
                    S.dma("sp", lambda e, b=b, t0=t0: e.dma_start(
                        out=xt[b][:], in_=xin[t0:t0 + 512, :].rearrange("(j p) d -> p j d", p=128)),
                        writes=[("oxt", b)], chan="oxt%d" % b)
                    for j in range(4):
                        for half in range(2):
                            yb = k % 4
                            k += 1
                            for kc in range(KC):
                                S.op("pe", lambda e, yb=yb, b=b, kc=kc, j=j, half=half: e.matmul(
                                    yps[yb][:], lhsT=ct[b][:, kc, j * 128:(j + 1) * 128], rhs=wo[:, kc, half * 512:(half + 1) * 512],
                                    start=(kc == 0), stop=(kc == KC - 1)),
                                    reads=[("ct", b), "wo"], writes=[("yps", yb)])
                            S.op("dve", lambda e, yb=yb, b=b, j=j, half=half: e.tensor_tensor(
                                out=xt[b][:, j, half * 512:(half + 1) * 512], in0=yps[yb][:], in1=xt[b][:, j, half * 512:(half + 1) * 512], op=ALU.add),
                                reads=[("yps", yb), ("oxt", b)], writes=[("oxt", b)])
                    S.dma("sp", lambda e, b=b, t0=t0: e.dma_start(
                        out=xout[t0:t0 + 512, :].rearrange("(j p) d -> p j d", p=128), in_=xt[b][:]),
                        reads=[("oxt", b)], chan="oxt%d" % b)
                S.emit_pass()

        def pass_ffn(l, xin, xout):
            with ExitStack() as es:
                wup = sbt(es, "wup", [128, 8, 2 * DFF], BF16)
                wdn = sbt(es, "wdn", [128, NFC, D], BF16)
                upv = f_w_up[l].rearrange("(c p) n -> p c n", p=128)
                for q in range(8):
                    S.dma("pool", lambda e, q=q: e.dma_start(out=wup[:, :, q * 704:(q + 1) * 704], in_=upv[:, :, q * 704:(q + 1) * 704]),
                          writes=[("wup", q)], chan="wup")
                S.alias("wup", len(S.ops) - 1)
                dnv = f_w_down[l].rearrange("(c p) n -> p c n", p=128)
                for q in range(2):
                    S.dma("pool", lambda e, q=q: e.dma_start(out=wdn[:, q * 11:(q + 1) * 11, :], in_=dnv[:, q * 11:(q + 1) * 11, :]),
                          writes=[("wdn", q)], chan="wdn")
                S.alias("wdn", len(S.ops) - 1)
                gn = load_gain_cols(es, "fgn", f_norm[l], D, 8, 32.0, "fgn")
                cw = sbt(es, "cw", [128, 3, NFC], F32)
                for jj in range(3):
                    S.dma("sp", lambda e, jj=jj: e.dma_start(out=cw[:, jj, :], in_=f_conv_w[l, jj].rearrange("(c p) -> p c", p=128)),
                          writes=[("cw", jj)], chan="cw")
                S.alias("cw", len(S.ops) - 1)
                cbi = sbt(es, "cbi", [128, NFC], F32)
                S.dma("sp", lambda e: e.dma_start(out=cbi[:], in_=f_conv_b[l].rearrange("(c p) -> p c", p=128)),
                      writes=["cbi"], chan="cbi")
                xt = sbt(es, "fxt", [128, 4, D], F32)
                xh = sbt(es, "fxh", [2, D], F32)
                xn = [sbt(es, "fxn%d" % i, [128, D], BF16) for i in range(2)]
                xhn = sbt(es, "fxhn", [2, D], BF16)
                ssq = sbt(es, "fssq", [128, 8], F32)
                rst = sbt(es, "frst", [128, 8], F32)
                xnT = sbt(es, "fxnT", [128, 8, 512], BF16)
                xnTh = sbt(es, "fxnTh", [128, 8, 2], BF16)
                gsb = [sbt(es, "gsb%d" % i, [128, 514], F32) for i in range(3)]
                aa = [sbt(es, "aa%d" % i, [128, 512], F32) for i in range(3)]
                hT = sbt(es, "hT", [128, NFC, 512], BF16)
                junk = [hT[:, 0:2, :].rearrange("p a b -> p (a b)"), hT[:, 2:4, :].rearrange("p a b -> p (a b)")]
                gps = [pst(es, "gps%d" % i) for i in range(2)]
                vps = [pst(es, "fvps%d" % i) for i in range(3)]
                hpsl = [pst(es, "hps%d" % i) for i in range(1)]
                yps = [pst(es, "fyps%d" % i) for i in range(2)]
                ptr = yps[1][:, :].bitcast(BF16)
                k = 0
                fi = 0
                for g in range(G):
                    t0 = g * 512
                    has_l = (t0 > 0)
                    has_r = (t0 + 512 < T)
                    S.dma("sp", lambda e, t0=t0: e.dma_start(
                        out=xt[:], in_=xin[t0:t0 + 512, :].rearrange("(j p) d -> p j d", p=128)),
                        writes=["fxt"], chan="fxt")
                    S.op("pool", lambda e: e.memset(ssq[:], 0.0), writes=["fssq"])
                    if has_l or has_r:
                        S.op("pool", lambda e: e.memset(xh[:], 1.0), writes=["fxh"])
                        if has_l:
                            S.dma("sp", lambda e, t0=t0: e.dma_start(out=xh[0:1, :], in_=xin[t0 - 1:t0, :]), reads=["fxh"], writes=["fxh"], chan="fxh")
                        if has_r:
                            S.dma("sp", lambda e, t0=t0: e.dma_start(out=xh[1:2, :], in_=xin[t0 + 512:t0 + 513, :]), reads=["fxh"], writes=["fxh"], chan="fxh")
                        S.op("act", lambda e: e.activation(out=junk[0][0:2, :], in_=xh[:], func=AF.Square, accum_out=ssq[0:2, 4:5]),
                             reads=["fxh", "fssq"], writes=["fssq", ("hT", 0), ("hT", 1)])
                    for j in range(4):
                        S.op("act", lambda e, j=j: e.activation(
                            out=junk[j % 2], in_=xt[:, j, :], func=AF.Square, accum_out=ssq[:, j:j + 1]),
                            reads=["fxt", "fssq"], writes=["fssq", ("hT", 2 * (j % 2)), ("hT", 2 * (j % 2) + 1)])
                    S.op("act", lambda e: e.activation(out=rst[:], in_=ssq[:], func=AF.Sqrt, bias=epsc[:, 0:1], scale=1.0),
                         reads=["fssq", ("epsc", 0)], writes=["frst"])
                    S.op("dve", lambda e: e.reciprocal(out=rst[:], in_=rst[:]), reads=["frst"], writes=["frst"])
                    if has_l or has_r:
                        S.op("dve", lambda e: e.tensor_scalar(out=xhn[:], in0=xh[:], scalar1=rst[0:2, 4:5], scalar2=None, op0=ALU.mult),
                             reads=["fxh", "frst"], writes=["fxhn"])
                    for j in range(4):
                        nb_ = j % 2
                        S.op("dve" if j % 2 else "pool", lambda e, j=j, nb_=nb_: e.tensor_scalar(
                            out=xn[nb_][:], in0=xt[:, j, :], scalar1=rst[:, j:j + 1], scalar2=None, op0=ALU.mult),
                            reads=["fxt", "frst"], writes=[("fxn", nb_)])
                        for c in range(8):
                            S.op("pe", lambda e, c=c, j=j, nb_=nb_: e.transpose(
                                ptr[:, c * 128:(c + 1) * 128], xn[nb_][:, c * 128:(c + 1) * 128], cbm(CB_IDENT)),
                                reads=[("fxn", nb_), "cb"], writes=[("fyps", 1)])
                        S.op("dve", lambda e, j=j: e.tensor_tensor(
                            out=xnT[:, :, j * 128:(j + 1) * 128], in0=ptr[:, :].rearrange("p (c t) -> p c t", c=8),
                            in1=gn[:, :].unsqueeze(2).broadcast_to([128, 8, 128]), op=ALU.mult),
                            reads=[("fyps", 1), "fgn"], writes=[("fxnT", j)])
                    if has_l or has_r:
                        for c in range(8):
                            S.op("pe", lambda e, c=c: e.transpose(
                                ptr[:, c * 128:c * 128 + 2], xhn[0:2, c * 128:(c + 1) * 128], cbm(CB_IDENT, 2, 2)),
                                reads=["fxhn", "cb"], writes=[("fyps", 1)])
                        S.op("dve", lambda e: e.tensor_tensor(
                            out=xnTh[:], in0=ptr[:, :].rearrange("p (c t) -> p c t", c=8)[:, :, 0:2],
                            in1=gn[:, :].unsqueeze(2).broadcast_to([128, 8, 2]), op=ALU.mult),
                             reads=[("fyps", 1), "fgn"], writes=["fxnTh"])
                    xk = [("fxnT", j) for j in range(4)]
                    pend_tail = []
                    for fc in range(NFC):
                        gb = fi % 2
                        v3 = fi % 3
                        fi += 1
                        for (dst, col0, key) in ((gps[gb], fc * 128, ("gps", gb)), (vps[v3], DFF + fc * 128, ("fvps", v3))):
                            for c in range(8):
                                S.op("pe", lambda e, dst=dst, col0=col0, c=c: e.matmul(
                                    dst[:], lhsT=wup[:, c, col0:col0 + 128], rhs=xnT[:, c, :],
                                    start=(c == 0), stop=(c == 7)),
                                    reads=xk + ["wup"], writes=[key])
                            if dst is gps[gb] and (has_l or has_r):
                                for c in range(8):
                                    S.op("pe", lambda e, col0=col0, c=c, fc=fc: e.matmul(
                                        hpsl[0][:, 2 * fc:2 * fc + 2], lhsT=wup[:, c, col0:col0 + 128], rhs=xnTh[:, c, :],
                                        start=(c == 0), stop=(c == 7)),
                                        reads=["fxnTh", "wup"], writes=[("hps", 0)])
                        S.op("act", lambda e, gb=gb, v3=v3: e.activation(out=gsb[v3][:, 1:513], in_=gps[gb][:], func=AF.Copy),
                             reads=[("gps", gb)], writes=[("gsb", v3, 1)])
                        for side, has in ((0, has_l), (1, has_r)):
                            colo = 0 if side == 0 else 513
                            if not has:
                                S.op("pool", lambda e, gb=v3, colo=colo: e.memset(gsb[gb][:, colo:colo + 1], 0.0),
                                     writes=[("gsb", v3, 0 if side == 0 else 2)])
                            else:
                                tpos = t0 if side == 0 else t0 + 512
                                sc = pc[:, 0:1] if tpos == HB else 1.0
                                S.op("dve", lambda e, gb=v3, colo=colo, fc=fc, side=side, sc=sc: e.tensor_scalar(
                                    out=gsb[gb][:, colo:colo + 1], in0=hpsl[0][:, 2 * fc + side:2 * fc + side + 1], scalar1=sc, scalar2=None, op0=ALU.mult),
                                    reads=[("hps", 0), "pc"], writes=[("gsb", v3, 0 if side == 0 else 2)])
                        gk = [("gsb", v3, i) for i in range(3)]
                        S.op("pool", lambda e, gb=v3, fc=fc: e.tensor_scalar(
                            out=aa[gb][:], in0=gsb[gb][:, 1:513], scalar1=cw[:, 1, fc:fc + 1], scalar2=None, op0=ALU.mult),
                            reads=gk + ["cw"], writes=[("aa", v3)])
                        S.op("dve", lambda e, gb=v3, fc=fc: e.scalar_tensor_tensor(
                            out=aa[gb][:], in0=gsb[gb][:, 0:512], scalar=cw[:, 0, fc:fc + 1], in1=aa[gb][:], op0=ALU.mult, op1=ALU.add),
                            reads=gk + ["cw", ("aa", v3)], writes=[("aa", v3)])
                        S.op("dve", lambda e, gb=v3, fc=fc: e.scalar_tensor_tensor(
                            out=aa[gb][:], in0=gsb[gb][:, 2:514], scalar=cw[:, 2, fc:fc + 1], in1=aa[gb][:], op0=ALU.mult, op1=ALU.add),
                            reads=gk + ["cw", ("aa", v3)], writes=[("aa", v3)])
                        def tail(v3=v3, fc=fc):
                            S.op("act", lambda e, gb=v3, fc=fc: e.activation(
                                out=aa[gb][:], in_=aa[gb][:], func=AF.Gelu_apprx_tanh, bias=cbi[:, fc:fc + 1], scale=1.0),
                                reads=[("aa", v3), "cbi"], writes=[("aa", v3)])
                            S.op("dve", lambda e, gb=v3, fc=fc: e.tensor_tensor(
                                out=hT[:, fc, :], in0=vps[gb][:], in1=aa[gb][:], op=ALU.mult),
                                reads=[("fvps", v3), ("aa", v3)], writes=[("hT", fc)])
                        if pend_tail:
                            pend_tail.pop()()
                        pend_tail.append(tail)
                    pend_tail.pop()()
                    hk = [("hT", fc) for fc in range(NFC)]
                    for j in range(4):
                        for half in range(2):
                            yb = k % 2
                            k += 1
                            for fc in range(NFC):
                                S.op("pe", lambda e, yb=yb, fc=fc, j=j, half=half: e.matmul(
                                    yps[yb][:], lhsT=hT[:, fc, j * 128:(j + 1) * 128], rhs=wdn[:, fc, half * 512:(half + 1) * 512],
                                    start=(fc == 0), stop=(fc == NFC - 1)),
                                    reads=[("hT", fc), "wdn"], writes=[("fyps", yb)])
                            S.op("dve", lambda e, yb=yb, j=j, half=half: e.tensor_tensor(
                                out=xt[:, j, half * 512:(half + 1) * 512], in0=yps[yb][:], in1=xt[:, j, half * 512:(half + 1) * 512], op=ALU.add),
                                reads=[("fyps", yb), "fxt"], writes=["fxt"])
                    S.dma("sp", lambda e, t0=t0: e.dma_start(
                        out=xout[t0:t0 + 512, :].rearrange("(j p) d -> p j d", p=128), in_=xt[:]),
                        reads=["fxt"], chan="fxt")
                S.emit_pass()

        def pass_c1():
            with ExitStack() as es:
                win1 = sbt(es, "win1", [128, 8, 544], BF16)
                S.dma("pool", lambda e: e.dma_start(out=win1[:], in_=o_w_in.rearrange("(c p) n -> p c n", p=128)),
                      writes=["win1"], chan="win1")
                wuq = sbt(es, "wuq", [128, 2, 1536], BF16)
                S.dma("pool", lambda e: e.dma_start(out=wuq[:], in_=o_w_uq.rearrange("(c p) n -> p c n", p=128)),
                      writes=["wuq"], chan="wuq")
                wukx = sbt(es, "wukx", [128, 2, 16, 96], BF16)
                wuv = sbt(es, "wuv", [128, 2, 16, 64], BF16)
                S.op("pool", lambda e: e.memset(wukx[:], 0.0), writes=["wukx"])
                ukv = o_w_ukv.rearrange("(c p) (h n) -> p c h n", p=128, n=128)
                for i in range(2):
                    S.dma("pool", lambda e, i=i: e.dma_start(out=wukx[:, i, :, 0:64], in_=ukv[:, i, :, 0:64]),
                          reads=["wukx"], writes=["wukx"], chan="wukx")
                    S.dma("pool", lambda e, i=i: e.dma_start(out=wuv[:, i, :, :], in_=ukv[:, i, :, 64:128]),
                          writes=["wuv"], chan="wuv")
                gn = load_gain_cols(es, "cgn", o_norm, D, 8, 32.0, "cgn")
                gql = load_gain_cols(es, "gql", o_q_lora_gain, 256, 2, 16.0, "gql")
                gkl = load_gain_cols(es, "gkl", o_kv_gain, 256, 2, 16.0, "gkl")
                g96 = sbt(es, "g96", [96, 2], F32)
                S.dma("sp", lambda e: e.dma_start(out=g96[:, 0:1], in_=o_q_gain.rearrange("(p o) -> p o", o=1)), writes=[("g96", 0)], chan="g96")
                S.dma("sp", lambda e: e.dma_start(out=g96[:, 1:2], in_=o_k_gain.rearrange("(p o) -> p o", o=1)), writes=[("g96", 1)], chan="g96")
                S.alias("g96", len(S.ops) - 1)
                xt = sbt(es, "cxt", [128, 4, D], F32)
                xn = [sbt(es, "cxn%d" % i, [128, D], BF16) for i in range(2)]
                junk = [sbt(es, "cjunk%d" % i, [128, D], BF16) for i in range(2)]
                ssq = sbt(es, "cssq", [128, 4], F32)
                rst = sbt(es, "crst", [128, 4], F32)
                xnT = sbt(es, "cxnT", [128, 8, 512], BF16)
                cst = [sbt(es, "ccst%d" % i, [96, 2, 512], F32) for i in range(2)]
                csq = [sbt(es, "csq%d" % i, [128, 512], BF16) for i in range(2)]
                craw = [sbt(es, "craw%d" % i, [128, 512], F32) for i in range(2)]
                crs = sbt(es, "crs", [128, 512], F32)
                cn = [sbt(es, "cn%d" % i, [128, 2, 512], BF16) for i in range(2)]
                krT = sbt(es, "krT", [32, 512], BF16)
                sq = [sbt(es, "hsq%d" % i, [96, 512], BF16) for i in range(2)]
                qg = [sbt(es, "hqg%d" % i, [96, 512], BF16) for i in range(2)]
                sd = [sbt(es, "hsd%d" % i, [96, 512], F32) for i in range(2)]
                t1 = [sbt(es, "ht1%d" % i, [96, 512], F32) for i in range(2)]
                t2 = [sbt(es, "ht2%d" % i, [96, 512], F32) for i in range(2)]
                qf = [sbt(es, "hqf%d" % i, [96, 512], BF16) for i in range(3)]
                vst = [sbt(es, "cvst%d" % i, [128, 16, 128], BF16) for i in range(2)]
                for i in range(2):
                    S.op("pool", lambda e, i=i: e.memset(vst[i][:, :, 64:128], 1.0), writes=[("cvst1", i)])
                ptr = pst(es, "cptr", BF16, 1024)
                cps = [pst(es, "cps%d" % i) for i in range(2)]
                qps = [pst(es, "cqps%d" % i) for i in range(2)]
                sps = pst(es, "csps")
                rps = pst(es, "crps")
                vps = pst(es, "cvps")
                ci = 0
                hi = 0
                vi = 0
                for g in range(G):
                    t0 = g * 512
                    b = g % 2
                    S.dma("sp", lambda e, t0=t0: e.dma_start(
                        out=xt[:], in_=x2[t0:t0 + 512, :].rearrange("(j p) d -> p j d", p=128)),
                        writes=["cxt"], chan="cxt")
                    S.dma("sp", lambda e, b=b, t0=t0: e.dma_start(
                        out=cst[b][:], in_=cs96[:, :, t0:t0 + 512].rearrange("a p t -> p a t")),
                        writes=[("ccst", b)], chan="ccst%d" % b)
                    S.op("pool", lambda e: e.memset(ssq[:], 0.0), writes=["cssq"])
                    for j in range(4):
                        S.op("act", lambda e, j=j: e.activation(
                            out=junk[j % 2][:], in_=xt[:, j, :], func=AF.Square, accum_out=ssq[:, j:j + 1]),
                            reads=["cxt", "cssq"], writes=["cssq", ("cjunk", j % 2)])
                    S.op("act", lambda e: e.activation(out=rst[:], in_=ssq[:], func=AF.Sqrt, bias=epsc[:, 0:1], scale=1.0),
                         reads=["cssq", ("epsc", 0)], writes=["crst"])
                    S.op("dve", lambda e: e.reciprocal(out=rst[:], in_=rst[:]), reads=["crst"], writes=["crst"])
                    for j in range(4):
                        nb_ = j % 2
                        S.op("dve" if j % 2 else "pool", lambda e, j=j, nb_=nb_: e.tensor_scalar(
                            out=xn[nb_][:], in0=xt[:, j, :], scalar1=rst[:, j:j + 1], scalar2=None, op0=ALU.mult),
                            reads=["cxt", "crst"], writes=[("cxn", nb_)])
                        for c in range(8):
                            S.op("pe", lambda e, c=c, nb_=nb_: e.transpose(
                                ptr[:, c * 128:(c + 1) * 128], xn[nb_][:, c * 128:(c + 1) * 128], cbm(CB_IDENT)),
                                reads=[("cxn", nb_), "cb"], writes=["cptr"])
                        S.op("dve", lambda e, j=j: e.tensor_tensor(
                            out=xnT[:, :, j * 128:(j + 1) * 128], in0=ptr[:, :].rearrange("p (c t) -> p c t", c=8),
                            in1=gn[:, :].unsqueeze(2).broadcast_to([128, 8, 128]), op=ALU.mult),
                            reads=["cptr", "cgn"], writes=[("cxnT", j)])
                    xk = [("cxnT", j) for j in range(4)]
                    for which, (col0, gl_, glk) in enumerate(((0, gql, "gql"), (256, gkl, "gkl"))):
                        for i in range(2):
                            cb_ = ci % 2
                            ci += 1
                            for c in range(8):
                                S.op("pe", lambda e, cb_=cb_, c=c, col0=col0, i=i: e.matmul(
                                    cps[cb_][:], lhsT=win1[:, c, col0 + i * 128:col0 + (i + 1) * 128], rhs=xnT[:, c, :],
                                    start=(c == 0), stop=(c == 7)),
                                    reads=xk + ["win1"], writes=[("cps", cb_)])
                            S.op("act", lambda e, cb_=cb_, i=i: e.activation(out=csq[i][:], in_=cps[cb_][:], func=AF.Square),
                                 reads=[("cps", cb_)], writes=[("csq", i)])
                            S.op("act", lambda e, cb_=cb_, i=i: e.activation(out=craw[i][:], in_=cps[cb_][:], func=AF.Copy),
                                 reads=[("cps", cb_)], writes=[("craw", i)])
                        for i in range(2):
                            S.op("pe", lambda e, i=i: e.matmul(sps[:], lhsT=cbm(CB_ONES), rhs=csq[i][:], start=(i == 0), stop=(i == 1)),
                                 reads=[("csq", i), "cb"], writes=["csps"])
                        S.op("act", lambda e: e.activation(out=crs[:], in_=sps[:], func=AF.Sqrt, bias=epsc[:, 2:3], scale=1.0),
                             reads=["csps", ("epsc", 2)], writes=["crs"])
                        S.op("dve", lambda e: e.reciprocal(out=crs[:], in_=crs[:]), reads=["crs"], writes=["crs"])
                        for i in range(2):
                            S.op("dve", lambda e, i=i, which=which, gl_=gl_: e.scalar_tensor_tensor(
                                out=cn[which][:, i, :], in0=craw[i][:], scalar=gl_[:, i:i + 1], in1=crs[:], op0=ALU.mult, op1=ALU.mult),
                                reads=[("craw", i), "crs", glk], writes=[("cn", which, i)])
                    cb_ = ci % 2
                    ci += 1
                    for c in range(8):
                        S.op("pe", lambda e, cb_=cb_, c=c: e.matmul(
                            cps[cb_][0:32, :], lhsT=win1[:, c, 512:544], rhs=xnT[:, c, :], start=(c == 0), stop=(c == 7)),
                            reads=xk + ["win1"], writes=[("cps", cb_)])
                    S.op("act", lambda e, cb_=cb_: e.activation(out=krT[:], in_=cps[cb_][0:32, :], func=AF.Copy),
                         reads=[("cps", cb_)], writes=["krT"])
                    pend = []
                    for isk in range(2):
                        for h in range(16):
                            qb = hi % 2
                            hi += 1
                            if isk == 0:
                                for i in range(2):
                                    S.op("pe", lambda e, qb=qb, h=h, i=i: e.matmul(
                                        qps[qb][0:96, :], lhsT=wuq[:, i, h * 96:(h + 1) * 96], rhs=cn[0][:, i, :],
                                        start=(i == 0), stop=(i == 1)),
                                        reads=[("cn", 0, i), "wuq"], writes=[("cqps", qb)])
                            else:
                                for i in range(2):
                                    S.op("pe", lambda e, qb=qb, h=h, i=i: e.matmul(
                                        qps[qb][0:96, :], lhsT=wukx[:, i, h, :], rhs=cn[1][:, i, :],
                                        start=(i == 0), stop=False),
                                        reads=[("cn", 1, i), "wukx"], writes=[("cqps", qb)])
                                S.op("pe", lambda e, qb=qb: e.matmul(
                                    qps[qb][0:96, :], lhsT=cbm(CB_E, 32, 96), rhs=krT[:], start=False, stop=True),
                                    reads=["krT", "cb"], writes=[("cqps", qb)])
                            def post(qb=qb, isk=isk, h=h, b=b, t0=t0, hi=hi):
                                S.op("act", lambda e, qb=qb: e.activation(out=sq[qb][:], in_=qps[qb][0:96, :], func=AF.Square),
                                     reads=[("cqps", qb)], writes=[("hsq", qb)])
                                S.op("act", lambda e, qb=qb, isk=isk: e.activation(
                                    out=qg[qb][:], in_=qps[qb][0:96, :], func=AF.Copy, scale=g96[:, isk:isk + 1]),
                                    reads=[("cqps", qb), "g96"], writes=[("hqg", qb)])
                                S.op("pe", lambda e, qb=qb: e.matmul(sps[0:96, :], lhsT=cbm(CB_ONES, 96, 96), rhs=sq[qb][:], start=True, stop=True),
                                     reads=[("hsq", qb), "cb"], writes=["csps"])
                                S.op("pe", lambda e, qb=qb: e.matmul(rps[0:96, :], lhsT=cbm(CB_R96, 96, 96), rhs=qg[qb][:], start=True, stop=True),
                                     reads=[("hqg", qb), "cb"], writes=["crps"])
                                S.op("act", lambda e, qb=qb: e.activation(
                                    out=sd[qb][:], in_=sps[0:96, :], func=AF.Sqrt, bias=epsc[0:96, 3:4], scale=1.0),
                                    reads=["csps", ("epsc", 3)], writes=[("hsd", qb)])
                                S.op("pool", lambda e, qb=qb, b=b: e.tensor_tensor(
                                    out=t1[qb][:], in0=qg[qb][:], in1=cst[b][:, 0, :], op=ALU.mult),
                                    reads=[("hqg", qb), ("ccst", b)], writes=[("ht1", qb)])
                                S.op("dve", lambda e, qb=qb, b=b: e.tensor_tensor(
                                    out=t2[qb][:], in0=rps[0:96, :], in1=cst[b][:, 1, :], op=ALU.mult),
                                    reads=["crps", ("ccst", b)], writes=[("ht2", qb)])
                                S.op("pool", lambda e, qb=qb: e.tensor_tensor(
                                    out=t1[qb][:], in0=t1[qb][:], in1=t2[qb][:], op=ALU.add),
                                    reads=[("ht1", qb), ("ht2", qb)], writes=[("ht1", qb)])
                                S.op("dve", lambda e, qb=qb: e.reciprocal(out=sd[qb][:], in_=sd[qb][:]),
                                     reads=[("hsd", qb)], writes=[("hsd", qb)])
                                fb = hi % 3
                                S.op("dve", lambda e, qb=qb, fb=fb: e.tensor_tensor(
                                    out=qf[fb][:], in0=t1[qb][:], in1=sd[qb][:], op=ALU.mult),
                                    reads=[("ht1", qb), ("hsd", qb)], writes=[("hqf", fb)])
                                dst = k1 if isk else q1
                                S.dma("sp", lambda e, fb=fb, h=h, t0=t0, dst=dst: e.dma_start(
                                    out=dst[h * 96:(h + 1) * 96, t0:t0 + 512], in_=qf[fb][:]),
                                    reads=[("hqf", fb)], chan="hqf%d" % fb)
                            if pend:
                                pend.pop()()
                            pend.append(post)
                    pend.pop()()
                    for j in range(4):
                        vb = vi % 2
                        vi += 1
                        for half in range(2):
                            for i in range(2):
                                S.op("pe", lambda e, half=half, i=i, j=j: e.matmul(
                                    vps[:], lhsT=cn[1][:, i, j * 128:(j + 1) * 128],
                                    rhs=wuv[:, i, half * 8:(half + 1) * 8, :].rearrange("p h d -> p (h d)"),
                                    start=(i == 0), stop=(i == 1)),
                                    reads=[("cn", 1, i), "wuv"], writes=["cvps"])
                            S.op("act" if half else "dve", (lambda e, vb=vb, half=half: e.activation(
                                out=vst[vb][:, half * 8:(half + 1) * 8, 0:64], in_=vps[:, :].rearrange("p (h d) -> p h d", h=8), func=AF.Copy)) if half else
                                (lambda e, vb=vb, half=half: e.tensor_copy(
                                    out=vst[vb][:, half * 8:(half + 1) * 8, 0:64], in_=vps[:, :].rearrange("p (h d) -> p h d", h=8))),
                                reads=["cvps"], writes=[("cvst", vb, half)])
                        S.dma("sp", lambda e, vb=vb, t0=t0, j=j: e.dma_start(
                            out=v1[t0 + j * 128:t0 + (j + 1) * 128, :], in_=vst[vb][:].rearrange("p h d -> p (h d)")),
                            reads=[("cvst", vb, 0), ("cvst", vb, 1), ("cvst1", vb)], chan="cvst%d" % vb)
                S.emit_pass()

        def pass_c2():
            with ExitStack() as es:
                kT = [sbt(es, "dkT%d" % i, [96, T], BF16) for i in range(2)]
                qT = [sbt(es, "dqT%d" % i, [96, T], BF16) for i in range(2)]
                vt = [sbt(es, "dvt%d" % i, [128, NT, 128], BF16) for i in range(2)]
                ost = [sbt(es, "dost%d" % i, [64, T], BF16) for i in range(2)]
                pt = [sbt(es, "dpt%d" % i, [128, 512], BF16) for i in range(4)]
                rec = [sbt(es, "drec%d" % i, [128, 512], F32) for i in range(2)]
                sps = [pst(es, "dsps%d" % i) for i in range(4)]
                ops_ = [pst(es, "dops%d" % i) for i in range(2)]
                SC = float(np.sqrt(96.0))
                NQT = T // 512
                steps = [(h, qt, kb) for h in range(16) for qt in range(NQT) for kb in range(NT)]
                LAG = 3

                def load_head(h):
                    b = h % 2
                    S.dma("sp", lambda e, b=b, h=h: e.dma_start(out=kT[b][:], in_=k1[h * 96:(h + 1) * 96, :]),
                          writes=[("dkT", b)], chan="dkT%d" % b)
                    S.dma("sp", lambda e, b=b, h=h: e.dma_start(out=qT[b][:], in_=q1[h * 96:(h + 1) * 96, :]),
                          writes=[("dqT", b)], chan="dqT%d" % b)
                    S.dma("sp", lambda e, b=b, h=h: e.dma_start(
                        out=vt[b][:], in_=v1[:, h * 128:(h + 1) * 128].rearrange("(m p) c -> p m c", p=128)),
                        writes=[("dvt", b)], chan="dvt%d" % b)

                def emit_s(i):
                    h, qt, kb = steps[i]
                    b = h % 2
                    sb_ = i % 4
                    cross = ((qt * 512 < HB) != (kb * 128 < HB))
                    S.op("pe", lambda e, sb_=sb_, b=b, kb=kb, qt=qt: e.matmul(
                        sps[sb_][:], lhsT=kT[b][:, kb * 128:(kb + 1) * 128], rhs=qT[b][:, qt * 512:(qt + 1) * 512],
                        start=True, stop=True),
                        reads=[("dkT", b), ("dqT", b)], writes=[("dsps", sb_)])
                    bias_ap = pc[:, 1:2] if cross else epsc[:, 4:5]
                    S.op("act", lambda e, sb_=sb_, bias_ap=bias_ap: e.activation(
                        out=pt[sb_][:], in_=sps[sb_][:], func=AF.Exp, bias=bias_ap, scale=SC),
                        reads=[("dsps", sb_), "pc", ("epsc", 4)], writes=[("dpt", sb_)])

                def emit_pv(i):
                    h, qt, kb = steps[i]
                    b = h % 2
                    sb_ = i % 4
                    ob = (h * NQT + qt) % 2
                    S.op("pe", lambda e, sb_=sb_, b=b, kb=kb, ob=ob: e.matmul(
                        ops_[ob][:], lhsT=vt[b][:, kb, :], rhs=pt[sb_][:], start=(kb == 0), stop=(kb == NT - 1)),
                        reads=[("dpt", sb_), ("dvt", b)], writes=[("dops", ob)])
                    if kb == NT - 1:
                        S.op("dve", lambda e, ob=ob: e.reciprocal(out=rec[ob][64:128, :], in_=ops_[ob][64:128, :]),
                             reads=[("dops", ob)], writes=[("drec", ob)])
                        S.op("dve", lambda e, ob=ob, b=b, qt=qt: e.tensor_tensor(
                            out=ost[b][:, qt * 512:(qt + 1) * 512], in0=ops_[ob][0:64, :], in1=rec[ob][64:128, :], op=ALU.mult),
                            reads=[("dops", ob), ("drec", ob)], writes=[("dost", b)])
                        if qt == NQT - 1:
                            S.dma("sp", lambda e, b=b, h=h: e.dma_start(out=cat1[h * 64:(h + 1) * 64, :], in_=ost[b][:]),
                                  reads=[("dost", b)], chan="dost%d" % b)

                load_head(0)
                for i in range(len(steps) + LAG):
                    if i < len(steps):
                        h, qt, kb = steps[i]
                        if h + 1 < 16 and kb == 0 and qt == (1 if NQT > 1 else 0) and (qt > 0 or True):
                            if not (NQT == 1):
                                load_head(h + 1)
                        emit_s(i)
                    if i >= LAG:
                        emit_pv(i - LAG)
                        if NQT == 1:
                            h2, qt2, kb2 = steps[i - LAG]
                            if kb2 == NT - 1 and h2 + 1 < 16:
                                pass
                S.emit_pass()

        if "a1" in passes:
            pass_a1()
        if "a2" in passes:
            pass_a2()
        if "a3" in passes:
            pass_outproj("a3", x_in, cat0, 6, e_w_out, x1)
        if "b" in passes:
            pass_ffn(0, x1, x2)
        if "c1" in passes:
            pass_c1()
        if "c2" in passes:
            pass_c2()
        if "c3" in passes:
            pass_outproj("c3", x2, cat1, 8, o_w_out, x3)
        if "d" in passes:
            pass_ffn(1, x3, y_out)

        S.final_wait()
    return nc


def _rope_tables(T, L):
    pos = (np.arange(T) % L).astype(np.float32)

    def tab(Dr):
        inv = np.power(np.float32(10000.0), -np.arange(0, Dr, 2, dtype=np.float32) / np.float32(Dr)).astype(np.float32)
        ang = (pos[:, None] * inv[None, :]).astype(np.float32)
        return np.cos(ang).astype(np.float32).T, np.sin(ang).astype(np.float32).T

    c64, s64 = tab(64)
    c32, s32 = tab(32)
    cs64 = np.stack([np.tile(c64, (4, 1)), np.tile(s64, (4, 1))]).astype(np.float32)
    c96 = np.concatenate([np.ones((64, T), np.float32), c32, c32])
    s96 = np.concatenate([np.zeros((64, T), np.float32), s32, s32])
    cs96 = np.stack([c96, s96]).astype(np.float32)
    return np.ascontiguousarray(cs64), np.ascontiguousarray(cs96)


def _const_mats(is_prompt):
    m = np.zeros((NCB, 128, 128), np.float32)
    m[CB_IDENT] = np.eye(128)
    k = np.arange(128)
    m[CB_B64] = (k[:, None] // 64 == k[None, :] // 64)
    for o in range(128):
        if o % 64 < 32:
            m[CB_R64][o + 32, o] = -1.0
        else:
            m[CB_R64][o - 32, o] = 1.0
    m[CB_ONES] = 1.0
    for i in range(16):
        m[CB_R96][80 + i, 64 + i] = -1.0
        m[CB_R96][64 + i, 80 + i] = 1.0
    for i in range(32):
        m[CB_E][i, 64 + i] = 1.0
    j = k[:, None]
    i = k[None, :]
    ge = (j >= i).astype(np.float32)
    le = (j <= i).astype(np.float32)
    gef = ge * (j >= 64)
    lel = le * (j < 64)
    m[CB_GE], m[CB_LE], m[CB_GEF], m[CB_LEL] = ge, le, gef, lel
    if is_prompt:
        m[CB_GEB], m[CB_LEB] = gef, lel
        m[CB_GEAB], m[CB_LEAB] = 0.0, 0.0
    else:
        m[CB_GEB], m[CB_LEB] = ge, le
        m[CB_GEAB], m[CB_LEAB] = ge, le
    out = np.ascontiguousarray(m.transpose(1, 0, 2).reshape(128, NCB * 128))
    pcv = np.zeros((128, 4), np.float32)
    pcv[:, 0] = 0.0 if is_prompt else 1.0
    pcv[:, 1] = NEGB if is_prompt else 0.0
    return out, pcv


_WNAMES = ("e_norm", "e_w_in", "e_a_q_gain", "e_a_k_gain", "e_a_sink", "e_b_q_gain", "e_b_k_gain", "e_w_out",
           "o_norm", "o_w_in", "o_q_lora_gain", "o_w_uq", "o_kv_gain", "o_w_ukv", "o_q_gain", "o_k_gain", "o_w_out")
_FNAMES = ("f_norm", "f_w_up", "f_conv_w", "f_conv_b", "f_w_down")


def make_in_maps(streams, T, weights):
    wmap = {}
    for n in _WNAMES:
        a = np.asarray(weights[n], np.float32)
        wmap[n] = np.ascontiguousarray(a.reshape(a.shape[1:]))
    for n in _FNAMES:
        wmap[n] = np.ascontiguousarray(np.asarray(weights[n], np.float32))
    tabs = {}
    in_maps = []
    for (xs, is_p) in streams:
        if is_p not in tabs:
            cs64, cs96 = _rope_tables(T, T // 2 if is_p else T)
            cm, pcv = _const_mats(is_p)
            tabs[is_p] = (cs64, cs96, cm, pcv)
        cs64, cs96, cm, pcv = tabs[is_p]
        m = dict(wmap)
        m.update(x=np.ascontiguousarray(xs, dtype=np.float32), cs64=cs64, cs96=cs96, consts=cm, percore=pcv)
        in_maps.append(m)
    return in_maps


_PROG = {}


def kernel(**inputs):
    T = 8192
    xp = np.asarray(inputs["x_prompt"], np.float32)
    xs = np.asarray(inputs["x_sample"], np.float32)
    streams = []
    for c in range(4):
        streams.append((xp[2 * c:2 * c + 2].reshape(T, D), True))
    for c in range(4):
        streams.append((xs[c], False))
    in_maps = make_in_maps(streams, T, inputs)
    if T not in _PROG:
        _PROG[T] = build_program(T)
    res = run_bass_kernel_spmd(_PROG[T], in_maps, core_ids=list(range(8)))
    ys = [np.asarray(r["y"], np.float32) for r in res.results]
    y_prompt = np.stack([ys[c].reshape(2, T // 2, D) for c in range(4)]).reshape(8, T // 2, D)
    y_sample = np.stack(ys[4:8])
    return (y_prompt, y_sample)
```

```python
import os
import numpy as np
from contextlib import ExitStack
import concourse.bass as bass
import concourse.mybir as mybir
from concourse.bass_utils import run_bass_kernel_spmd

F32 = mybir.dt.float32
BF16 = mybir.dt.bfloat16
AF = mybir.ActivationFunctionType
ALU = mybir.AluOpType

D = 1024
EPS = 1e-6
DFF = 2816
NFC = DFF // 128
PAD = 1024
NQK = 18
NEGB = -30000.0

ENGS = ("pe", "act", "dve", "pool", "sp")
STRICT = not os.environ.get("NOSTRICT")
PENG = "dve" if os.environ.get("POOL2DVE") else "pool"
ENG_FFN_CONV = os.environ.get("ENG_FFN_CONV", "dve")
ENG_T1 = os.environ.get("ENG_T1", "pool")
ENG_ADD = os.environ.get("ENG_ADD", "pool")
ENG_MASK = os.environ.get("ENG_MASK", "dve")


class Sched:
    def __init__(self, nc, es):
        self.nc = nc
        self.es = es
        self.sem = {e: es.enter_context(nc.semaphore("sem_" + e)) for e in ("pe", "act", "dve", "pool")}
        self.sigcnt = {e: 0 for e in self.sem}
        self.chans = {}
        self.waited = {e: {} for e in ENGS}
        self.ops = []
        self.last_w = {}
        self.readers = {}
        self.bar_val = 0
        self.n_inst = 0

    def chan(self, name):
        if name not in self.chans:
            self.chans[name] = dict(sem=self.es.enter_context(self.nc.semaphore("ch_" + name)), issued=0, name=name)
        return self.chans[name]

    def _add(self, o, reads, writes):
        idx = len(self.ops)
        deps = []
        seen = set()

        def add_dep(d, raw):
            if d in seen:
                return
            seen.add(d)
            deps.append((d, raw))

        for k in reads:
            if k in self.last_w:
                add_dep(self.last_w[k], True)
        for k in writes:
            if k in self.last_w:
                lw = self.ops[self.last_w[k]]
                if not (o["dma"] and lw["dma"] and lw["chan"] is o["chan"] and lw["eng"] == o["eng"]):
                    add_dep(self.last_w[k], False)
            for r in self.readers.get(k, ()):
                add_dep(r, False)
        for k in reads:
            self.readers.setdefault(k, []).append(idx)
        for k in writes:
            self.last_w[k] = idx
            self.readers[k] = []
        need = []
        for d, raw in deps:
            od = self.ops[d]
            if od["dma"]:
                need.append(("ch", od["chan"], od["chan"]["issued"]))
            else:
                if od["eng"] == o["eng"] and not o["dma"]:
                    if o["eng"] == "pe" or (not raw and not STRICT):
                        continue
                od["sig"] = True
                need.append(("op", d))
        o["need"] = need
        self.ops.append(o)
        return idx

    def op(self, eng, fn, reads=(), writes=()):
        o = dict(eng=eng, fn=fn, dma=False, sig=False)
        return self._add(o, reads, writes)

    def dma(self, eng, fn, reads=(), writes=(), chan=None):
        ch = self.chan(chan)
        o = dict(eng=eng, fn=fn, dma=True, sig=False, chan=ch)
        idx = self._add(o, reads, writes)
        ch["issued"] += 1
        return idx

    def alias(self, key, idx):
        self.last_w[key] = idx
        self.readers[key] = []

    def emit_pass(self):
        nc = self.nc
        ops = self.ops
        last = {}
        for i, o in enumerate(ops):
            if not o["dma"]:
                last[o["eng"]] = i
        for e, i in last.items():
            ops[i]["sig"] = True
        for o in ops:
            if not o["dma"] and o["sig"]:
                self.sigcnt[o["eng"]] += 1
                o["sigval"] = self.sigcnt[o["eng"]]
        per = {e: [o for o in ops if o["eng"] == e] for e in ENGS}
        bar_prev = self.bar_val
        self.sigcnt["dve"] += 1
        bar_new = self.sigcnt["dve"]
        sched = self

        def wait(e, eng, sem, key, val):
            w = sched.waited[e]
            if w.get(key, 0) >= val:
                return
            w[key] = val
            eng.wait_ge(sem, val)
            sched.n_inst += 1

        def run_engine(e, eng):
            if bar_prev > 0:
                wait(e, eng, sched.sem["dve"], "dve", bar_prev)
            for o in per[e]:
                for n in o["need"]:
                    if n[0] == "ch":
                        wait(e, eng, n[1]["sem"], "ch_" + n[1]["name"], 16 * n[2])
                    else:
                        od = ops[n[1]]
                        wait(e, eng, sched.sem[od["eng"]], od["eng"], od["sigval"])
                ins = o["fn"](eng)
                sched.n_inst += 1
                if o["dma"]:
                    ins.then_inc(o["chan"]["sem"], 16)
                elif o["sig"]:
                    ins.then_inc(sched.sem[e], 1)
            if e == "dve":
                for e2 in ("pe", "act", "pool"):
                    if sched.sigcnt[e2] > 0:
                        wait(e, eng, sched.sem[e2], e2, sched.sigcnt[e2])
                for ch in sched.chans.values():
                    if ch["issued"] > 0:
                        wait(e, eng, ch["sem"], "ch_" + ch["name"], 16 * ch["issued"])
                eng.memset(sched.bar_tile[0:1, 0:1], 0.0).then_inc(sched.sem["dve"], 1)
            else:
                pass

        with nc.Block() as block:
            @block.tensor
            def _(eng):
                run_engine("pe", eng)

            @block.scalar
            def _(eng):
                run_engine("act", eng)

            @block.vector
            def _(eng):
                run_engine("dve", eng)

            @block.gpsimd
            def _(eng):
                run_engine("pool", eng)

            @block.sync
            def _(eng):
                run_engine("sp", eng)

        self.bar_val = bar_new
        self.ops = []
        self.last_w = {}
        self.readers = {}

    def final_wait(self):
        nc = self.nc
        sched = self
        with nc.Block() as block:
            @block.sync
            def _(eng):
                eng.wait_ge(sched.sem["dve"], sched.bar_val)

            @block.tensor
            def _(eng):
                eng.wait_ge(sched.sem["dve"], sched.bar_val)

            @block.scalar
            def _(eng):
                eng.wait_ge(sched.sem["dve"], sched.bar_val)

            @block.gpsimd
            def _(eng):
                eng.wait_ge(sched.sem["dve"], sched.bar_val)


CB_IDENT, CB_B64, CB_R64, CB_ONES, CB_R96, CB_E, CB_GE, CB_LE, CB_GEF, CB_LEL, CB_GEB, CB_LEB, CB_GEAB, CB_LEAB = range(14)
NCB = 14


def build_program(T, debug=False, passes=None):
    nc = bass.Bass("TRN2", target_bir_lowering=False)
    G = T // 512
    NT = T // 128
    HB = T // 2
    SEG = 2048
    NSEG = T // SEG
    allp = ("a1", "a2", "a3", "b", "c1", "c2", "c3", "d")
    passes = allp if passes is None else passes

    def din(name, shape):
        return nc.dram_tensor(name, list(shape), F32, kind="ExternalInput").ap()

    def dscr(name, shape, dt):
        kind = "ExternalOutput" if (debug and name in debug) else "Internal"
        return nc.dram_tensor(name, list(shape), dt, kind=kind).ap()

    x_in = din("x", [T, D])
    cs64 = din("cs64", [2, 128, T])
    cs96 = din("cs96", [2, 96, T])
    consts = din("consts", [128, NCB * 128])
    percore = din("percore", [128, 4])
    e_norm = din("e_norm", [D])
    e_w_in = din("e_w_in", [D, 3072])
    e_a_q_gain = din("e_a_q_gain", [64])
    e_a_k_gain = din("e_a_k_gain", [64])
    e_a_sink = din("e_a_sink", [8])
    e_b_q_gain = din("e_b_q_gain", [3, 64])
    e_b_k_gain = din("e_b_k_gain", [3, 64])
    e_w_out = din("e_w_out", [768, D])
    o_norm = din("o_norm", [D])
    o_w_in = din("o_w_in", [D, 544])
    o_q_lora_gain = din("o_q_lora_gain", [256])
    o_w_uq = din("o_w_uq", [256, 1536])
    o_kv_gain = din("o_kv_gain", [256])
    o_w_ukv = din("o_w_ukv", [256, 2048])
    o_q_gain = din("o_q_gain", [96])
    o_k_gain = din("o_k_gain", [96])
    o_w_out = din("o_w_out", [D, D])
    f_norm = din("f_norm", [2, D])
    f_w_up = din("f_w_up", [2, D, 2 * DFF])
    f_conv_w = din("f_conv_w", [2, 3, DFF])
    f_conv_b = din("f_conv_b", [2, DFF])
    f_w_down = din("f_w_down", [2, DFF, D])

    y_out = nc.dram_tensor("y", [T, D], F32, kind="ExternalOutput").ap()

    TP = T + 2 * PAD
    qk0 = dscr("qk0", [NQK * 128, TP], BF16)
    v0 = dscr("v0", [TP, 896], BF16)
    cat0 = dscr("cat0", [768, T], BF16)
    x1 = dscr("x1", [T, D], F32)
    x2 = dscr("x2", [T, D], F32)
    q1 = dscr("q1", [16 * 96, T], BF16)
    k1 = dscr("k1", [16 * 96, T], BF16)
    v1 = dscr("v1", [T, 16 * 128], BF16)
    cat1 = dscr("cat1", [D, T], BF16)
    x3 = dscr("x3", [T, D], F32)

    es_glob = ExitStack()
    with es_glob:
        S = Sched(nc, es_glob)
        S.bar_tile = es_glob.enter_context(nc.sbuf_tensor("bar_tile", [1, 8], F32))
        cb = es_glob.enter_context(nc.sbuf_tensor("cb", [128, NCB, 128], BF16))
        pc = es_glob.enter_context(nc.sbuf_tensor("pc", [128, 4], F32))
        epsc = es_glob.enter_context(nc.sbuf_tensor("epsc", [128, 8], F32))
        ncd = nc.allow_non_contiguous_dma(reason="tiny parameter loads")
        es_glob.enter_context(ncd)

        def cbm(i, p=128, f=128):
            return cb[0:p, i, 0:f]

        uniq = [0]

        def sbt(es, name, shape, dt):
            uniq[0] += 1
            return es.enter_context(nc.sbuf_tensor("%s_%d" % (name, uniq[0]), list(shape), dt))

        def pst(es, name, dt=F32, cols=512):
            uniq[0] += 1
            return es.enter_context(nc.psum_tensor("%s_%d" % (name, uniq[0]), [128, cols], dt))

        with ExitStack() as es:
            zt = sbt(es, "zt", [128, 8 * 896], BF16)
            S.dma("pool", lambda e: e.dma_start(out=cb[:].rearrange("p a b -> p (a b)"), in_=consts[:, :]),
                  writes=["cb"], chan="cb")
            S.dma("sp", lambda e: e.dma_start(out=pc[:], in_=percore[:, :]), writes=["pc"], chan="pc")
            S.op("pool", lambda e: e.memset(zt[:], 0.0), writes=["zt"])
            for i, v in enumerate((D * EPS, 64 * EPS, 256 * EPS, 96 * EPS, 0.0)):
                S.op("pool", lambda e, i=i, v=v: e.memset(epsc[:, i:i + 1], float(v)), writes=[("epsc", i)])
            if "a1" in passes:
                for side in range(2):
                    c0 = 0 if side == 0 else PAD + T
                    for kc in range(NQK):
                        S.dma("sp", lambda e, kc=kc, c0=c0: e.dma_start(
                            out=qk0[kc * 128:(kc + 1) * 128, c0:c0 + PAD], in_=zt[:, 0:PAD]),
                            reads=["zt"], chan="zpad")
                    S.dma("sp", lambda e, c0=c0: e.dma_start(
                        out=v0[c0:c0 + PAD, :].rearrange("(p a) c -> p (a c)", p=128), in_=zt[:, 0:8 * 896]),
                        reads=["zt"], chan="zpad")
            S.emit_pass()

        def load_gain_cols(es, name, src_vec, n, chunks, mul, chan):
            t = sbt(es, name, [128, chunks], F32)
            S.dma("sp", lambda e: e.dma_start(out=t[:], in_=src_vec.rearrange("(c p) -> p c", p=128)),
                  writes=[name], chan=chan)
            if mul != 1.0:
                S.op("pool", lambda e: e.tensor_scalar(out=t[:], in0=t[:], scalar1=float(mul), scalar2=None, op0=ALU.mult),
                     reads=[name], writes=[name])
            return t

        def norm_transpose(es_unused, bufs, gi, src_dram, t0, ntok_tiles, gcol, xnT, xnT_key, col0, keep_x=None):
            pass

        def pass_a1():
            with ExitStack() as es:
                wqk = sbt(es, "wqk", [128, 8, NQK * 128], BF16)
                wv = sbt(es, "wv", [128, 8, 896], BF16)
                colmap = []
                for c in range(4):
                    colmap.append((c, 0, c * 128, 128))
                colmap += [(4, 0, 512, 64), (4, 64, 512, 64), (5, 0, 576, 64), (5, 64, 576, 64)]
                for g in range(3):
                    base = 768 + g * 768
                    for c in range(2):
                        colmap.append((6 + g * 4 + c, 0, base + c * 128, 128))
                        colmap.append((6 + g * 4 + 2 + c, 0, base + 256 + c * 128, 128))
                w_in_v = e_w_in.rearrange("(c p) n -> p c n", p=128)
                for (dc, do, sc, n) in colmap:
                    S.dma("pool", lambda e, dc=dc, do=do, sc=sc, n=n: e.dma_start(
                        out=wqk[:, :, dc * 128 + do: dc * 128 + do + n], in_=w_in_v[:, :, sc:sc + n]),
                        writes=[("wqk", dc, do)], chan="wqk")
                S.alias("wqk", len(S.ops) - 1)
                vmap = [(0, 640, 128)] + [(128 + g * 256, 768 + g * 768 + 512, 256) for g in range(3)]
                for (do, sc, n) in vmap:
                    S.dma("pool", lambda e, do=do, sc=sc, n=n: e.dma_start(
                        out=wv[:, :, do:do + n], in_=w_in_v[:, :, sc:sc + n]), writes=[("wv", do)], chan="wv")
                S.alias("wv", len(S.ops) - 1)
                gn = load_gain_cols(es, "gn", e_norm, D, 8, 32.0, "gn")
                gq = sbt(es, "gq", [128, NQK], F32)
                gsrc = [e_a_q_gain] * 4 + [e_a_k_gain] * 2
                for g in range(3):
                    gsrc += [e_b_q_gain[g], e_b_q_gain[g], e_b_k_gain[g], e_b_k_gain[g]]
                for c, gs in enumerate(gsrc):
                    for h in range(2):
                        S.dma("sp", lambda e, c=c, gs=gs, h=h: e.dma_start(
                            out=gq[h * 64:(h + 1) * 64, c:c + 1], in_=gs.rearrange("(p o) -> p o", o=1)),
                            writes=[("gq", c, h)], chan="gq")
                S.alias("gq", len(S.ops) - 1)
                xt = [sbt(es, "xt%d" % i, [128, 4, D], F32) for i in range(2)]
                xn = sbt(es, "xn", [128, 4, D], BF16)
                junk = [sbt(es, "junk%d" % i, [128, D], BF16) for i in range(4)]
                ssq = [sbt(es, "ssq%d" % i, [128, 4], F32) for i in range(2)]
                rst = [sbt(es, "rst%d" % i, [128, 4], F32) for i in range(2)]
                xnT = [sbt(es, "xnT%d" % i, [128, 8, 512], BF16) for i in range(2)]
                cst = [sbt(es, "cst%d" % i, [128, 2, 512], F32) for i in range(2)]
                sq = [sbt(es, "sq%d" % i, [128, 512], BF16) for i in range(2)]
                qg = [sbt(es, "qg%d" % i, [128, 512], BF16) for i in range(2)]
                sd = [sbt(es, "sd%d" % i, [128, 512], F32) for i in range(2)]
                t1 = [sbt(es, "t1%d" % i, [128, 512], F32) for i in range(2)]
                t2 = [sbt(es, "t2%d" % i, [128, 512], F32) for i in range(2)]
                t3 = [sbt(es, "t3%d" % i, [128, 512], F32) for i in range(2)]
                qf = [sbt(es, "qf%d" % i, [128, 512], BF16) for i in range(3)]
                vst = [sbt(es, "vst%d" % i, [128, 896], BF16) for i in range(2)]
                ptr = [pst(es, "ptr%d" % i, BF16, 1024) for i in range(2)]
                qps = [pst(es, "qps%d" % i) for i in range(2)]
                sps = pst(es, "sps")
                rps = pst(es, "rps")
                vps = [pst(es, "vps%d" % i) for i in range(2)]
                ci = 0
                vi = 0
                for g in range(G):
                    t0 = g * 512
                    b = g % 2
                    S.dma("sp", lambda e, b=b, t0=t0: e.dma_start(
                        out=xt[b][:], in_=x_in[t0:t0 + 512, :].rearrange("(j p) d -> p j d", p=128)),
                        writes=[("xt", b)], chan="xt%d" % b)
                    S.dma("sp", lambda e, b=b, t0=t0: e.dma_start(
                        out=cst[b][:], in_=cs64[:, :, t0:t0 + 512].rearrange("a p t -> p a t")),
                        writes=[("cst", b)], chan="cst%d" % b)
                    S.op("pool", lambda e, b=b: e.memset(ssq[b][:], 0.0), writes=[("ssq", b, j) for j in range(4)])
                    for j in range(4):
                        S.op("act", lambda e, b=b, j=j: e.activation(
                            out=junk[j][:], in_=xt[b][:, j, :], func=AF.Square, accum_out=ssq[b][:, j:j + 1]),
                            reads=[("xt", b), ("ssq", b, j)], writes=[("ssq", b, j), ("junk", j)])
                    S.op("act", lambda e, b=b: e.activation(
                        out=rst[b][:], in_=ssq[b][:], func=AF.Sqrt, bias=epsc[:, 0:1], scale=1.0),
                        reads=[("ssq", b, j) for j in range(4)] + [("epsc", 0)], writes=[("rst", b)])
                    S.op("dve", lambda e, b=b: e.reciprocal(out=rst[b][:], in_=rst[b][:]),
                         reads=[("rst", b)], writes=[("rst", b)])
                    for j in range(4):
                        S.op("dve" if j % 2 else "pool", lambda e, b=b, j=j: e.tensor_scalar(
                            out=xn[:, j, :], in0=xt[b][:, j, :], scalar1=rst[b][:, j:j + 1], scalar2=None, op0=ALU.mult),
                            reads=[("xt", b), ("rst", b)], writes=[("xn", j)])
                    for c in range(8):
                        pb = c % 2
                        for j in range(4):
                            S.op("pe", lambda e, pb=pb, c=c, j=j: e.transpose(
                                ptr[pb][:, j * 128:(j + 1) * 128], xn[:, j, c * 128:(c + 1) * 128], cbm(CB_IDENT)),
                                reads=[("xn", j), "cb"], writes=[("ptr", pb)])
                        S.op("act" if c % 2 else "dve", (lambda e, pb=pb, c=c, b=b: e.activation(
                            out=xnT[b][:, c, :], in_=ptr[pb][:, 0:512], func=AF.Copy, scale=gn[:, c:c + 1])) if c % 2 else
                            (lambda e, pb=pb, c=c, b=b: e.tensor_scalar(
                                out=xnT[b][:, c, :], in0=ptr[pb][:, 0:512], scalar1=gn[:, c:c + 1], scalar2=None, op0=ALU.mult)),
                            reads=[("ptr", pb), "gn"], writes=[("xnT", b, c)])
                    xk = [("xnT", b, c) for c in range(8)]
                    pend = []
                    for fc in range(NQK):
                        qb = ci % 2
                        ci += 1
                        for c in range(8):
                            S.op("pe", lambda e, qb=qb, fc=fc, c=c, b=b: e.matmul(
                                qps[qb][:], lhsT=wqk[:, c, fc * 128:(fc + 1) * 128], rhs=xnT[b][:, c, :],
                                start=(c == 0), stop=(c == 7)),
                                reads=[("xnT", b, c), "wqk"], writes=[("qps", qb)])
                        def post(qb=qb, fc=fc, b=b, g=g, t0=t0):
                            S.op("act", lambda e, qb=qb: e.activation(out=sq[qb][:], in_=qps[qb][:], func=AF.Square),
                                 reads=[("qps", qb)], writes=[("sq", qb)])
                            S.op("act", lambda e, qb=qb, fc=fc: e.activation(
                                out=qg[qb][:], in_=qps[qb][:], func=AF.Copy, scale=gq[:, fc:fc + 1]),
                                reads=[("qps", qb), "gq"], writes=[("qg", qb)])
                            S.op("pe", lambda e, qb=qb: e.matmul(sps[:], lhsT=cbm(CB_B64), rhs=sq[qb][:], start=True, stop=True),
                                 reads=[("sq", qb), "cb"], writes=["sps"])
                            S.op("pe", lambda e, qb=qb: e.matmul(rps[:], lhsT=cbm(CB_R64), rhs=qg[qb][:], start=True, stop=True),
                                 reads=[("qg", qb), "cb"], writes=["rps"])
                            S.op("act", lambda e, qb=qb: e.activation(
                                out=sd[qb][:], in_=sps[:], func=AF.Sqrt, bias=epsc[:, 1:2], scale=1.0),
                                reads=["sps", ("epsc", 1)], writes=[("sd", qb)])
                            S.op(ENG_T1, lambda e, qb=qb, b=b: e.tensor_tensor(
                                out=t1[qb][:], in0=qg[qb][:], in1=cst[b][:, 0, :], op=ALU.mult),
                                reads=[("qg", qb), ("cst", b)], writes=[("t1", qb)])
                            S.op("dve", lambda e, qb=qb, b=b: e.tensor_tensor(
                                out=t2[qb][:], in0=rps[:], in1=cst[b][:, 1, :], op=ALU.mult),
                                reads=["rps", ("cst", b)], writes=[("t2", qb)])
                            S.op(ENG_ADD, lambda e, qb=qb: e.tensor_tensor(
                                out=t3[qb][:], in0=t1[qb][:], in1=t2[qb][:], op=ALU.add),
                                reads=[("t1", qb), ("t2", qb)], writes=[("t3", qb)])
                            fb = (g * NQK + fc) % 3
                            S.op("dve", lambda e, qb=qb: e.reciprocal(out=sd[qb][:], in_=sd[qb][:]),
                                 reads=[("sd", qb)], writes=[("sd", qb)])
                            S.op("dve", lambda e, qb=qb, fb=fb: e.tensor_tensor(
                                out=qf[fb][:], in0=t3[qb][:], in1=sd[qb][:], op=ALU.mult),
                                reads=[("t3", qb), ("sd", qb)], writes=[("qf", fb)])
                            S.dma("sp", lambda e, fb=fb, fc=fc, t0=t0: e.dma_start(
                                out=qk0[fc * 128:(fc + 1) * 128, PAD + t0:PAD + t0 + 512], in_=qf[fb][:]),
                                reads=[("qf", fb)], chan="qf%d" % fb)
                        if pend:
                            pend.pop()()
                        pend.append(post)
                    pend.pop()()
                    for j in range(4):
                        vb = vi % 2
                        vi += 1
                        for (c0, c1, pi) in ((0, 512, 0), (512, 896, 1)):
                            for c in range(8):
                                S.op("pe", lambda e, pi=pi, c0=c0, c1=c1, c=c, b=b, j=j: e.matmul(
                                    vps[pi][:, 0:c1 - c0], lhsT=xnT[b][:, c, j * 128:(j + 1) * 128], rhs=wv[:, c, c0:c1],
                                    start=(c == 0), stop=(c == 7)),
                                    reads=[("xnT", b, c), "wv"], writes=[("vps", pi)])
                            S.op("act" if pi else "dve", (lambda e, pi=pi, c0=c0, c1=c1, vb=vb: e.activation(
                                out=vst[vb][:, c0:c1], in_=vps[pi][:, 0:c1 - c0], func=AF.Copy)) if pi else
                                (lambda e, pi=pi, c0=c0, c1=c1, vb=vb: e.tensor_copy(out=vst[vb][:, c0:c1], in_=vps[pi][:, 0:c1 - c0])),
                                reads=[("vps", pi)], writes=[("vst", vb, pi)])
                        S.dma("sp", lambda e, vb=vb, t0=t0, j=j: e.dma_start(
                            out=v0[PAD + t0 + j * 128:PAD + t0 + (j + 1) * 128, :], in_=vst[vb][:]),
                            reads=[("vst", vb, 0), ("vst", vb, 1)], chan="vst%d" % vb)
                S.emit_pass()

        def pass_a2():
            with ExitStack() as es:
                qTA = sbt(es, "qTA", [128, 4, SEG], BF16)
                kTA = sbt(es, "kTA", [128, 2, SEG + 2 * PAD], BF16)
                vtA = sbt(es, "vtA", [128, 18, 2, 128], BF16)
                qTB = sbt(es, "qTB", [128, 2, SEG], BF16)
                kTB = sbt(es, "kTB", [128, 2, SEG + 2 * PAD], BF16)
                vtB = sbt(es, "vtB", [128, 32, 4, 128], BF16)
                acc = sbt(es, "acc", [128, 4, SEG], F32)
                recB = sbt(es, "recB", [128, SEG], F32)
                ostA = sbt(es, "ostA", [128, 4, SEG], BF16)
                ostB = sbt(es, "ostB", [128, 2, SEG], BF16)
                pt = [sbt(es, "pt%d" % i, [128, 512], BF16) for i in range(3)]
                dn = [sbt(es, "dn%d" % i, [128, 512], F32) for i in range(2)]
                snk = sbt(es, "snk", [128, 8], F32)
                spsX = [pst(es, "s2psx%d" % i) for i in range(2)]
                spsY = [pst(es, "s2psy%d" % i) for i in range(2)]
                ops_ = [pst(es, "o2ps%d" % i) for i in range(2)]
                S.op("pool", lambda e: e.memset(vtA[:, :, :, 64:128], 1.0), writes=["vtA1"])
                S.op("pool", lambda e: e.memset(vtB[:, :, :, 64:128], 1.0), writes=["vtB1"])
                S.dma("sp", lambda e: e.dma_start(out=snk[:], in_=e_a_sink.partition_broadcast(128)), writes=["snk"], chan="snk")
                S.op("act", lambda e: e.activation(out=snk[:], in_=snk[:], func=AF.Exp), reads=["snk"], writes=["snk"])
                cnt = dict(p=0, s=0, o=0, d=0)
                pend_pv = []

                def flush_pv():
                    while pend_pv:
                        pend_pv.pop(0)()

                def attend(items, mask, vt_single, o_t, first, last):
                    order = [i for i in range(len(items)) if items[i][0] == 0] + [i for i in range(len(items)) if items[i][0] != 0]
                    nx = sum(1 for it in items if it[0] == 0)
                    n = len(items)
                    si = cnt["s"] % 2
                    cnt["s"] += 1
                    pi = cnt["p"] % 3
                    cnt["p"] += 1
                    for col, idx in enumerate(order):
                        po, l, r, rk, _ = items[idx]
                        bank = spsX[si] if po == 0 else spsY[si]
                        bkey = ("spsX", si) if po == 0 else ("spsY", si)
                        cc = col if po == 0 else col - nx
                        S.op("pe", lambda e, l=l, r=r, cc=cc, bank=bank: e.matmul(
                            bank[:, cc * 128:(cc + 1) * 128], lhsT=l, rhs=r, start=True, stop=True),
                            reads=rk, writes=[bkey])
                    flush_pv()
                    if nx > 0:
                        S.op("act", lambda e, si=si, pi=pi, nx=nx: e.activation(
                            out=pt[pi][:, 0:nx * 128], in_=spsX[si][:, 0:nx * 128], func=AF.Exp, scale=8.0),
                            reads=[("spsX", si)], writes=[("pt", pi, 0)])
                    if n - nx > 0:
                        S.op("act", lambda e, si=si, pi=pi, nx=nx, n=n: e.activation(
                            out=pt[pi][:, nx * 128:n * 128], in_=spsY[si][:, 0:(n - nx) * 128], func=AF.Exp, scale=8.0),
                            reads=[("spsY", si)], writes=[("pt", pi, 1)])
                    pk = [("pt", pi, 0), ("pt", pi, 1)]
                    if mask is not None:
                        S.op(ENG_MASK, lambda e, pi=pi, n=n, mask=mask: e.tensor_tensor(
                            out=pt[pi][:, 0:n * 128].rearrange("p (a b) -> p a b", a=n),
                            in0=pt[pi][:, 0:n * 128].rearrange("p (a b) -> p a b", a=n),
                            in1=cbm(mask).unsqueeze(1).broadcast_to([128, n, 128]), op=ALU.mult),
                            reads=pk + ["cb"], writes=pk)
                    def emit_pv(vt_single=vt_single, items=items, order=order, pi=pi, o_t=o_t, n=n, first=first, last=last, pk=pk):
                        if vt_single is not None:
                            vl, vk = vt_single
                            S.op("pe", lambda e, vl=vl, pi=pi, o_t=o_t, n=n: e.matmul(
                                ops_[o_t][:, 0:n * 128], lhsT=vl, rhs=pt[pi][:, 0:n * 128], start=first, stop=last),
                                reads=pk + vk, writes=[("ops", o_t)])
                        else:
                            for col, idx in enumerate(order):
                                vl, vk = items[idx][4]
                                st_ = first and col == 0
                                sp_ = last and col == n - 1
                                S.op("pe", lambda e, vl=vl, col=col, pi=pi, o_t=o_t, st_=st_, sp_=sp_: e.matmul(
                                    ops_[o_t][:, col * 128:(col + 1) * 128], lhsT=vl, rhs=pt[pi][:, col * 128:(col + 1) * 128],
                                    start=st_, stop=sp_),
                                    reads=pk + vk, writes=[("ops", o_t)])
                    pend_pv.append(emit_pv)
                    return order

                for sg in range(NSEG):
                    tseg = sg * SEG
                    S.dma("sp", lambda e, tseg=tseg: e.dma_start(
                        out=qTA[:], in_=qk0[0:512, PAD + tseg:PAD + tseg + SEG].rearrange("(c p) t -> p c t", p=128)),
                        writes=["qTA"], chan="qTA")
                    S.dma("sp", lambda e, tseg=tseg: e.dma_start(
                        out=kTA[:], in_=qk0[512:768, tseg:tseg + SEG + 2 * PAD].rearrange("(c p) t -> p c t", p=128)),
                        writes=["kTA"], chan="kTA")
                    for h in range(2):
                        S.dma("sp", lambda e, tseg=tseg, h=h: e.dma_start(
                            out=vtA[:, :, h, 0:64],
                            in_=v0[PAD + tseg - 128:PAD + tseg - 128 + 18 * 128, h * 64:(h + 1) * 64].rearrange("(m j) d -> j m d", j=128)),
                            writes=["vtA"], chan="vtA")
                    for nn in range(SEG // 128 if not os.environ.get("SKIP_A") else 0):
                        tb = tseg + nn * 128
                        tiles = []
                        if tb > 0:
                            tiles.append((-1, CB_GEAB if tb == HB else CB_GE))
                        tiles.append((0, None))
                        if tb + 128 < T:
                            tiles.append((1, CB_LEAB if tb + 128 == HB else CB_LE))
                        for hkv in range(2):
                            ot = cnt["o"] % 2
                            cnt["o"] += 1
                            for ti, (off, mask) in enumerate(tiles):
                                kc0 = PAD + 128 * (nn + off)
                                items = []
                                for g in range(4):
                                    hq = 4 * hkv + g
                                    po = (hq % 2) * 64
                                    items.append((po, kTA[po:po + 64, hkv, kc0:kc0 + 128],
                                                  qTA[po:po + 64, hq // 2, nn * 128:(nn + 1) * 128], ["kTA", "qTA"], None))
                                order = attend(items, mask, (vtA[:, nn + 1 + off, hkv, :], ["vtA", "vtA1"]),
                                               ot, ti == 0, ti == len(tiles) - 1)
                            flush_pv()
                            di = cnt["d"] % 2
                            cnt["d"] += 1
                            for col, g in enumerate(order):
                                hq = 4 * hkv + g
                                S.op("dve", lambda e, di=di, ot=ot, col=col, hq=hq: e.tensor_scalar(
                                    out=dn[di][64:128, col * 128:(col + 1) * 128], in0=ops_[ot][64:128, col * 128:(col + 1) * 128],
                                    scalar1=snk[64:128, hq:hq + 1], scalar2=None, op0=ALU.add),
                                    reads=[("ops", ot), "snk"], writes=[("dn", di, col)])
                            S.op("dve", lambda e, di=di: e.reciprocal(out=dn[di][64:128, :], in_=dn[di][64:128, :]),
                                 reads=[("dn", di, g) for g in range(4)], writes=[("dn", di, g) for g in range(4)])
                            for col, g in enumerate(order):
                                hq = 4 * hkv + g
                                po = (hq % 2) * 64
                                S.op("dve", lambda e, di=di, ot=ot, col=col, hq=hq, po=po, nn=nn: e.tensor_tensor(
                                    out=ostA[po:po + 64, hq // 2, nn * 128:(nn + 1) * 128],
                                    in0=ops_[ot][0:64, col * 128:(col + 1) * 128], in1=dn[di][64:128, col * 128:(col + 1) * 128], op=ALU.mult),
                                    reads=[("ops", ot), ("dn", di, col)], writes=[("ostA", hq, nn)])
                    S.dma("sp", lambda e, tseg=tseg: e.dma_start(
                        out=cat0[0:512, tseg:tseg + SEG].rearrange("(c p) t -> p c t", p=128), in_=ostA[:]),
                        reads=[("ostA", hq, nn) for hq in range(8) for nn in range(SEG // 128)], chan="ostA")
                    for g, r in enumerate((1, 4, 16)):
                        if os.environ.get("SKIP_B") and str(g) in os.environ.get("SKIP_B"):
                            continue
                        nblk = SEG // (128 * r)
                        Fblocks = T // (128 * r)
                        nbnd = HB // (128 * r)
                        qrow = (6 + 4 * g) * 128
                        S.dma("sp", lambda e, tseg=tseg, qrow=qrow: e.dma_start(
                            out=qTB[:], in_=qk0[qrow:qrow + 256, PAD + tseg:PAD + tseg + SEG].rearrange("(c p) t -> p c t", p=128)),
                            writes=["qTB"], chan="qTB")
                        S.dma("sp", lambda e, tseg=tseg, qrow=qrow: e.dma_start(
                            out=kTB[:], in_=qk0[qrow + 256:qrow + 512, tseg:tseg + SEG + 2 * PAD].rearrange("(c p) t -> p c t", p=128)),
                            writes=["kTB"], chan="kTB")
                        vcol = 128 + 256 * g
                        for c in range(r):
                            base = PAD + tseg - 64 * r + c
                            nrow = (nblk + 1) * 128
                            for h in range(4):
                                S.dma("sp", lambda e, base=base, nrow=nrow, r=r, c=c, nblk=nblk, vcol=vcol, h=h: e.dma_start(
                                    out=vtB[:, c * (nblk + 1):(c + 1) * (nblk + 1), h, 0:64],
                                    in_=v0[base:base + (nrow - 1) * r + 1:r, vcol + h * 64:vcol + (h + 1) * 64].rearrange("(m j) d -> j m d", j=128)),
                                    writes=["vtB"], chan="vtB")
                        for nn in range(nblk):
                            n = nblk * sg + nn
                            for c in range(r):
                                ot = cnt["o"] % 2
                                cnt["o"] += 1
                                for which in range(2):
                                    mm = nn + which
                                    if which == 0:
                                        mask = CB_GEF if n == 0 else (CB_GEB if n == nbnd else CB_GE)
                                    else:
                                        mask = CB_LEL if n + 1 == Fblocks else (CB_LEB if n + 1 == nbnd else CB_LE)
                                    k0 = PAD - 64 * r + c + 128 * r * mm
                                    q0 = 128 * r * nn + c
                                    items = []
                                    for h in range(4):
                                        po = (h % 2) * 64
                                        items.append((po, kTB[po:po + 64, h // 2, k0:k0 + 127 * r + 1:r],
                                                      qTB[po:po + 64, h // 2, q0:q0 + 127 * r + 1:r], ["kTB", "qTB"],
                                                      (vtB[:, c * (nblk + 1) + mm, h, :], ["vtB", "vtB1"])))
                                    order = attend(items, mask, None, ot, which == 0, which == 1)
                                assert order == [0, 2, 1, 3]
                                flush_pv()
                                for half in range(2):
                                    av = acc[:, half:4:2, 128 * r * nn:128 * r * (nn + 1)].rearrange("p h (i r) -> p h r i", r=r)[:, :, c, :]
                                    ov = ops_[ot][:, half * 256:(half + 1) * 256].rearrange("p (h i) -> p h i", h=2)
                                    if g == 0:
                                        S.op("dve", lambda e, av=av, ov=ov: e.tensor_copy(out=av, in_=ov),
                                             reads=[("ops", ot)], writes=["acc"])
                                    else:
                                        S.op("dve", lambda e, av=av, ov=ov: e.tensor_tensor(out=av, in0=ov, in1=av, op=ALU.add),
                                             reads=[("ops", ot), "acc"], writes=["acc"])
                    for h in range(4):
                        po = (h % 2) * 64
                        S.op("dve", lambda e, h=h: e.reciprocal(out=recB[0:64, :], in_=acc[64:128, h, :]),
                             reads=["acc"], writes=["recB"])
                        S.op("pool", lambda e, h=h, po=po: e.tensor_tensor(
                            out=ostB[po:po + 64, h // 2, :], in0=acc[0:64, h, :], in1=recB[0:64, :], op=ALU.mult),
                            reads=["acc", "recB"], writes=[("ostB", h)])
                    S.dma("sp", lambda e, tseg=tseg: e.dma_start(
                        out=cat0[512:768, tseg:tseg + SEG].rearrange("(c p) t -> p c t", p=128), in_=ostB[:]),
                        reads=[("ostB", h) for h in range(4)], chan="ostB")
                S.emit_pass()

        def pass_outproj(tag, xin, catT, KC, w_dram, xout):
            with ExitStack() as es:
                wo = sbt(es, "wo", [128, KC, D], BF16)
                S.dma("pool", lambda e: e.dma_start(out=wo[:], in_=w_dram.rearrange("(c p) n -> p c n", p=128)),
                      writes=["wo"], chan="wo")
                ct = [sbt(es, "ct%d" % i, [128, KC, 512], BF16) for i in range(2)]
                xt = [sbt(es, "oxt%d" % i, [128, 4, D], F32) for i in range(2)]
                yps = [pst(es, "yps%d" % i) for i in range(4)]
                k = 0
                for g in range(G):
                    t0 = g * 512
                    b = g % 2
                    S.dma("sp", lambda e, b=b, t0=t0: e.dma_start(
                        out=ct[b][:], in_=catT[:, t0:t0 + 512].rearrange("(c p) t -> p c t", p=128)),
                        writes=[("ct", b)], chan="ct%d" % b)
                    S.dma("sp", lambda e, b=b, t0=t0: e.dma_start(
                        out=xt[b][:], in_=xin[t0:t0 + 512, :].rearrange("(j p) d -> p j d", p=128)),
                        writes=[("oxt", b)], chan="oxt%d" % b)
                    for j in range(4):
                        for half in range(2):
                            yb = k % 4
                            k += 1
                            for kc in range(KC):
                                S.op("pe", lambda e, yb=yb, b=b, kc=kc, j=j, half=half: e.matmul(
                                    yps[yb][:], lhsT=ct[b][:, kc, j * 128:(j + 1) * 128], rhs=wo[:, kc, half * 512:(half + 1) * 512],
                                    start=(kc == 0), stop=(kc == KC - 1)),
                                    reads=[("ct", b), "wo"], writes=[("yps", yb)])
                            S.op("dve", lambda e, yb=yb, b=b, j=j, half=half: e.tensor_tensor(
                                out=xt[b][:, j, half * 512:(half + 1) * 512], in0=yps[yb][:], in1=xt[b][:, j, half * 512:(half + 1) * 512], op=ALU.add),
                                reads=[("yps", yb), ("oxt", b)], writes=[("oxt", b)])
                    S.dma("sp", lambda e, b=b, t0=t0: e.dma_start(
                        out=xout[t0:t0 + 512, :].rearrange("(j p) d -> p j d", p=128), in_=xt[b][:]),
                        reads=[("oxt", b)], chan="oxt%d" % b)
                S.emit_pass()

        def pass_ffn(l, xin, xout):
            with ExitStack() as es:
                wup = sbt(es, "wup", [128, 8, 2 * DFF], BF16)
                wdn = sbt(es, "wdn", [128, NFC, D], BF16)
                upv = f_w_up[l].rearrange("(c p) n -> p c n", p=128)
                for q in range(8):
                    S.dma("pool", lambda e, q=q: e.dma_start(out=wup[:, :, q * 704:(q + 1) * 704], in_=upv[:, :, q * 704:(q + 1) * 704]),
                          writes=[("wup", q)], chan="wup")
                S.alias("wup", len(S.ops) - 1)
                dnv = f_w_down[l].rearrange("(c p) n -> p c n", p=128)
                for q in range(2):
                    S.dma("pool", lambda e, q=q: e.dma_start(out=wdn[:, q * 11:(q + 1) * 11, :], in_=dnv[:, q * 11:(q + 1) * 11, :]),
                          writes=[("wdn", q)], chan="wdn")
                S.alias("wdn", len(S.ops) - 1)
                gn = load_gain_cols(es, "fgn", f_norm[l], D, 8, 32.0, "fgn")
                cw = sbt(es, "cw", [128, 3, NFC], F32)
                for jj in range(3):
                    S.dma("sp", lambda e, jj=jj: e.dma_start(out=cw[:, jj, :], in_=f_conv_w[l, jj].rearrange("(c p) -> p c", p=128)),
                          writes=[("cw", jj)], chan="cw")
                S.alias("cw", len(S.ops) - 1)
                cbi = sbt(es, "cbi", [128, NFC], F32)
                S.dma("sp", lambda e: e.dma_start(out=cbi[:], in_=f_conv_b[l].rearrange("(c p) -> p c", p=128)),
                      writes=["cbi"], chan="cbi")
                xt = sbt(es, "fxt", [128, 4, D], F32)
                xh = sbt(es, "fxh", [2, D], F32)
                xn = [sbt(es, "fxn%d" % i, [128, D], BF16) for i in range(2)]
                xhn = sbt(es, "fxhn", [2, D], BF16)
                ssq = sbt(es, "fssq", [128, 8], F32)
                rst = sbt(es, "frst", [128, 8], F32)
                xnT = sbt(es, "fxnT", [128, 8, 512], BF16)
                xnTh = sbt(es, "fxnTh", [128, 8, 2], BF16)
                gsb = [sbt(es, "gsb%d" % i, [128, 514], F32) for i in range(3)]
                aa = [sbt(es, "aa%d" % i, [128, 512], F32) for i in range(3)]
                hT = sbt(es, "hT", [128, NFC, 512], BF16)
                junk = [hT[:, 0:2, :].rearrange("p a b -> p (a b)"), hT[:, 2:4, :].rearrange("p a b -> p (a b)")]
                gps = [pst(es, "gps%d" % i) for i in range(2)]
                vps = [pst(es, "fvps%d" % i) for i in range(3)]
                hpsl = [pst(es, "hps%d" % i) for i in range(1)]
                yps = [pst(es, "fyps%d" % i) for i in range(2)]
                ptr = yps[1][:, :].bitcast(BF16)
                k = 0
                fi = 0
                for g in range(G):
                    t0 = g * 512
                    has_l = (t0 > 0)
                    has_r = (t0 + 512 < T)
                    S.dma("sp", lambda e, t0=t0: e.dma_start(
                        out=xt[:], in_=xin[t0:t0 + 512, :].rearrange("(j p) d -> p j d", p=128)),
                        writes=["fxt"], chan="fxt")
                    S.op("pool", lambda e: e.memset(ssq[:], 0.0), writes=["fssq"])
                    if has_l or has_r:
                        S.op("pool", lambda e: e.memset(xh[:], 1.0), writes=["fxh"])
                        if has_l:
                            S.dma("sp", lambda e, t0=t0: e.dma_start(out=xh[0:1, :], in_=xin[t0 - 1:t0, :]), reads=["fxh"], writes=["fxh"], chan="fxh")
                        if has_r:
                            S.dma("sp", lambda e, t0=t0: e.dma_start(out=xh[1:2, :], in_=xin[t0 + 512:t0 + 513, :]), reads=["fxh"], writes=["fxh"], chan="fxh")
                        S.op("act", lambda e: e.activation(out=junk[0][0:2, :], in_=xh[:], func=AF.Square, accum_out=ssq[0:2, 4:5]),
                             reads=["fxh", "fssq"], writes=["fssq", ("hT", 0), ("hT", 1)])
                    for j in range(4):
                        S.op("act", lambda e, j=j: e.activation(
                            out=junk[j % 2], in_=xt[:, j, :], func=AF.Square, accum_out=ssq[:, j:j + 1]),
                            reads=["fxt", "fssq"], writes=["fssq", ("hT", 2 * (j % 2)), ("hT", 2 * (j % 2) + 1)])
                    S.op("act", lambda e: e.activation(out=rst[:], in_=ssq[:], func=AF.Sqrt, bias=epsc[:, 0:1], scale=1.0),
                         reads=["fssq", ("epsc", 0)], writes=["frst"])
                    S.op("dve", lambda e: e.reciprocal(out=rst[:], in_=rst[:]), reads=["frst"], writes=["frst"])
                    if has_l or has_r:
                        S.op("dve", lambda e: e.tensor_scalar(out=xhn[:], in0=xh[:], scalar1=rst[0:2, 4:5], scalar2=None, op0=ALU.mult),
                             reads=["fxh", "frst"], writes=["fxhn"])
                    for j in range(4):
                        nb_ = j % 2
                        S.op("dve" if j % 2 else "act", (lambda e, j=j, nb_=nb_: e.tensor_scalar(
                            out=xn[nb_][:], in0=xt[:, j, :], scalar1=rst[:, j:j + 1], scalar2=None, op0=ALU.mult)) if j % 2 else
                            (lambda e, j=j, nb_=nb_: e.activation(out=xn[nb_][:], in_=xt[:, j, :], func=AF.Copy, scale=rst[:, j:j + 1])),
                            reads=["fxt", "frst"], writes=[("fxn", nb_)])
                        for c in range(8):
                            S.op("pe", lambda e, c=c, j=j, nb_=nb_: e.transpose(
                                ptr[:, c * 128:(c + 1) * 128], xn[nb_][:, c * 128:(c + 1) * 128], cbm(CB_IDENT)),
                                reads=[("fxn", nb_), "cb"], writes=[("fyps", 1)])
                        S.op("dve", lambda e, j=j: e.tensor_tensor(
                            out=xnT[:, :, j * 128:(j + 1) * 128], in0=ptr[:, :].rearrange("p (c t) -> p c t", c=8),
                            in1=gn[:, :].unsqueeze(2).broadcast_to([128, 8, 128]), op=ALU.mult),
                            reads=[("fyps", 1), "fgn"], writes=[("fxnT", j)])
                    if has_l or has_r:
                        for c in range(8):
                            S.op("pe", lambda e, c=c: e.transpose(
                                ptr[:, c * 128:c * 128 + 2], xhn[0:2, c * 128:(c + 1) * 128], cbm(CB_IDENT, 2, 2)),
                                reads=["fxhn", "cb"], writes=[("fyps", 1)])
                        S.op("dve", lambda e: e.tensor_tensor(
                            out=xnTh[:], in0=ptr[:, :].rearrange("p (c t) -> p c t", c=8)[:, :, 0:2],
                            in1=gn[:, :].unsqueeze(2).broadcast_to([128, 8, 2]), op=ALU.mult),
                             reads=[("fyps", 1), "fgn"], writes=["fxnTh"])
                    xk = [("fxnT", j) for j in range(4)]
                    pend_tail = []
                    for fc in range(NFC):
                        gb = fi % 2
                        v3 = fi % 3
                        fi += 1
                        for (dst, col0, key) in ((gps[gb], fc * 128, ("gps", gb)), (vps[v3], DFF + fc * 128, ("fvps", v3))):
                            for c in range(8):
                                S.op("pe", lambda e, dst=dst, col0=col0, c=c: e.matmul(
                                    dst[:], lhsT=wup[:, c, col0:col0 + 128], rhs=xnT[:, c, :],
                                    start=(c == 0), stop=(c == 7)),
                                    reads=xk + ["wup"], writes=[key])
                            if dst is gps[gb] and (has_l or has_r):
                                for c in range(8):
                                    S.op("pe", lambda e, col0=col0, c=c, fc=fc: e.matmul(
                                        hpsl[0][:, 2 * fc:2 * fc + 2], lhsT=wup[:, c, col0:col0 + 128], rhs=xnTh[:, c, :],
                                        start=(c == 0), stop=(c == 7)),
                                        reads=["fxnTh", "wup"], writes=[("hps", 0)])
                        S.op("act", lambda e, gb=gb, v3=v3: e.activation(out=gsb[v3][:, 1:513], in_=gps[gb][:], func=AF.Copy),
                             reads=[("gps", gb)], writes=[("gsb", v3, 1)])
                        for side, has in ((0, has_l), (1, has_r)):
                            colo = 0 if side == 0 else 513
                            if not has:
                                S.op("pool", lambda e, gb=v3, colo=colo: e.memset(gsb[gb][:, colo:colo + 1], 0.0),
                                     writes=[("gsb", v3, 0 if side == 0 else 2)])
                            else:
                                tpos = t0 if side == 0 else t0 + 512
                                sc = pc[:, 0:1] if tpos == HB else 1.0
                                S.op("dve", lambda e, gb=v3, colo=colo, fc=fc, side=side, sc=sc: e.tensor_scalar(
                                    out=gsb[gb][:, colo:colo + 1], in0=hpsl[0][:, 2 * fc + side:2 * fc + side + 1], scalar1=sc, scalar2=None, op0=ALU.mult),
                                    reads=[("hps", 0), "pc"], writes=[("gsb", v3, 0 if side == 0 else 2)])
                        gk = [("gsb", v3, i) for i in range(3)]
                        S.op(ENG_FFN_CONV, lambda e, gb=v3, fc=fc: e.tensor_scalar(
                            out=aa[gb][:], in0=gsb[gb][:, 1:513], scalar1=cw[:, 1, fc:fc + 1], scalar2=None, op0=ALU.mult),
                            reads=gk + ["cw"], writes=[("aa", v3)])
                        S.op("dve", lambda e, gb=v3, fc=fc: e.scalar_tensor_tensor(
                            out=aa[gb][:], in0=gsb[gb][:, 0:512], scalar=cw[:, 0, fc:fc + 1], in1=aa[gb][:], op0=ALU.mult, op1=ALU.add),
                            reads=gk + ["cw", ("aa", v3)], writes=[("aa", v3)])
                        S.op("dve", lambda e, gb=v3, fc=fc: e.scalar_tensor_tensor(
                            out=aa[gb][:], in0=gsb[gb][:, 2:514], scalar=cw[:, 2, fc:fc + 1], in1=aa[gb][:], op0=ALU.mult, op1=ALU.add),
                            reads=gk + ["cw", ("aa", v3)], writes=[("aa", v3)])
                        def tail(v3=v3, fc=fc):
                            S.op("act", lambda e, gb=v3, fc=fc: e.activation(
                                out=aa[gb][:], in_=aa[gb][:], func=AF.Gelu_apprx_tanh, bias=cbi[:, fc:fc + 1], scale=1.0),
                                reads=[("aa", v3), "cbi"], writes=[("aa", v3)])
                            S.op("dve", lambda e, gb=v3, fc=fc: e.tensor_tensor(
                                out=hT[:, fc, :], in0=vps[gb][:], in1=aa[gb][:], op=ALU.mult),
                                reads=[("fvps", v3), ("aa", v3)], writes=[("hT", fc)])
                        if pend_tail:
                            pend_tail.pop()()
                        pend_tail.append(tail)
                    pend_tail.pop()()
                    hk = [("hT", fc) for fc in range(NFC)]
                    for j in range(4):
                        for half in range(2):
                            yb = k % 2
                            k += 1
                            for fc in range(NFC):
                                S.op("pe", lambda e, yb=yb, fc=fc, j=j, half=half: e.matmul(
                                    yps[yb][:], lhsT=hT[:, fc, j * 128:(j + 1) * 128], rhs=wdn[:, fc, half * 512:(half + 1) * 512],
                                    start=(fc == 0), stop=(fc == NFC - 1)),
                                    reads=[("hT", fc), "wdn"], writes=[("fyps", yb)])
                            S.op("dve", lambda e, yb=yb, j=j, half=half: e.tensor_tensor(
                                out=xt[:, j, half * 512:(half + 1) * 512], in0=yps[yb][:], in1=xt[:, j, half * 512:(half + 1) * 512], op=ALU.add),
                                reads=[("fyps", yb), "fxt"], writes=["fxt"])
                    S.dma("sp", lambda e, t0=t0: e.dma_start(
                        out=xout[t0:t0 + 512, :].rearrange("(j p) d -> p j d", p=128), in_=xt[:]),
                        reads=["fxt"], chan="fxt")
                S.emit_pass()

        def pass_c1():
            with ExitStack() as es:
                win1 = sbt(es, "win1", [128, 8, 544], BF16)
                S.dma("pool", lambda e: e.dma_start(out=win1[:], in_=o_w_in.rearrange("(c p) n -> p c n", p=128)),
                      writes=["win1"], chan="win1")
                wuq = sbt(es, "wuq", [128, 2, 1536], BF16)
                S.dma("pool", lambda e: e.dma_start(out=wuq[:], in_=o_w_uq.rearrange("(c p) n -> p c n", p=128)),
                      writes=["wuq"], chan="wuq")
                wukx = sbt(es, "wukx", [128, 2, 16, 96], BF16)
                wuv = sbt(es, "wuv", [128, 2, 16, 64], BF16)
                S.op("pool", lambda e: e.memset(wukx[:], 0.0), writes=["wukx"])
                ukv = o_w_ukv.rearrange("(c p) (h n) -> p c h n", p=128, n=128)
                for i in range(2):
                    S.dma("pool", lambda e, i=i: e.dma_start(out=wukx[:, i, :, 0:64], in_=ukv[:, i, :, 0:64]),
                          reads=["wukx"], writes=["wukx"], chan="wukx")
                    S.dma("pool", lambda e, i=i: e.dma_start(out=wuv[:, i, :, :], in_=ukv[:, i, :, 64:128]),
                          writes=["wuv"], chan="wuv")
                gn = load_gain_cols(es, "cgn", o_norm, D, 8, 32.0, "cgn")
                gql = load_gain_cols(es, "gql", o_q_lora_gain, 256, 2, 16.0, "gql")
                gkl = load_gain_cols(es, "gkl", o_kv_gain, 256, 2, 16.0, "gkl")
                g96 = sbt(es, "g96", [96, 2], F32)
                S.dma("sp", lambda e: e.dma_start(out=g96[:, 0:1], in_=o_q_gain.rearrange("(p o) -> p o", o=1)), writes=[("g96", 0)], chan="g96")
                S.dma("sp", lambda e: e.dma_start(out=g96[:, 1:2], in_=o_k_gain.rearrange("(p o) -> p o", o=1)), writes=[("g96", 1)], chan="g96")
                S.alias("g96", len(S.ops) - 1)
                xt = sbt(es, "cxt", [128, 4, D], F32)
                xn = [sbt(es, "cxn%d" % i, [128, D], BF16) for i in range(2)]
                junk = [sbt(es, "cjunk%d" % i, [128, D], BF16) for i in range(2)]
                ssq = sbt(es, "cssq", [128, 4], F32)
                rst = sbt(es, "crst", [128, 4], F32)
                xnT = sbt(es, "cxnT", [128, 8, 512], BF16)
                cst = [sbt(es, "ccst%d" % i, [96, 2, 512], F32) for i in range(2)]
                csq = [sbt(es, "csq%d" % i, [128, 512], BF16) for i in range(2)]
                craw = [sbt(es, "craw%d" % i, [128, 512], F32) for i in range(2)]
                crs = sbt(es, "crs", [128, 512], F32)
                cn = [sbt(es, "cn%d" % i, [128, 2, 512], BF16) for i in range(2)]
                krT = sbt(es, "krT", [32, 512], BF16)
                sq = [sbt(es, "hsq%d" % i, [96, 512], BF16) for i in range(2)]
                qg = [sbt(es, "hqg%d" % i, [96, 512], BF16) for i in range(2)]
                sd = [sbt(es, "hsd%d" % i, [96, 512], F32) for i in range(2)]
                t1 = [sbt(es, "ht1%d" % i, [96, 512], F32) for i in range(2)]
                t2 = [sbt(es, "ht2%d" % i, [96, 512], F32) for i in range(2)]
                qf = [sbt(es, "hqf%d" % i, [96, 512], BF16) for i in range(3)]
                vst = [sbt(es, "cvst%d" % i, [128, 16, 128], BF16) for i in range(2)]
                for i in range(2):
                    S.op("pool", lambda e, i=i: e.memset(vst[i][:, :, 64:128], 1.0), writes=[("cvst1", i)])
                ptr = pst(es, "cptr", BF16, 1024)
                cps = [pst(es, "cps%d" % i) for i in range(2)]
                qps = [pst(es, "cqps%d" % i) for i in range(2)]
                sps = pst(es, "csps")
                rps = pst(es, "crps")
                vps = pst(es, "cvps")
                ci = 0
                hi = 0
                vi = 0
                for g in range(G):
                    t0 = g * 512
                    b = g % 2
                    S.dma("sp", lambda e, t0=t0: e.dma_start(
                        out=xt[:], in_=x2[t0:t0 + 512, :].rearrange("(j p) d -> p j d", p=128)),
                        writes=["cxt"], chan="cxt")
                    S.dma("sp", lambda e, b=b, t0=t0: e.dma_start(
                        out=cst[b][:], in_=cs96[:, :, t0:t0 + 512].rearrange("a p t -> p a t")),
                        writes=[("ccst", b)], chan="ccst%d" % b)
                    S.op("pool", lambda e: e.memset(ssq[:], 0.0), writes=["cssq"])
                    for j in range(4):
                        S.op("act", lambda e, j=j: e.activation(
                            out=junk[j % 2][:], in_=xt[:, j, :], func=AF.Square, accum_out=ssq[:, j:j + 1]),
                            reads=["cxt", "cssq"], writes=["cssq", ("cjunk", j % 2)])
                    S.op("act", lambda e: e.activation(out=rst[:], in_=ssq[:], func=AF.Sqrt, bias=epsc[:, 0:1], scale=1.0),
                         reads=["cssq", ("epsc", 0)], writes=["crst"])
                    S.op("dve", lambda e: e.reciprocal(out=rst[:], in_=rst[:]), reads=["crst"], writes=["crst"])
                    for j in range(4):
                        nb_ = j % 2
                        S.op("dve" if j % 2 else "pool", lambda e, j=j, nb_=nb_: e.tensor_scalar(
                            out=xn[nb_][:], in0=xt[:, j, :], scalar1=rst[:, j:j + 1], scalar2=None, op0=ALU.mult),
                            reads=["cxt", "crst"], writes=[("cxn", nb_)])
                        for c in range(8):
                            S.op("pe", lambda e, c=c, nb_=nb_: e.transpose(
                                ptr[:, c * 128:(c + 1) * 128], xn[nb_][:, c * 128:(c + 1) * 128], cbm(CB_IDENT)),
                                reads=[("cxn", nb_), "cb"], writes=["cptr"])
                        S.op("dve", lambda e, j=j: e.tensor_tensor(
                            out=xnT[:, :, j * 128:(j + 1) * 128], in0=ptr[:, :].rearrange("p (c t) -> p c t", c=8),
                            in1=gn[:, :].unsqueeze(2).broadcast_to([128, 8, 128]), op=ALU.mult),
                            reads=["cptr", "cgn"], writes=[("cxnT", j)])
                    xk = [("cxnT", j) for j in range(4)]
                    for which, (col0, gl_, glk) in enumerate(((0, gql, "gql"), (256, gkl, "gkl"))):
                        for i in range(2):
                            cb_ = ci % 2
                            ci += 1
                            for c in range(8):
                                S.op("pe", lambda e, cb_=cb_, c=c, col0=col0, i=i: e.matmul(
                                    cps[cb_][:], lhsT=win1[:, c, col0 + i * 128:col0 + (i + 1) * 128], rhs=xnT[:, c, :],
                                    start=(c == 0), stop=(c == 7)),
                                    reads=xk + ["win1"], writes=[("cps", cb_)])
                            S.op("act", lambda e, cb_=cb_, i=i: e.activation(out=csq[i][:], in_=cps[cb_][:], func=AF.Square),
                                 reads=[("cps", cb_)], writes=[("csq", i)])
                            S.op("act", lambda e, cb_=cb_, i=i: e.activation(out=craw[i][:], in_=cps[cb_][:], func=AF.Copy),
                                 reads=[("cps", cb_)], writes=[("craw", i)])
                        for i in range(2):
                            S.op("pe", lambda e, i=i: e.matmul(sps[:], lhsT=cbm(CB_ONES), rhs=csq[i][:], start=(i == 0), stop=(i == 1)),
                                 reads=[("csq", i), "cb"], writes=["csps"])
                        S.op("act", lambda e: e.activation(out=crs[:], in_=sps[:], func=AF.Sqrt, bias=epsc[:, 2:3], scale=1.0),
                             reads=["csps", ("epsc", 2)], writes=["crs"])
                        S.op("dve", lambda e: e.reciprocal(out=crs[:], in_=crs[:]), reads=["crs"], writes=["crs"])
                        for i in range(2):
                            S.op("dve", lambda e, i=i, which=which, gl_=gl_: e.scalar_tensor_tensor(
                                out=cn[which][:, i, :], in0=craw[i][:], scalar=gl_[:, i:i + 1], in1=crs[:], op0=ALU.mult, op1=ALU.mult),
                                reads=[("craw", i), "crs", glk], writes=[("cn", which, i)])
                    cb_ = ci % 2
                    ci += 1
                    for c in range(8):
                        S.op("pe", lambda e, cb_=cb_, c=c: e.matmul(
                            cps[cb_][0:32, :], lhsT=win1[:, c, 512:544], rhs=xnT[:, c, :], start=(c == 0), stop=(c == 7)),
                            reads=xk + ["win1"], writes=[("cps", cb_)])
                    S.op("act", lambda e, cb_=cb_: e.activation(out=krT[:], in_=cps[cb_][0:32, :], func=AF.Copy),
                         reads=[("cps", cb_)], writes=["krT"])
                    pend = []
                    for isk in range(2):
                        for h in range(16):
                            qb = hi % 2
                            hi += 1
                            if isk == 0:
                                for i in range(2):
                                    S.op("pe", lambda e, qb=qb, h=h, i=i: e.matmul(
                                        qps[qb][0:96, :], lhsT=wuq[:, i, h * 96:(h + 1) * 96], rhs=cn[0][:, i, :],
                                        start=(i == 0), stop=(i == 1)),
                                        reads=[("cn", 0, i), "wuq"], writes=[("cqps", qb)])
                            else:
                                for i in range(2):
                                    S.op("pe", lambda e, qb=qb, h=h, i=i: e.matmul(
                                        qps[qb][0:96, :], lhsT=wukx[:, i, h, :], rhs=cn[1][:, i, :],
                                        start=(i == 0), stop=False),
                                        reads=[("cn", 1, i), "wukx"], writes=[("cqps", qb)])
                                S.op("pe", lambda e, qb=qb: e.matmul(
                                    qps[qb][0:96, :], lhsT=cbm(CB_E, 32, 96), rhs=krT[:], start=False, stop=True),
                                    reads=["krT", "cb"], writes=[("cqps", qb)])
                            def post(qb=qb, isk=isk, h=h, b=b, t0=t0, hi=hi):
                                S.op("act", lambda e, qb=qb: e.activation(out=sq[qb][:], in_=qps[qb][0:96, :], func=AF.Square),
                                     reads=[("cqps", qb)], writes=[("hsq", qb)])
                                S.op("act", lambda e, qb=qb, isk=isk: e.activation(
                                    out=qg[qb][:], in_=qps[qb][0:96, :], func=AF.Copy, scale=g96[:, isk:isk + 1]),
                                    reads=[("cqps", qb), "g96"], writes=[("hqg", qb)])
                                S.op("pe", lambda e, qb=qb: e.matmul(sps[0:96, :], lhsT=cbm(CB_ONES, 96, 96), rhs=sq[qb][:], start=True, stop=True),
                                     reads=[("hsq", qb), "cb"], writes=["csps"])
                                S.op("pe", lambda e, qb=qb: e.matmul(rps[0:96, :], lhsT=cbm(CB_R96, 96, 96), rhs=qg[qb][:], start=True, stop=True),
                                     reads=[("hqg", qb), "cb"], writes=["crps"])
                                S.op("act", lambda e, qb=qb: e.activation(
                                    out=sd[qb][:], in_=sps[0:96, :], func=AF.Sqrt, bias=epsc[0:96, 3:4], scale=1.0),
                                    reads=["csps", ("epsc", 3)], writes=[("hsd", qb)])
                                S.op(ENG_T1, lambda e, qb=qb, b=b: e.tensor_tensor(
                                    out=t1[qb][:], in0=qg[qb][:], in1=cst[b][:, 0, :], op=ALU.mult),
                                    reads=[("hqg", qb), ("ccst", b)], writes=[("ht1", qb)])
                                S.op("dve", lambda e, qb=qb, b=b: e.tensor_tensor(
                                    out=t2[qb][:], in0=rps[0:96, :], in1=cst[b][:, 1, :], op=ALU.mult),
                                    reads=["crps", ("ccst", b)], writes=[("ht2", qb)])
                                S.op(ENG_ADD, lambda e, qb=qb: e.tensor_tensor(
                                    out=t1[qb][:], in0=t1[qb][:], in1=t2[qb][:], op=ALU.add),
                                    reads=[("ht1", qb), ("ht2", qb)], writes=[("ht1", qb)])
                                S.op("dve", lambda e, qb=qb: e.reciprocal(out=sd[qb][:], in_=sd[qb][:]),
                                     reads=[("hsd", qb)], writes=[("hsd", qb)])
                                fb = hi % 3
                                S.op("dve", lambda e, qb=qb, fb=fb: e.tensor_tensor(
                                    out=qf[fb][:], in0=t1[qb][:], in1=sd[qb][:], op=ALU.mult),
                                    reads=[("ht1", qb), ("hsd", qb)], writes=[("hqf", fb)])
                                dst = k1 if isk else q1
                                S.dma("sp", lambda e, fb=fb, h=h, t0=t0, dst=dst: e.dma_start(
                                    out=dst[h * 96:(h + 1) * 96, t0:t0 + 512], in_=qf[fb][:]),
                                    reads=[("hqf", fb)], chan="hqf%d" % fb)
                            if pend:
                                pend.pop()()
                            pend.append(post)
                    pend.pop()()
                    for j in range(4):
                        vb = vi % 2
                        vi += 1
                        for half in range(2):
                            for i in range(2):
                                S.op("pe", lambda e, half=half, i=i, j=j: e.matmul(
                                    vps[:], lhsT=cn[1][:, i, j * 128:(j + 1) * 128],
                                    rhs=wuv[:, i, half * 8:(half + 1) * 8, :].rearrange("p h d -> p (h d)"),
                                    start=(i == 0), stop=(i == 1)),
                                    reads=[("cn", 1, i), "wuv"], writes=["cvps"])
                            S.op("act" if half else "dve", (lambda e, vb=vb, half=half: e.activation(
                                out=vst[vb][:, half * 8:(half + 1) * 8, 0:64], in_=vps[:, :].rearrange("p (h d) -> p h d", h=8), func=AF.Copy)) if half else
                                (lambda e, vb=vb, half=half: e.tensor_copy(
                                    out=vst[vb][:, half * 8:(half + 1) * 8, 0:64], in_=vps[:, :].rearrange("p (h d) -> p h d", h=8))),
                                reads=["cvps"], writes=[("cvst", vb, half)])
                        S.dma("sp", lambda e, vb=vb, t0=t0, j=j: e.dma_start(
                            out=v1[t0 + j * 128:t0 + (j + 1) * 128, :], in_=vst[vb][:].rearrange("p h d -> p (h d)")),
                            reads=[("cvst", vb, 0), ("cvst", vb, 1), ("cvst1", vb)], chan="cvst%d" % vb)
                S.emit_pass()

        def pass_c2():
            with ExitStack() as es:
                kT = [sbt(es, "dkT%d" % i, [96, T], BF16) for i in range(2)]
                qT = [sbt(es, "dqT%d" % i, [96, T], BF16) for i in range(2)]
                vt = [sbt(es, "dvt%d" % i, [128, NT, 128], BF16) for i in range(2)]
                ost = [sbt(es, "dost%d" % i, [64, T], BF16) for i in range(2)]
                pt = [sbt(es, "dpt%d" % i, [128, 512], BF16) for i in range(4)]
                rec = [sbt(es, "drec%d" % i, [128, 512], F32) for i in range(2)]
                sps = [pst(es, "dsps%d" % i) for i in range(4)]
                ops_ = [pst(es, "dops%d" % i) for i in range(2)]
                SC = float(np.sqrt(96.0))
                NQT = T // 512
                steps = [(h, qt, kb) for h in range(16) for qt in range(NQT) for kb in range(NT)]
                LAG = 3

                def load_head(h):
                    b = h % 2
                    S.dma("sp", lambda e, b=b, h=h: e.dma_start(out=kT[b][:], in_=k1[h * 96:(h + 1) * 96, :]),
                          writes=[("dkT", b)], chan="dkT%d" % b)
                    S.dma("sp", lambda e, b=b, h=h: e.dma_start(out=qT[b][:], in_=q1[h * 96:(h + 1) * 96, :]),
                          writes=[("dqT", b)], chan="dqT%d" % b)
                    S.dma("sp", lambda e, b=b, h=h: e.dma_start(
                        out=vt[b][:], in_=v1[:, h * 128:(h + 1) * 128].rearrange("(m p) c -> p m c", p=128)),
                        writes=[("dvt", b)], chan="dvt%d" % b)

                def emit_s(i):
                    h, qt, kb = steps[i]
                    b = h % 2
                    sb_ = i % 4
                    cross = ((qt * 512 < HB) != (kb * 128 < HB))
                    S.op("pe", lambda e, sb_=sb_, b=b, kb=kb, qt=qt: e.matmul(
                        sps[sb_][:], lhsT=kT[b][:, kb * 128:(kb + 1) * 128], rhs=qT[b][:, qt * 512:(qt + 1) * 512],
                        start=True, stop=True),
                        reads=[("dkT", b), ("dqT", b)], writes=[("dsps", sb_)])
                    bias_ap = pc[:, 1:2] if cross else epsc[:, 4:5]
                    S.op("act", lambda e, sb_=sb_, bias_ap=bias_ap: e.activation(
                        out=pt[sb_][:], in_=sps[sb_][:], func=AF.Exp, bias=bias_ap, scale=SC),
                        reads=[("dsps", sb_), "pc", ("epsc", 4)], writes=[("dpt", sb_)])

                def emit_pv(i):
                    h, qt, kb = steps[i]
                    b = h % 2
                    sb_ = i % 4
                    ob = (h * NQT + qt) % 2
                    S.op("pe", lambda e, sb_=sb_, b=b, kb=kb, ob=ob: e.matmul(
                        ops_[ob][:], lhsT=vt[b][:, kb, :], rhs=pt[sb_][:], start=(kb == 0), stop=(kb == NT - 1)),
                        reads=[("dpt", sb_), ("dvt", b)], writes=[("dops", ob)])
                    if kb == NT - 1:
                        S.op("dve", lambda e, ob=ob: e.reciprocal(out=rec[ob][64:128, :], in_=ops_[ob][64:128, :]),
                             reads=[("dops", ob)], writes=[("drec", ob)])
                        S.op("dve", lambda e, ob=ob, b=b, qt=qt: e.tensor_tensor(
                            out=ost[b][:, qt * 512:(qt + 1) * 512], in0=ops_[ob][0:64, :], in1=rec[ob][64:128, :], op=ALU.mult),
                            reads=[("dops", ob), ("drec", ob)], writes=[("dost", b)])
                        if qt == NQT - 1:
                            S.dma("sp", lambda e, b=b, h=h: e.dma_start(out=cat1[h * 64:(h + 1) * 64, :], in_=ost[b][:]),
                                  reads=[("dost", b)], chan="dost%d" % b)

                load_head(0)
                for i in range(len(steps) + LAG):
                    if i < len(steps):
                        h, qt, kb = steps[i]
                        if h + 1 < 16 and kb == 0 and qt == (1 if NQT > 1 else 0) and (qt > 0 or True):
                            if not (NQT == 1):
                                load_head(h + 1)
                        emit_s(i)
                    if i >= LAG:
                        emit_pv(i - LAG)
                        if NQT == 1:
                            h2, qt2, kb2 = steps[i - LAG]
                            if kb2 == NT - 1 and h2 + 1 < 16:
                                pass
                S.emit_pass()

        if "a1" in passes:
            pass_a1()
        if "a2" in passes:
            pass_a2()
        if "a3" in passes:
            pass_outproj("a3", x_in, cat0, 6, e_w_out, x1)
        if "b" in passes:
            pass_ffn(0, x1, x2)
        if "c1" in passes:
            pass_c1()
        if "c2" in passes:
            pass_c2()
        if "c3" in passes:
            pass_outproj("c3", x2, cat1, 8, o_w_out, x3)
        if "d" in passes:
            pass_ffn(1, x3, y_out)

        S.final_wait()
    return nc


def _rope_tables(T, L):
    pos = (np.arange(T) % L).astype(np.float32)

    def tab(Dr):
        inv = np.power(np.float32(10000.0), -np.arange(0, Dr, 2, dtype=np.float32) / np.float32(Dr)).astype(np.float32)
        ang = (pos[:, None] * inv[None, :]).astype(np.float32)
        return np.cos(ang).astype(np.float32).T, np.sin(ang).astype(np.float32).T

    c64, s64 = tab(64)
    c32, s32 = tab(32)
    cs64 = np.stack([np.tile(c64, (4, 1)), np.tile(s64, (4, 1))]).astype(np.float32)
    c96 = np.concatenate([np.ones((64, T), np.float32), c32, c32])
    s96 = np.concatenate([np.zeros((64, T), np.float32), s32, s32])
    cs96 = np.stack([c96, s96]).astype(np.float32)
    return np.ascontiguousarray(cs64), np.ascontiguousarray(cs96)


def _const_mats(is_prompt):
    m = np.zeros((NCB, 128, 128), np.float32)
    m[CB_IDENT] = np.eye(128)
    k = np.arange(128)
    m[CB_B64] = (k[:, None] // 64 == k[None, :] // 64)
    for o in range(128):
        if o % 64 < 32:
            m[CB_R64][o + 32, o] = -1.0
        else:
            m[CB_R64][o - 32, o] = 1.0
    m[CB_ONES] = 1.0
    for i in range(16):
        m[CB_R96][80 + i, 64 + i] = -1.0
        m[CB_R96][64 + i, 80 + i] = 1.0
    for i in range(32):
        m[CB_E][i, 64 + i] = 1.0
    j = k[:, None]
    i = k[None, :]
    ge = (j >= i).astype(np.float32)
    le = (j <= i).astype(np.float32)
    gef = ge * (j >= 64)
    lel = le * (j < 64)
    m[CB_GE], m[CB_LE], m[CB_GEF], m[CB_LEL] = ge, le, gef, lel
    if is_prompt:
        m[CB_GEB], m[CB_LEB] = gef, lel
        m[CB_GEAB], m[CB_LEAB] = 0.0, 0.0
    else:
        m[CB_GEB], m[CB_LEB] = ge, le
        m[CB_GEAB], m[CB_LEAB] = ge, le
    out = np.ascontiguousarray(m.transpose(1, 0, 2).reshape(128, NCB * 128))
    pcv = np.zeros((128, 4), np.float32)
    pcv[:, 0] = 0.0 if is_prompt else 1.0
    pcv[:, 1] = NEGB if is_prompt else 0.0
    return out, pcv


_WNAMES = ("e_norm", "e_w_in", "e_a_q_gain", "e_a_k_gain", "e_a_sink", "e_b_q_gain", "e_b_k_gain", "e_w_out",
           "o_norm", "o_w_in", "o_q_lora_gain", "o_w_uq", "o_kv_gain", "o_w_ukv", "o_q_gain", "o_k_gain", "o_w_out")
_FNAMES = ("f_norm", "f_w_up", "f_conv_w", "f_conv_b", "f_w_down")


def make_in_maps(streams, T, weights):
    wmap = {}
    for n in _WNAMES:
        a = np.asarray(weights[n], np.float32)
        wmap[n] = np.ascontiguousarray(a.reshape(a.shape[1:]))
    for n in _FNAMES:
        wmap[n] = np.ascontiguousarray(np.asarray(weights[n], np.float32))
    tabs = {}
    in_maps = []
    for (xs, is_p) in streams:
        if is_p not in tabs:
            cs64, cs96 = _rope_tables(T, T // 2 if is_p else T)
            cm, pcv = _const_mats(is_p)
            tabs[is_p] = (cs64, cs96, cm, pcv)
        cs64, cs96, cm, pcv = tabs[is_p]
        m = dict(wmap)
        m.update(x=np.ascontiguousarray(xs, dtype=np.float32), cs64=cs64, cs96=cs96, consts=cm, percore=pcv)
        in_maps.append(m)
    return in_maps


_PROG = {}


def kernel(**inputs):
    T = 8192
    xp = np.asarray(inputs["x_prompt"], np.float32)
    xs = np.asarray(inputs["x_sample"], np.float32)
    streams = []
    for c in range(4):
        streams.append((xp[2 * c:2 * c + 2].reshape(T, D), True))
    for c in range(4):
        streams.append((xs[c], False))
    in_maps = make_in_maps(streams, T, inputs)
    if T not in _PROG:
        _PROG[T] = build_program(T)
    res = run_bass_kernel_spmd(_PROG[T], in_maps, core_ids=list(range(8)))
    ys = [np.asarray(r["y"], np.float32) for r in res.results]
    y_prompt = np.stack([ys[c].reshape(2, T // 2, D) for c in range(4)]).reshape(8, T // 2, D)
    y_sample = np.stack(ys[4:8])
    return (y_prompt, y_sample)
```

```python
import os
import numpy as np
from contextlib import ExitStack
import concourse.bass as bass
import concourse.mybir as mybir
from concourse.bass_utils import run_bass_kernel_spmd

F32 = mybir.dt.float32
BF16 = mybir.dt.bfloat16
AF = mybir.ActivationFunctionType
ALU = mybir.AluOpType

D = 1024
EPS = 1e-6
DFF = 2816
NFC = DFF // 128
PAD = 1024
NQK = 18
NEGB = -30000.0

ENGS = ("pe", "act", "dve", "pool", "sp")
STRICT = not os.environ.get("NOSTRICT")
PENG = "dve" if os.environ.get("POOL2DVE") else "pool"
ENG_FFN_CONV = os.environ.get("ENG_FFN_CONV", "dve")
ENG_T1 = os.environ.get("ENG_T1", "pool")
ENG_ADD = os.environ.get("ENG_ADD", "pool")
ENG_MASK = os.environ.get("ENG_MASK", "dve")


class Sched:
    def __init__(self, nc, es):
        self.nc = nc
        self.es = es
        self.sem = {e: es.enter_context(nc.semaphore("sem_" + e)) for e in ("pe", "act", "dve", "pool")}
        self.sigcnt = {e: 0 for e in self.sem}
        self.chans = {}
        self.waited = {e: {} for e in ENGS}
        self.ops = []
        self.last_w = {}
        self.readers = {}
        self.bar_val = 0
        self.n_inst = 0

    def chan(self, name):
        if name not in self.chans:
            self.chans[name] = dict(sem=self.es.enter_context(self.nc.semaphore("ch_" + name)), issued=0, name=name)
        return self.chans[name]

    def _add(self, o, reads, writes):
        idx = len(self.ops)
        deps = []
        seen = set()

        def add_dep(d, raw):
            if d in seen:
                return
            seen.add(d)
            deps.append((d, raw))

        for k in reads:
            if k in self.last_w:
                add_dep(self.last_w[k], True)
        for k in writes:
            if k in self.last_w:
                lw = self.ops[self.last_w[k]]
                if not (o["dma"] and lw["dma"] and lw["chan"] is o["chan"] and lw["eng"] == o["eng"]):
                    add_dep(self.last_w[k], False)
            for r in self.readers.get(k, ()):
                add_dep(r, False)
        for k in reads:
            self.readers.setdefault(k, []).append(idx)
        for k in writes:
            self.last_w[k] = idx
            self.readers[k] = []
        need = []
        for d, raw in deps:
            od = self.ops[d]
            if od["dma"]:
                need.append(("ch", od["chan"], od["chan"]["issued"]))
            else:
                if od["eng"] == o["eng"] and not o["dma"]:
                    if o["eng"] == "pe" or (not raw and not STRICT):
                        continue
                od["sig"] = True
                need.append(("op", d))
        o["need"] = need
        self.ops.append(o)
        return idx

    def op(self, eng, fn, reads=(), writes=()):
        o = dict(eng=eng, fn=fn, dma=False, sig=False)
        return self._add(o, reads, writes)

    def dma(self, eng, fn, reads=(), writes=(), chan=None):
        ch = self.chan(chan)
        o = dict(eng=eng, fn=fn, dma=True, sig=False, chan=ch)
        idx = self._add(o, reads, writes)
        ch["issued"] += 1
        return idx

    def alias(self, key, idx):
        self.last_w[key] = idx
        self.readers[key] = []

    def emit_pass(self):
        nc = self.nc
        ops = self.ops
        last = {}
        for i, o in enumerate(ops):
            if not o["dma"]:
                last[o["eng"]] = i
        for e, i in last.items():
            ops[i]["sig"] = True
        for o in ops:
            if not o["dma"] and o["sig"]:
                self.sigcnt[o["eng"]] += 1
                o["sigval"] = self.sigcnt[o["eng"]]
        per = {e: [o for o in ops if o["eng"] == e] for e in ENGS}
        bar_prev = self.bar_val
        self.sigcnt["dve"] += 1
        bar_new = self.sigcnt["dve"]
        sched = self

        def wait(e, eng, sem, key, val):
            w = sched.waited[e]
            if w.get(key, 0) >= val:
                return
            w[key] = val
            eng.wait_ge(sem, val)
            sched.n_inst += 1

        def run_engine(e, eng):
            if bar_prev > 0:
                wait(e, eng, sched.sem["dve"], "dve", bar_prev)
            for o in per[e]:
                for n in o["need"]:
                    if n[0] == "ch":
                        wait(e, eng, n[1]["sem"], "ch_" + n[1]["name"], 16 * n[2])
                    else:
                        od = ops[n[1]]
                        wait(e, eng, sched.sem[od["eng"]], od["eng"], od["sigval"])
                ins = o["fn"](eng)
                sched.n_inst += 1
                if o["dma"]:
                    ins.then_inc(o["chan"]["sem"], 16)
                elif o["sig"]:
                    ins.then_inc(sched.sem[e], 1)
            if e == "dve":
                for e2 in ("pe", "act", "pool"):
                    if sched.sigcnt[e2] > 0:
                        wait(e, eng, sched.sem[e2], e2, sched.sigcnt[e2])
                for ch in sched.chans.values():
                    if ch["issued"] > 0:
                        wait(e, eng, ch["sem"], "ch_" + ch["name"], 16 * ch["issued"])
                eng.memset(sched.bar_tile[0:1, 0:1], 0.0).then_inc(sched.sem["dve"], 1)
            else:
                pass

        with nc.Block() as block:
            @block.tensor
            def _(eng):
                run_engine("pe", eng)

            @block.scalar
            def _(eng):
                run_engine("act", eng)

            @block.vector
            def _(eng):
                run_engine("dve", eng)

            @block.gpsimd
            def _(eng):
                run_engine("pool", eng)

            @block.sync
            def _(eng):
                run_engine("sp", eng)

        self.bar_val = bar_new
        self.ops = []
        self.last_w = {}
        self.readers = {}

    def final_wait(self):
        nc = self.nc
        sched = self
        with nc.Block() as block:
            @block.sync
            def _(eng):
                eng.wait_ge(sched.sem["dve"], sched.bar_val)

            @block.tensor
            def _(eng):
                eng.wait_ge(sched.sem["dve"], sched.bar_val)

            @block.scalar
            def _(eng):
                eng.wait_ge(sched.sem["dve"], sched.bar_val)

            @block.gpsimd
            def _(eng):
                eng.wait_ge(sched.sem["dve"], sched.bar_val)


CB_IDENT, CB_B64, CB_R64, CB_ONES, CB_R96, CB_E, CB_GE, CB_LE, CB_GEF, CB_LEL, CB_GEB, CB_LEB, CB_GEAB, CB_LEAB = range(14)
NCB = 14


def build_program(T, debug=False, passes=None):
    nc = bass.Bass("TRN2", target_bir_lowering=False)
    G = T // 512
    NT = T // 128
    HB = T // 2
    SEG = 2048
    NSEG = T // SEG
    allp = ("a1", "a2", "a3", "b", "c1", "c2", "c3", "d")
    passes = allp if passes is None else passes

    def din(name, shape):
        return nc.dram_tensor(name, list(shape), F32, kind="ExternalInput").ap()

    def dscr(name, shape, dt):
        kind = "ExternalOutput" if (debug and name in debug) else "Internal"
        return nc.dram_tensor(name, list(shape), dt, kind=kind).ap()

    x_in = din("x", [T, D])
    cs64 = din("cs64", [2, 128, T])
    cs96 = din("cs96", [2, 96, T])
    consts = din("consts", [128, NCB * 128])
    percore = din("percore", [128, 4])
    e_norm = din("e_norm", [D])
    e_w_in = din("e_w_in", [D, 3072])
    e_a_q_gain = din("e_a_q_gain", [64])
    e_a_k_gain = din("e_a_k_gain", [64])
    e_a_sink = din("e_a_sink", [8])
    e_b_q_gain = din("e_b_q_gain", [3, 64])
    e_b_k_gain = din("e_b_k_gain", [3, 64])
    e_w_out = din("e_w_out", [768, D])
    o_norm = din("o_norm", [D])
    o_w_in = din("o_w_in", [D, 544])
    o_q_lora_gain = din("o_q_lora_gain", [256])
    o_w_uq = din("o_w_uq", [256, 1536])
    o_kv_gain = din("o_kv_gain", [256])
    o_w_ukv = din("o_w_ukv", [256, 2048])
    o_q_gain = din("o_q_gain", [96])
    o_k_gain = din("o_k_gain", [96])
    o_w_out = din("o_w_out", [D, D])
    f_norm = din("f_norm", [2, D])
    f_w_up = din("f_w_up", [2, D, 2 * DFF])
    f_conv_w = din("f_conv_w", [2, 3, DFF])
    f_conv_b = din("f_conv_b", [2, DFF])
    f_w_down = din("f_w_down", [2, DFF, D])

    y_out = nc.dram_tensor("y", [T, D], F32, kind="ExternalOutput").ap()

    TP = T + 2 * PAD
    qk0 = dscr("qk0", [NQK * 128, TP], BF16)
    v0 = dscr("v0", [TP, 896], BF16)
    cat0 = dscr("cat0", [768, T], BF16)
    x1 = dscr("x1", [T, D], F32)
    x2 = dscr("x2", [T, D], F32)
    q1 = dscr("q1", [16 * 96, T], BF16)
    k1 = dscr("k1", [16 * 96, T], BF16)
    v1 = dscr("v1", [T, 16 * 128], BF16)
    cat1 = dscr("cat1", [D, T], BF16)
    x3 = dscr("x3", [T, D], F32)

    es_glob = ExitStack()
    with es_glob:
        S = Sched(nc, es_glob)
        S.bar_tile = es_glob.enter_context(nc.sbuf_tensor("bar_tile", [1, 8], F32))
        cb = es_glob.enter_context(nc.sbuf_tensor("cb", [128, NCB, 128], BF16))
        pc = es_glob.enter_context(nc.sbuf_tensor("pc", [128, 4], F32))
        epsc = es_glob.enter_context(nc.sbuf_tensor("epsc", [128, 8], F32))
        ncd = nc.allow_non_contiguous_dma(reason="tiny parameter loads")
        es_glob.enter_context(ncd)

        def cbm(i, p=128, f=128):
            return cb[0:p, i, 0:f]

        uniq = [0]

        def sbt(es, name, shape, dt):
            uniq[0] += 1
            return es.enter_context(nc.sbuf_tensor("%s_%d" % (name, uniq[0]), list(shape), dt))

        def pst(es, name, dt=F32, cols=512):
            uniq[0] += 1
            return es.enter_context(nc.psum_tensor("%s_%d" % (name, uniq[0]), [128, cols], dt))

        with ExitStack() as es:
            zt = sbt(es, "zt", [128, 8 * 896], BF16)
            S.dma("pool", lambda e: e.dma_start(out=cb[:].rearrange("p a b -> p (a b)"), in_=consts[:, :]),
                  writes=["cb"], chan="cb")
            S.dma("sp", lambda e: e.dma_start(out=pc[:], in_=percore[:, :]), writes=["pc"], chan="pc")
            S.op("pool", lambda e: e.memset(zt[:], 0.0), writes=["zt"])
            for i, v in enumerate((D * EPS, 64 * EPS, 256 * EPS, 96 * EPS, 0.0)):
                S.op("pool", lambda e, i=i, v=v: e.memset(epsc[:, i:i + 1], float(v)), writes=[("epsc", i)])
            if "a1" in passes:
                for side in range(2):
                    c0 = 0 if side == 0 else PAD + T
                    for kc in range(NQK):
                        S.dma("sp", lambda e, kc=kc, c0=c0: e.dma_start(
                            out=qk0[kc * 128:(kc + 1) * 128, c0:c0 + PAD], in_=zt[:, 0:PAD]),
                            reads=["zt"], chan="zpad")
                    S.dma("sp", lambda e, c0=c0: e.dma_start(
                        out=v0[c0:c0 + PAD, :].rearrange("(p a) c -> p (a c)", p=128), in_=zt[:, 0:8 * 896]),
                        reads=["zt"], chan="zpad")
            S.emit_pass()

        def load_gain_cols(es, name, src_vec, n, chunks, mul, chan):
            t = sbt(es, name, [128, chunks], F32)
            S.dma("sp", lambda e: e.dma_start(out=t[:], in_=src_vec.rearrange("(c p) -> p c", p=128)),
                  writes=[name], chan=chan)
            if mul != 1.0:
                S.op("pool", lambda e: e.tensor_scalar(out=t[:], in0=t[:], scalar1=float(mul), scalar2=None, op0=ALU.mult),
                     reads=[name], writes=[name])
            return t

        def norm_transpose(es_unused, bufs, gi, src_dram, t0, ntok_tiles, gcol, xnT, xnT_key, col0, keep_x=None):
            pass

        def pass_a1():
            with ExitStack() as es:
                wqk = sbt(es, "wqk", [128, 8, NQK * 128], BF16)
                wv = sbt(es, "wv", [128, 8, 896], BF16)
                colmap = []
                for c in range(4):
                    colmap.append((c, 0, c * 128, 128))
                colmap += [(4, 0, 512, 64), (4, 64, 512, 64), (5, 0, 576, 64), (5, 64, 576, 64)]
                for g in range(3):
                    base = 768 + g * 768
                    for c in range(2):
                        colmap.append((6 + g * 4 + c, 0, base + c * 128, 128))
                        colmap.append((6 + g * 4 + 2 + c, 0, base + 256 + c * 128, 128))
                w_in_v = e_w_in.rearrange("(c p) n -> p c n", p=128)
                for (dc, do, sc, n) in colmap:
                    S.dma("pool", lambda e, dc=dc, do=do, sc=sc, n=n: e.dma_start(
                        out=wqk[:, :, dc * 128 + do: dc * 128 + do + n], in_=w_in_v[:, :, sc:sc + n]),
                        writes=[("wqk", dc, do)], chan="wqk")
                S.alias("wqk", len(S.ops) - 1)
                vmap = [(0, 640, 128)] + [(128 + g * 256, 768 + g * 768 + 512, 256) for g in range(3)]
                for (do, sc, n) in vmap:
                    S.dma("pool", lambda e, do=do, sc=sc, n=n: e.dma_start(
                        out=wv[:, :, do:do + n], in_=w_in_v[:, :, sc:sc + n]), writes=[("wv", do)], chan="wv")
                S.alias("wv", len(S.ops) - 1)
                gn = load_gain_cols(es, "gn", e_norm, D, 8, 32.0, "gn")
                gq = sbt(es, "gq", [128, NQK], F32)
                gsrc = [e_a_q_gain] * 4 + [e_a_k_gain] * 2
                for g in range(3):
                    gsrc += [e_b_q_gain[g], e_b_q_gain[g], e_b_k_gain[g], e_b_k_gain[g]]
                for c, gs in enumerate(gsrc):
                    for h in range(2):
                        S.dma("sp", lambda e, c=c, gs=gs, h=h: e.dma_start(
                            out=gq[h * 64:(h + 1) * 64, c:c + 1], in_=gs.rearrange("(p o) -> p o", o=1)),
                            writes=[("gq", c, h)], chan="gq")
                S.alias("gq", len(S.ops) - 1)
                xt = [sbt(es, "xt%d" % i, [128, 4, D], F32) for i in range(2)]
                xn = sbt(es, "xn", [128, 4, D], BF16)
                junk = [sbt(es, "junk%d" % i, [128, D], BF16) for i in range(4)]
                ssq = [sbt(es, "ssq%d" % i, [128, 4], F32) for i in range(2)]
                rst = [sbt(es, "rst%d" % i, [128, 4], F32) for i in range(2)]
                xnT = [sbt(es, "xnT%d" % i, [128, 8, 512], BF16) for i in range(2)]
                cst = [sbt(es, "cst%d" % i, [128, 2, 512], F32) for i in range(2)]
                sq = [sbt(es, "sq%d" % i, [128, 512], BF16) for i in range(2)]
                qg = [sbt(es, "qg%d" % i, [128, 512], BF16) for i in range(2)]
                sd = [sbt(es, "sd%d" % i, [128, 512], F32) for i in range(2)]
                t1 = [sbt(es, "t1%d" % i, [128, 512], F32) for i in range(2)]
                t2 = [sbt(es, "t2%d" % i, [128, 512], F32) for i in range(2)]
                t3 = [sbt(es, "t3%d" % i, [128, 512], F32) for i in range(2)]
                qf = [sbt(es, "qf%d" % i, [128, 512], BF16) for i in range(3)]
                vst = [sbt(es, "vst%d" % i, [128, 896], BF16) for i in range(2)]
                ptr = [pst(es, "ptr%d" % i, BF16, 1024) for i in range(2)]
                qps = [pst(es, "qps%d" % i) for i in range(2)]
                sps = pst(es, "sps")
                rps = pst(es, "rps")
                vps = [pst(es, "vps%d" % i) for i in range(2)]
                ci = 0
                vi = 0
                for g in range(G):
                    t0 = g * 512
                    b = g % 2
                    S.dma("sp", lambda e, b=b, t0=t0: e.dma_start(
                        out=xt[b][:], in_=x_in[t0:t0 + 512, :].rearrange("(j p) d -> p j d", p=128)),
                        writes=[("xt", b)], chan="xt%d" % b)
                    S.dma("sp", lambda e, b=b, t0=t0: e.dma_start(
                        out=cst[b][:], in_=cs64[:, :, t0:t0 + 512].rearrange("a p t -> p a t")),
                        writes=[("cst", b)], chan="cst%d" % b)
                    S.op("pool", lambda e, b=b: e.memset(ssq[b][:], 0.0), writes=[("ssq", b, j) for j in range(4)])
                    for j in range(4):
                        S.op("act", lambda e, b=b, j=j: e.activation(
                            out=junk[j][:], in_=xt[b][:, j, :], func=AF.Square, accum_out=ssq[b][:, j:j + 1]),
                            reads=[("xt", b), ("ssq", b, j)], writes=[("ssq", b, j), ("junk", j)])
                    S.op("act", lambda e, b=b: e.activation(
                        out=rst[b][:], in_=ssq[b][:], func=AF.Sqrt, bias=epsc[:, 0:1], scale=1.0),
                        reads=[("ssq", b, j) for j in range(4)] + [("epsc", 0)], writes=[("rst", b)])
                    S.op("dve", lambda e, b=b: e.reciprocal(out=rst[b][:], in_=rst[b][:]),
                         reads=[("rst", b)], writes=[("rst", b)])
                    for j in range(4):
                        S.op("dve" if j % 2 else "pool", lambda e, b=b, j=j: e.tensor_scalar(
                            out=xn[:, j, :], in0=xt[b][:, j, :], scalar1=rst[b][:, j:j + 1], scalar2=None, op0=ALU.mult),
                            reads=[("xt", b), ("rst", b)], writes=[("xn", j)])
                    for c in range(8):
                        pb = c % 2
                        for j in range(4):
                            S.op("pe", lambda e, pb=pb, c=c, j=j: e.transpose(
                                ptr[pb][:, j * 128:(j + 1) * 128], xn[:, j, c * 128:(c + 1) * 128], cbm(CB_IDENT)),
                                reads=[("xn", j), "cb"], writes=[("ptr", pb)])
                        S.op("act" if c % 2 else "dve", (lambda e, pb=pb, c=c, b=b: e.activation(
                            out=xnT[b][:, c, :], in_=ptr[pb][:, 0:512], func=AF.Copy, scale=gn[:, c:c + 1])) if c % 2 else
                            (lambda e, pb=pb, c=c, b=b: e.tensor_scalar(
                                out=xnT[b][:, c, :], in0=ptr[pb][:, 0:512], scalar1=gn[:, c:c + 1], scalar2=None, op0=ALU.mult)),
                            reads=[("ptr", pb), "gn"], writes=[("xnT", b, c)])
                    xk = [("xnT", b, c) for c in range(8)]
                    pend = []
                    for fc in range(NQK):
                        qb = ci % 2
                        ci += 1
                        for c in range(8):
                            S.op("pe", lambda e, qb=qb, fc=fc, c=c, b=b: e.matmul(
                                qps[qb][:], lhsT=wqk[:, c, fc * 128:(fc + 1) * 128], rhs=xnT[b][:, c, :],
                                start=(c == 0), stop=(c == 7)),
                                reads=[("xnT", b, c), "wqk"], writes=[("qps", qb)])
                        def post(qb=qb, fc=fc, b=b, g=g, t0=t0):
                            S.op("act", lambda e, qb=qb: e.activation(out=sq[qb][:], in_=qps[qb][:], func=AF.Square),
                                 reads=[("qps", qb)], writes=[("sq", qb)])
                            S.op("act", lambda e, qb=qb, fc=fc: e.activation(
                                out=qg[qb][:], in_=qps[qb][:], func=AF.Copy, scale=gq[:, fc:fc + 1]),
                                reads=[("qps", qb), "gq"], writes=[("qg", qb)])
                            S.op("pe", lambda e, qb=qb: e.matmul(sps[:], lhsT=cbm(CB_B64), rhs=sq[qb][:], start=True, stop=True),
                                 reads=[("sq", qb), "cb"], writes=["sps"])
                            S.op("pe", lambda e, qb=qb: e.matmul(rps[:], lhsT=cbm(CB_R64), rhs=qg[qb][:], start=True, stop=True),
                                 reads=[("qg", qb), "cb"], writes=["rps"])
                            S.op("act", lambda e, qb=qb: e.activation(
                                out=sd[qb][:], in_=sps[:], func=AF.Sqrt, bias=epsc[:, 1:2], scale=1.0),
                                reads=["sps", ("epsc", 1)], writes=[("sd", qb)])
                            S.op(ENG_T1, lambda e, qb=qb, b=b: e.tensor_tensor(
                                out=t1[qb][:], in0=qg[qb][:], in1=cst[b][:, 0, :], op=ALU.mult),
                                reads=[("qg", qb), ("cst", b)], writes=[("t1", qb)])
                            S.op("dve", lambda e, qb=qb, b=b: e.tensor_tensor(
                                out=t2[qb][:], in0=rps[:], in1=cst[b][:, 1, :], op=ALU.mult),
                                reads=["rps", ("cst", b)], writes=[("t2", qb)])
                            S.op(ENG_ADD, lambda e, qb=qb: e.tensor_tensor(
                                out=t3[qb][:], in0=t1[qb][:], in1=t2[qb][:], op=ALU.add),
                                reads=[("t1", qb), ("t2", qb)], writes=[("t3", qb)])
                            fb = (g * NQK + fc) % 3
                            S.op("dve", lambda e, qb=qb: e.reciprocal(out=sd[qb][:], in_=sd[qb][:]),
                                 reads=[("sd", qb)], writes=[("sd", qb)])
                            S.op("dve", lambda e, qb=qb, fb=fb: e.tensor_tensor(
                                out=qf[fb][:], in0=t3[qb][:], in1=sd[qb][:], op=ALU.mult),
                                reads=[("t3", qb), ("sd", qb)], writes=[("qf", fb)])
                            S.dma("sp", lambda e, fb=fb, fc=fc, t0=t0: e.dma_start(
                                out=qk0[fc * 128:(fc + 1) * 128, PAD + t0:PAD + t0 + 512], in_=qf[fb][:]),
                                reads=[("qf", fb)], chan="qf%d" % fb)
                        if pend:
                            pend.pop()()
                        pend.append(post)
                    pend.pop()()
                    for j in range(4):
                        vb = vi % 2
                        vi += 1
                        for (c0, c1, pi) in ((0, 512, 0), (512, 896, 1)):
                            for c in range(8):
                                S.op("pe", lambda e, pi=pi, c0=c0, c1=c1, c=c, b=b, j=j: e.matmul(
                                    vps[pi][:, 0:c1 - c0], lhsT=xnT[b][:, c, j * 128:(j + 1) * 128], rhs=wv[:, c, c0:c1],
                                    start=(c == 0), stop=(c == 7)),
                                    reads=[("xnT", b, c), "wv"], writes=[("vps", pi)])
                            S.op("act" if pi else "dve", (lambda e, pi=pi, c0=c0, c1=c1, vb=vb: e.activation(
                                out=vst[vb][:, c0:c1], in_=vps[pi][:, 0:c1 - c0], func=AF.Copy)) if pi else
                                (lambda e, pi=pi, c0=c0, c1=c1, vb=vb: e.tensor_copy(out=vst[vb][:, c0:c1], in_=vps[pi][:, 0:c1 - c0])),
                                reads=[("vps", pi)], writes=[("vst", vb, pi)])
                        S.dma("sp", lambda e, vb=vb, t0=t0, j=j: e.dma_start(
                            out=v0[PAD + t0 + j * 128:PAD + t0 + (j + 1) * 128, :], in_=vst[vb][:]),
                            reads=[("vst", vb, 0), ("vst", vb, 1)], chan="vst%d" % vb)
                S.emit_pass()

        def pass_a2():
            with ExitStack() as es:
                qTA = sbt(es, "qTA", [128, 4, SEG], BF16)
                kTA = sbt(es, "kTA", [128, 2, SEG + 2 * PAD], BF16)
                vtA = sbt(es, "vtA", [128, 18, 2, 128], BF16)
                qTB = sbt(es, "qTB", [128, 2, SEG], BF16)
                kTB = sbt(es, "kTB", [128, 2, SEG + 2 * PAD], BF16)
                vtB = sbt(es, "vtB", [128, 32, 4, 128], BF16)
                acc = sbt(es, "acc", [128, 4, SEG], F32)
                recB = sbt(es, "recB", [128, SEG], F32)
                ostA = sbt(es, "ostA", [128, 4, SEG], BF16)
                ostB = sbt(es, "ostB", [128, 2, SEG], BF16)
                pt = [sbt(es, "pt%d" % i, [128, 512], BF16) for i in range(3)]
                dn = [sbt(es, "dn%d" % i, [128, 512], F32) for i in range(2)]
                snk = sbt(es, "snk", [128, 8], F32)
                spsX = [pst(es, "s2psx%d" % i) for i in range(2)]
                spsY = [pst(es, "s2psy%d" % i) for i in range(2)]
                ops_ = [pst(es, "o2ps%d" % i) for i in range(2)]
                S.op("pool", lambda e: e.memset(vtA[:, :, :, 64:128], 1.0), writes=["vtA1"])
                S.op("pool", lambda e: e.memset(vtB[:, :, :, 64:128], 1.0), writes=["vtB1"])
                S.dma("sp", lambda e: e.dma_start(out=snk[:], in_=e_a_sink.partition_broadcast(128)), writes=["snk"], chan="snk")
                S.op("act", lambda e: e.activation(out=snk[:], in_=snk[:], func=AF.Exp), reads=["snk"], writes=["snk"])
                cnt = dict(p=0, s=0, o=0, d=0)
                pend_pv = []

                def flush_pv():
                    while pend_pv:
                        pend_pv.pop(0)()

                def attend(items, mask, vt_single, o_t, first, last):
                    order = [i for i in range(len(items)) if items[i][0] == 0] + [i for i in range(len(items)) if items[i][0] != 0]
                    nx = sum(1 for it in items if it[0] == 0)
                    n = len(items)
                    si = cnt["s"] % 2
                    cnt["s"] += 1
                    pi = cnt["p"] % 3
                    cnt["p"] += 1
                    for col, idx in enumerate(order):
                        po, l, r, rk, _ = items[idx]
                        bank = spsX[si] if po == 0 else spsY[si]
                        bkey = ("spsX", si) if po == 0 else ("spsY", si)
                        cc = col if po == 0 else col - nx
                        S.op("pe", lambda e, l=l, r=r, cc=cc, bank=bank: e.matmul(
                            bank[:, cc * 128:(cc + 1) * 128], lhsT=l, rhs=r, start=True, stop=True),
                            reads=rk, writes=[bkey])
                    flush_pv()
                    if nx > 0:
                        S.op("act", lambda e, si=si, pi=pi, nx=nx: e.activation(
                            out=pt[pi][:, 0:nx * 128], in_=spsX[si][:, 0:nx * 128], func=AF.Exp, scale=8.0),
                            reads=[("spsX", si)], writes=[("pt", pi, 0)])
                    if n - nx > 0:
                        S.op("act", lambda e, si=si, pi=pi, nx=nx, n=n: e.activation(
                            out=pt[pi][:, nx * 128:n * 128], in_=spsY[si][:, 0:(n - nx) * 128], func=AF.Exp, scale=8.0),
                            reads=[("spsY", si)], writes=[("pt", pi, 1)])
                    pk = [("pt", pi, 0), ("pt", pi, 1)]
                    if mask is not None:
                        S.op(ENG_MASK, lambda e, pi=pi, n=n, mask=mask: e.tensor_tensor(
                            out=pt[pi][:, 0:n * 128].rearrange("p (a b) -> p a b", a=n),
                            in0=pt[pi][:, 0:n * 128].rearrange("p (a b) -> p a b", a=n),
                            in1=cbm(mask).unsqueeze(1).broadcast_to([128, n, 128]), op=ALU.mult),
                            reads=pk + ["cb"], writes=pk)
                    def emit_pv(vt_single=vt_single, items=items, order=order, pi=pi, o_t=o_t, n=n, first=first, last=last, pk=pk):
                        if vt_single is not None:
                            vl, vk = vt_single
                            S.op("pe", lambda e, vl=vl, pi=pi, o_t=o_t, n=n: e.matmul(
                                ops_[o_t][:, 0:n * 128], lhsT=vl, rhs=pt[pi][:, 0:n * 128], start=first, stop=last),
                                reads=pk + vk, writes=[("ops", o_t)])
                        else:
                            for col, idx in enumerate(order):
                                vl, vk = items[idx][4]
                                st_ = first and col == 0
                                sp_ = last and col == n - 1
                                S.op("pe", lambda e, vl=vl, col=col, pi=pi, o_t=o_t, st_=st_, sp_=sp_: e.matmul(
                                    ops_[o_t][:, col * 128:(col + 1) * 128], lhsT=vl, rhs=pt[pi][:, col * 128:(col + 1) * 128],
                                    start=st_, stop=sp_),
                                    reads=pk + vk, writes=[("ops", o_t)])
                    pend_pv.append(emit_pv)
                    return order

                for sg in range(NSEG):
                    tseg = sg * SEG
                    S.dma("sp", lambda e, tseg=tseg: e.dma_start(
                        out=qTA[:], in_=qk0[0:512, PAD + tseg:PAD + tseg + SEG].rearrange("(c p) t -> p c t", p=128)),
                        writes=["qTA"], chan="qTA")
                    S.dma("sp", lambda e, tseg=tseg: e.dma_start(
                        out=kTA[:], in_=qk0[512:768, tseg:tseg + SEG + 2 * PAD].rearrange("(c p) t -> p c t", p=128)),
                        writes=["kTA"], chan="kTA")
                    for h in range(2):
                        S.dma("sp", lambda e, tseg=tseg, h=h: e.dma_start(
                            out=vtA[:, :, h, 0:64],
                            in_=v0[PAD + tseg - 128:PAD + tseg - 128 + 18 * 128, h * 64:(h + 1) * 64].rearrange("(m j) d -> j m d", j=128)),
                            writes=["vtA"], chan="vtA")
                    for nn in range(SEG // 128 if not os.environ.get("SKIP_A") else 0):
                        tb = tseg + nn * 128
                        tiles = []
                        if tb > 0:
                            tiles.append((-1, CB_GEAB if tb == HB else CB_GE))
                        tiles.append((0, None))
                        if tb + 128 < T:
                            tiles.append((1, CB_LEAB if tb + 128 == HB else CB_LE))
                        for hkv in range(2):
                            ot = cnt["o"] % 2
                            cnt["o"] += 1
                            for ti, (off, mask) in enumerate(tiles):
                                kc0 = PAD + 128 * (nn + off)
                                items = []
                                for g in range(4):
                                    hq = 4 * hkv + g
                                    po = (hq % 2) * 64
                                    items.append((po, kTA[po:po + 64, hkv, kc0:kc0 + 128],
                                                  qTA[po:po + 64, hq // 2, nn * 128:(nn + 1) * 128], ["kTA", "qTA"], None))
                                order = attend(items, mask, (vtA[:, nn + 1 + off, hkv, :], ["vtA", "vtA1"]),
                                               ot, ti == 0, ti == len(tiles) - 1)
                            flush_pv()
                            di = cnt["d"] % 2
                            cnt["d"] += 1
                            for col, g in enumerate(order):
                                hq = 4 * hkv + g
                                S.op("dve", lambda e, di=di, ot=ot, col=col, hq=hq: e.tensor_scalar(
                                    out=dn[di][64:128, col * 128:(col + 1) * 128], in0=ops_[ot][64:128, col * 128:(col + 1) * 128],
                                    scalar1=snk[64:128, hq:hq + 1], scalar2=None, op0=ALU.add),
                                    reads=[("ops", ot), "snk"], writes=[("dn", di, col)])
                            S.op("dve", lambda e, di=di: e.reciprocal(out=dn[di][64:128, :], in_=dn[di][64:128, :]),
                                 reads=[("dn", di, g) for g in range(4)], writes=[("dn", di, g) for g in range(4)])
                            for col, g in enumerate(order):
                                hq = 4 * hkv + g
                                po = (hq % 2) * 64
                                S.op("dve", lambda e, di=di, ot=ot, col=col, hq=hq, po=po, nn=nn: e.tensor_tensor(
                                    out=ostA[po:po + 64, hq // 2, nn * 128:(nn + 1) * 128],
                                    in0=ops_[ot][0:64, col * 128:(col + 1) * 128], in1=dn[di][64:128, col * 128:(col + 1) * 128], op=ALU.mult),
                                    reads=[("ops", ot), ("dn", di, col)], writes=[("ostA", hq, nn)])
                    S.dma("sp", lambda e, tseg=tseg: e.dma_start(
                        out=cat0[0:512, tseg:tseg + SEG].rearrange("(c p) t -> p c t", p=128), in_=ostA[:]),
                        reads=[("ostA", hq, nn) for hq in range(8) for nn in range(SEG // 128)], chan="ostA")
                    for g, r in enumerate((1, 4, 16)):
                        if os.environ.get("SKIP_B") and str(g) in os.environ.get("SKIP_B"):
                            continue
                        nblk = SEG // (128 * r)
                        Fblocks = T // (128 * r)
                        nbnd = HB // (128 * r)
                        qrow = (6 + 4 * g) * 128
                        S.dma("sp", lambda e, tseg=tseg, qrow=qrow: e.dma_start(
                            out=qTB[:], in_=qk0[qrow:qrow + 256, PAD + tseg:PAD + tseg + SEG].rearrange("(c p) t -> p c t", p=128)),
                            writes=["qTB"], chan="qTB")
                        S.dma("sp", lambda e, tseg=tseg, qrow=qrow: e.dma_start(
                            out=kTB[:], in_=qk0[qrow + 256:qrow + 512, tseg:tseg + SEG + 2 * PAD].rearrange("(c p) t -> p c t", p=128)),
                            writes=["kTB"], chan="kTB")
                        vcol = 128 + 256 * g
                        for c in range(r):
                            base = PAD + tseg - 64 * r + c
                            nrow = (nblk + 1) * 128
                            for h in range(4):
                                S.dma("sp", lambda e, base=base, nrow=nrow, r=r, c=c, nblk=nblk, vcol=vcol, h=h: e.dma_start(
                                    out=vtB[:, c * (nblk + 1):(c + 1) * (nblk + 1), h, 0:64],
                                    in_=v0[base:base + (nrow - 1) * r + 1:r, vcol + h * 64:vcol + (h + 1) * 64].rearrange("(m j) d -> j m d", j=128)),
                                    writes=["vtB"], chan="vtB")
                        for nn in range(nblk):
                            n = nblk * sg + nn
                            for c in range(r):
                                ot = cnt["o"] % 2
                                cnt["o"] += 1
                                for which in range(2):
                                    mm = nn + which
                                    if which == 0:
                                        mask = CB_GEF if n == 0 else (CB_GEB if n == nbnd else CB_GE)
                                    else:
                                        mask = CB_LEL if n + 1 == Fblocks else (CB_LEB if n + 1 == nbnd else CB_LE)
                                    k0 = PAD - 64 * r + c + 128 * r * mm
                                    q0 = 128 * r * nn + c
                                    items = []
                                    for h in range(4):
                                        po = (h % 2) * 64
                                        items.append((po, kTB[po:po + 64, h // 2, k0:k0 + 127 * r + 1:r],
                                                      qTB[po:po + 64, h // 2, q0:q0 + 127 * r + 1:r], ["kTB", "qTB"],
                                                      (vtB[:, c * (nblk + 1) + mm, h, :], ["vtB", "vtB1"])))
                                    order = attend(items, mask, None, ot, which == 0, which == 1)
                                assert order == [0, 2, 1, 3]
                                flush_pv()
                                for half in range(2):
                                    av = acc[:, half:4:2, 128 * r * nn:128 * r * (nn + 1)].rearrange("p h (i r) -> p h r i", r=r)[:, :, c, :]
                                    ov = ops_[ot][:, half * 256:(half + 1) * 256].rearrange("p (h i) -> p h i", h=2)
                                    if g == 0:
                                        S.op("dve", lambda e, av=av, ov=ov: e.tensor_copy(out=av, in_=ov),
                                             reads=[("ops", ot)], writes=["acc"])
                                    else:
                                        S.op("dve", lambda e, av=av, ov=ov: e.tensor_tensor(out=av, in0=ov, in1=av, op=ALU.add),
                                             reads=[("ops", ot), "acc"], writes=["acc"])
                    for h in range(4):
                        po = (h % 2) * 64
                        S.op("dve", lambda e, h=h: e.reciprocal(out=recB[0:64, :], in_=acc[64:128, h, :]),
                             reads=["acc"], writes=["recB"])
                        S.op("pool", lambda e, h=h, po=po: e.tensor_tensor(
                            out=ostB[po:po + 64, h // 2, :], in0=acc[0:64, h, :], in1=recB[0:64, :], op=ALU.mult),
                            reads=["acc", "recB"], writes=[("ostB", h)])
                    S.dma("sp", lambda e, tseg=tseg: e.dma_start(
                        out=cat0[512:768, tseg:tseg + SEG].rearrange("(c p) t -> p c t", p=128), in_=ostB[:]),
                        reads=[("ostB", h) for h in range(4)], chan="ostB")
                S.emit_pass()

        def pass_outproj(tag, xin, catT, KC, w_dram, xout):
            with ExitStack() as es:
                wo = sbt(es, "wo", [128, KC, D], BF16)
                S.dma("pool", lambda e: e.dma_start(out=wo[:], in_=w_dram.rearrange("(c p) n -> p c n", p=128)),
                      writes=["wo"], chan="wo")
                ct = [sbt(es, "ct%d" % i, [128, KC, 512], BF16) for i in range(2)]
                xt = [sbt(es, "oxt%d" % i, [128, 4, D], F32) for i in range(2)]
                yps = [pst(es, "yps%d" % i) for i in range(4)]
                k = 0
                for g in range(G):
                    t0 = g * 512
                    b = g % 2
                    S.dma("sp", lambda e, b=b, t0=t0: e.dma_start(
                        out=ct[b][:], in_=catT[:, t0:t0 + 512].rearrange("(c p) t -> p c t", p=128)),
                        writes=[("ct", b)], chan="ct%d" % b)
                    S.dma("sp", lambda e, b=b, t0=t0: e.dma_start(
                        out=xt[b][:], in_=xin[t0:t0 + 512, :].rearrange("(j p) d -> p j d", p=128)),
                        writes=[("oxt", b)], chan="oxt%d" % b)
                    for j in range(4):
                        for half in range(2):
                            yb = k % 4
                            k += 1
                            for kc in range(KC):
                                S.op("pe", lambda e, yb=yb, b=b, kc=kc, j=j, half=half: e.matmul(
                                    yps[yb][:], lhsT=ct[b][:, kc, j * 128:(j + 1) * 128], rhs=wo[:, kc, half * 512:(half + 1) * 512],
                                    start=(kc == 0), stop=(kc == KC - 1)),
                                    reads=[("ct", b), "wo"], writes=[("yps", yb)])
                            S.op("dve", lambda e, yb=yb, b=b, j=j, half=half: e.tensor_tensor(
                                out=xt[b][:, j, half * 512:(half + 1) * 512], in0=yps[yb][:], in1=xt[b][:, j, half * 512:(half + 1) * 512], op=ALU.add),
                                reads=[("yps", yb), ("oxt", b)], writes=[("oxt", b)])
                    S.dma("sp", lambda e, b=b, t0=t0: e.dma_start(
                        out=xout[t0:t0 + 512, :].rearrange("(j p) d -> p j d", p=128), in_=xt[b][:]),
                        reads=[("oxt", b)], chan="oxt%d" % b)
                S.emit_pass()

        def pass_ffn(l, xin, xout):
            with ExitStack() as es:
                wup = sbt(es, "wup", [128, 8, 2 * DFF], BF16)
                wdn = sbt(es, "wdn", [128, NFC, D], BF16)
                upv = f_w_up[l].rearrange("(c p) n -> p c n", p=128)
                for q in range(8):
                    S.dma("pool", lambda e, q=q: e.dma_start(out=wup[:, :, q * 704:(q + 1) * 704], in_=upv[:, :, q * 704:(q + 1) * 704]),
                          writes=[("wup", q)], chan="wup")
                S.alias("wup", len(S.ops) - 1)
                dnv = f_w_down[l].rearrange("(c p) n -> p c n", p=128)
                for q in range(2):
                    S.dma("pool", lambda e, q=q: e.dma_start(out=wdn[:, q * 11:(q + 1) * 11, :], in_=dnv[:, q * 11:(q + 1) * 11, :]),
                          writes=[("wdn", q)], chan="wdn")
                S.alias("wdn", len(S.ops) - 1)
                gn = load_gain_cols(es, "fgn", f_norm[l], D, 8, 32.0, "fgn")
                cw = sbt(es, "cw", [128, 3, NFC], F32)
                for jj in range(3):
                    S.dma("sp", lambda e, jj=jj: e.dma_start(out=cw[:, jj, :], in_=f_conv_w[l, jj].rearrange("(c p) -> p c", p=128)),
                          writes=[("cw", jj)], chan="cw")
                S.alias("cw", len(S.ops) - 1)
                cbi = sbt(es, "cbi", [128, NFC], F32)
                S.dma("sp", lambda e: e.dma_start(out=cbi[:], in_=f_conv_b[l].rearrange("(c p) -> p c", p=128)),
                      writes=["cbi"], chan="cbi")
                xt = sbt(es, "fxt", [128, 4, D], F32)
                xh = sbt(es, "fxh", [2, D], F32)
                xn = [sbt(es, "fxn%d" % i, [128, D], BF16) for i in range(2)]
                xhn = sbt(es, "fxhn", [2, D], BF16)
                ssq = sbt(es, "fssq", [128, 8], F32)
                rst = sbt(es, "frst", [128, 8], F32)
                xnT = sbt(es, "fxnT", [128, 8, 512], BF16)
                xnTh = sbt(es, "fxnTh", [128, 8, 2], BF16)
                gsb = [sbt(es, "gsb%d" % i, [128, 514], F32) for i in range(3)]
                aa = [sbt(es, "aa%d" % i, [128, 512], F32) for i in range(3)]
                hT = sbt(es, "hT", [128, NFC, 512], BF16)
                junk = [hT[:, 0:2, :].rearrange("p a b -> p (a b)"), hT[:, 2:4, :].rearrange("p a b -> p (a b)")]
                gps = [pst(es, "gps%d" % i) for i in range(2)]
                vps = [pst(es, "fvps%d" % i) for i in range(3)]
                hpsl = [pst(es, "hps%d" % i) for i in range(1)]
                yps = [pst(es, "fyps%d" % i) for i in range(2)]
                ptr = yps[1][:, :].bitcast(BF16)
                k = 0
                fi = 0
                for g in range(G):
                    t0 = g * 512
                    has_l = (t0 > 0)
                    has_r = (t0 + 512 < T)
                    S.dma("sp", lambda e, t0=t0: e.dma_start(
                        out=xt[:], in_=xin[t0:t0 + 512, :].rearrange("(j p) d -> p j d", p=128)),
                        writes=["fxt"], chan="fxt")
                    S.op("pool", lambda e: e.memset(ssq[:], 0.0), writes=["fssq"])
                    if has_l or has_r:
                        S.op("pool", lambda e: e.memset(xh[:], 1.0), writes=["fxh"])
                        if has_l:
                            S.dma("sp", lambda e, t0=t0: e.dma_start(out=xh[0:1, :], in_=xin[t0 - 1:t0, :]), reads=["fxh"], writes=["fxh"], chan="fxh")
                        if has_r:
                            S.dma("sp", lambda e, t0=t0: e.dma_start(out=xh[1:2, :], in_=xin[t0 + 512:t0 + 513, :]), reads=["fxh"], writes=["fxh"], chan="fxh")
                        S.op("act", lambda e: e.activation(out=junk[0][0:2, :], in_=xh[:], func=AF.Square, accum_out=ssq[0:2, 4:5]),
                             reads=["fxh", "fssq"], writes=["fssq", ("hT", 0), ("hT", 1)])
                    for j in range(4):
                        S.op("act", lambda e, j=j: e.activation(
                            out=junk[j % 2], in_=xt[:, j, :], func=AF.Square, accum_out=ssq[:, j:j + 1]),
                            reads=["fxt", "fssq"], writes=["fssq", ("hT", 2 * (j % 2)), ("hT", 2 * (j % 2) + 1)])
                    S.op("act", lambda e: e.activation(out=rst[:], in_=ssq[:], func=AF.Sqrt, bias=epsc[:, 0:1], scale=1.0),
                         reads=["fssq", ("epsc", 0)], writes=["frst"])
                    S.op("dve", lambda e: e.reciprocal(out=rst[:], in_=rst[:]), reads=["frst"], writes=["frst"])
                    if has_l or has_r:
                        S.op("dve", lambda e: e.tensor_scalar(out=xhn[:], in0=xh[:], scalar1=rst[0:2, 4:5], scalar2=None, op0=ALU.mult),
                             reads=["fxh", "frst"], writes=["fxhn"])
                    for j in range(4):
                        nb_ = j % 2
                        S.op("dve" if j % 2 else "act", (lambda e, j=j, nb_=nb_: e.tensor_scalar(
                            out=xn[nb_][:], in0=xt[:, j, :], scalar1=rst[:, j:j + 1], scalar2=None, op0=ALU.mult)) if j % 2 else
                            (lambda e, j=j, nb_=nb_: e.activation(out=xn[nb_][:], in_=xt[:, j, :], func=AF.Copy, scale=rst[:, j:j + 1])),
                            reads=["fxt", "frst"], writes=[("fxn", nb_)])
                        for c in range(8):
                            S.op("pe", lambda e, c=c, j=j, nb_=nb_: e.transpose(
                                ptr[:, c * 128:(c + 1) * 128], xn[nb_][:, c * 128:(c + 1) * 128], cbm(CB_IDENT)),
                                reads=[("fxn", nb_), "cb"], writes=[("fyps", 1)])
                        S.op("dve", lambda e, j=j: e.tensor_tensor(
                            out=xnT[:, :, j * 128:(j + 1) * 128], in0=ptr[:, :].rearrange("p (c t) -> p c t", c=8),
                            in1=gn[:, :].unsqueeze(2).broadcast_to([128, 8, 128]), op=ALU.mult),
                            reads=[("fyps", 1), "fgn"], writes=[("fxnT", j)])
                    if has_l or has_r:
                        for c in range(8):
                            S.op("pe", lambda e, c=c: e.transpose(
                                ptr[:, c * 128:c * 128 + 2], xhn[0:2, c * 128:(c + 1) * 128], cbm(CB_IDENT, 2, 2)),
                                reads=["fxhn", "cb"], writes=[("fyps", 1)])
                        S.op("dve", lambda e: e.tensor_tensor(
                            out=xnTh[:], in0=ptr[:, :].rearrange("p (c t) -> p c t", c=8)[:, :, 0:2],
                            in1=gn[:, :].unsqueeze(2).broadcast_to([128, 8, 2]), op=ALU.mult),
                             reads=[("fyps", 1), "fgn"], writes=["fxnTh"])
                    xk = [("fxnT", j) for j in range(4)]
                    pend_tail = []
                    for fc in range(NFC):
                        gb = fi % 2
                        v3 = fi % 3
                        fi += 1
                        for (dst, col0, key) in ((gps[gb], fc * 128, ("gps", gb)), (vps[v3], DFF + fc * 128, ("fvps", v3))):
                            for c in range(8):
                                S.op("pe", lambda e, dst=dst, col0=col0, c=c: e.matmul(
                                    dst[:], lhsT=wup[:, c, col0:col0 + 128], rhs=xnT[:, c, :],
                                    start=(c == 0), stop=(c == 7)),
                                    reads=xk + ["wup"], writes=[key])
                            if dst is gps[gb] and (has_l or has_r):
                                for c in range(8):
                                    S.op("pe", lambda e, col0=col0, c=c, fc=fc: e.matmul(
                                        hpsl[0][:, 2 * fc:2 * fc + 2], lhsT=wup[:, c, col0:col0 + 128], rhs=xnTh[:, c, :],
                                        start=(c == 0), stop=(c == 7)),
                                        reads=["fxnTh", "wup"], writes=[("hps", 0)])
                        S.op("act", lambda e, gb=gb, v3=v3: e.activation(out=gsb[v3][:, 1:513], in_=gps[gb][:], func=AF.Copy),
                             reads=[("gps", gb)], writes=[("gsb", v3, 1)])
                        for side, has in ((0, has_l), (1, has_r)):
                            colo = 0 if side == 0 else 513
                            if not has:
                                S.op("pool", lambda e, gb=v3, colo=colo: e.memset(gsb[gb][:, colo:colo + 1], 0.0),
                                     writes=[("gsb", v3, 0 if side == 0 else 2)])
                            else:
                                tpos = t0 if side == 0 else t0 + 512
                                sc = pc[:, 0:1] if tpos == HB else 1.0
                                S.op("dve", lambda e, gb=v3, colo=colo, fc=fc, side=side, sc=sc: e.tensor_scalar(
                                    out=gsb[gb][:, colo:colo + 1], in0=hpsl[0][:, 2 * fc + side:2 * fc + side + 1], scalar1=sc, scalar2=None, op0=ALU.mult),
                                    reads=[("hps", 0), "pc"], writes=[("gsb", v3, 0 if side == 0 else 2)])
                        gk = [("gsb", v3, i) for i in range(3)]
                        S.op(ENG_FFN_CONV, lambda e, gb=v3, fc=fc: e.tensor_scalar(
                            out=aa[gb][:], in0=gsb[gb][:, 1:513], scalar1=cw[:, 1, fc:fc + 1], scalar2=None, op0=ALU.mult),
                            reads=gk + ["cw"], writes=[("aa", v3)])
                        S.op("dve", lambda e, gb=v3, fc=fc: e.scalar_tensor_tensor(
                            out=aa[gb][:], in0=gsb[gb][:, 0:512], scalar=cw[:, 0, fc:fc + 1], in1=aa[gb][:], op0=ALU.mult, op1=ALU.add),
                            reads=gk + ["cw", ("aa", v3)], writes=[("aa", v3)])
                        S.op("dve", lambda e, gb=v3, fc=fc: e.scalar_tensor_tensor(
                            out=aa[gb][:], in0=gsb[gb][:, 2:514], scalar=cw[:, 2, fc:fc + 1], in1=aa[gb][:], op0=ALU.mult, op1=ALU.add),
                            reads=gk + ["cw", ("aa", v3)], writes=[("aa", v3)])
                        def tail(v3=v3, fc=fc):
                            S.op("act", lambda e, gb=v3, fc=fc: e.activation(
                                out=aa[gb][:], in_=aa[gb][:], func=AF.Gelu_apprx_tanh, bias=cbi[:, fc:fc + 1], scale=1.0),
                                reads=[("aa", v3), "cbi"], writes=[("aa", v3)])
                            S.op("dve", lambda e, gb=v3, fc=fc: e.tensor_tensor(
                                out=hT[:, fc, :], in0=vps[gb][:], in1=aa[gb][:], op=ALU.mult),
                                reads=[("fvps", v3), ("aa", v3)], writes=[("hT", fc)])
                        if pend_tail:
                            pend_tail.pop()()
                        pend_tail.append(tail)
                    pend_tail.pop()()
                    hk = [("hT", fc) for fc in range(NFC)]
                    for j in range(4):
                        for half in range(2):
                            yb = k % 2
                            k += 1
                            for fc in range(NFC):
                                S.op("pe", lambda e, yb=yb, fc=fc, j=j, half=half: e.matmul(
                                    yps[yb][:], lhsT=hT[:, fc, j * 128:(j + 1) * 128], rhs=wdn[:, fc, half * 512:(half + 1) * 512],
                                    start=(fc == 0), stop=(fc == NFC - 1)),
                                    reads=[("hT", fc), "wdn"], writes=[("fyps", yb)])
                            S.op("dve", lambda e, yb=yb, j=j, half=half: e.tensor_tensor(
                                out=xt[:, j, half * 512:(half + 1) * 512], in0=yps[yb][:], in1=xt[:, j, half * 512:(half + 1) * 512], op=ALU.add),
                                reads=[("fyps", yb), "fxt"], writes=["fxt"])
                    S.dma("sp", lambda e, t0=t0: e.dma_start(
                        out=xout[t0:t0 + 512, :].rearrange("(j p) d -> p j d", p=128), in_=xt[:]),
                        reads=["fxt"], chan="fxt")
                S.emit_pass()

        def pass_c1():
            with ExitStack() as es:
                win1 = sbt(es, "win1", [128, 8, 544], BF16)
                S.dma("pool", lambda e: e.dma_start(out=win1[:], in_=o_w_in.rearrange("(c p) n -> p c n", p=128)),
                      writes=["win1"], chan="win1")
                wuq = sbt(es, "wuq", [128, 2, 1536], BF16)
                S.dma("pool", lambda e: e.dma_start(out=wuq[:], in_=o_w_uq.rearrange("(c p) n -> p c n", p=128)),
                      writes=["wuq"], chan="wuq")
                wukx = sbt(es, "wukx", [128, 2, 16, 96], BF16)
                wuv = sbt(es, "wuv", [128, 2, 16, 64], BF16)
                S.op("pool", lambda e: e.memset(wukx[:], 0.0), writes=["wukx"])
                ukv = o_w_ukv.rearrange("(c p) (h n) -> p c h n", p=128, n=128)
                for i in range(2):
                    S.dma("pool", lambda e, i=i: e.dma_start(out=wukx[:, i, :, 0:64], in_=ukv[:, i, :, 0:64]),
                          reads=["wukx"], writes=["wukx"], chan="wukx")
                    S.dma("pool", lambda e, i=i: e.dma_start(out=wuv[:, i, :, :], in_=ukv[:, i, :, 64:128]),
                          writes=["wuv"], chan="wuv")
                gn = load_gain_cols(es, "cgn", o_norm, D, 8, 32.0, "cgn")
                gql = load_gain_cols(es, "gql", o_q_lora_gain, 256, 2, 16.0, "gql")
                gkl = load_gain_cols(es, "gkl", o_kv_gain, 256, 2, 16.0, "gkl")
                g96 = sbt(es, "g96", [96, 2], F32)
                S.dma("sp", lambda e: e.dma_start(out=g96[:, 0:1], in_=o_q_gain.rearrange("(p o) -> p o", o=1)), writes=[("g96", 0)], chan="g96")
                S.dma("sp", lambda e: e.dma_start(out=g96[:, 1:2], in_=o_k_gain.rearrange("(p o) -> p o", o=1)), writes=[("g96", 1)], chan="g96")
                S.alias("g96", len(S.ops) - 1)
                xt = sbt(es, "cxt", [128, 4, D], F32)
                xn = [sbt(es, "cxn%d" % i, [128, D], BF16) for i in range(2)]
                junk = [sbt(es, "cjunk%d" % i, [128, D], BF16) for i in range(2)]
                ssq = sbt(es, "cssq", [128, 4], F32)
                rst = sbt(es, "crst", [128, 4], F32)
                xnT = sbt(es, "cxnT", [128, 8, 512], BF16)
                cst = [sbt(es, "ccst%d" % i, [96, 2, 512], F32) for i in range(2)]
                csq = [sbt(es, "csq%d" % i, [128, 512], BF16) for i in range(2)]
                craw = [sbt(es, "craw%d" % i, [128, 512], F32) for i in range(2)]
                crs = sbt(es, "crs", [128, 512], F32)
                cn = [sbt(es, "cn%d" % i, [128, 2, 512], BF16) for i in range(2)]
                krT = sbt(es, "krT", [32, 512], BF16)
                sq = [sbt(es, "hsq%d" % i, [96, 512], BF16) for i in range(2)]
                qg = [sbt(es, "hqg%d" % i, [96, 512], BF16) for i in range(2)]
                sd = [sbt(es, "hsd%d" % i, [96, 512], F32) for i in range(2)]
                t1 = [sbt(es, "ht1%d" % i, [96, 512], F32) for i in range(2)]
                t2 = [sbt(es, "ht2%d" % i, [96, 512], F32) for i in range(2)]
                qf = [sbt(es, "hqf%d" % i, [96, 512], BF16) for i in range(3)]
                vst = [sbt(es, "cvst%d" % i, [128, 16, 128], BF16) for i in range(2)]
                for i in range(2):
                    S.op("pool", lambda e, i=i: e.memset(vst[i][:, :, 64:128], 1.0), writes=[("cvst1", i)])
                ptr = pst(es, "cptr", BF16, 1024)
                cps = [pst(es, "cps%d" % i) for i in range(2)]
                qps = [pst(es, "cqps%d" % i) for i in range(2)]
                sps = pst(es, "csps")
                rps = pst(es, "crps")
                vps = pst(es, "cvps")
                ci = 0
                hi = 0
                vi = 0
                for g in range(G):
                    t0 = g * 512
                    b = g % 2
                    S.dma("sp", lambda e, t0=t0: e.dma_start(
                        out=xt[:], in_=x2[t0:t0 + 512, :].rearrange("(j p) d -> p j d", p=128)),
                        writes=["cxt"], chan="cxt")
                    S.dma("sp", lambda e, b=b, t0=t0: e.dma_start(
                        out=cst[b][:], in_=cs96[:, :, t0:t0 + 512].rearrange("a p t -> p a t")),
                        writes=[("ccst", b)], chan="ccst%d" % b)
                    S.op("pool", lambda e: e.memset(ssq[:], 0.0), writes=["cssq"])
                    for j in range(4):
                        S.op("act", lambda e, j=j: e.activation(
                            out=junk[j % 2][:], in_=xt[:, j, :], func=AF.Square, accum_out=ssq[:, j:j + 1]),
                            reads=["cxt", "cssq"], writes=["cssq", ("cjunk", j % 2)])
                    S.op("act", lambda e: e.activation(out=rst[:], in_=ssq[:], func=AF.Sqrt, bias=epsc[:, 0:1], scale=1.0),
                         reads=["cssq", ("epsc", 0)], writes=["crst"])
                    S.op("dve", lambda e: e.reciprocal(out=rst[:], in_=rst[:]), reads=["crst"], writes=["crst"])
                    for j in range(4):
                        nb_ = j % 2
                        S.op("dve" if j % 2 else "pool", lambda e, j=j, nb_=nb_: e.tensor_scalar(
                            out=xn[nb_][:], in0=xt[:, j, :], scalar1=rst[:, j:j + 1], scalar2=None, op0=ALU.mult),
                            reads=["cxt", "crst"], writes=[("cxn", nb_)])
                        for c in range(8):
                            S.op("pe", lambda e, c=c, nb_=nb_: e.transpose(
                                ptr[:, c * 128:(c + 1) * 128], xn[nb_][:, c * 128:(c + 1) * 128], cbm(CB_IDENT)),
                                reads=[("cxn", nb_), "cb"], writes=["cptr"])
                        S.op("dve", lambda e, j=j: e.tensor_tensor(
                            out=xnT[:, :, j * 128:(j + 1) * 128], in0=ptr[:, :].rearrange("p (c t) -> p c t", c=8),
                            in1=gn[:, :].unsqueeze(2).broadcast_to([128, 8, 128]), op=ALU.mult),
                            reads=["cptr", "cgn"], writes=[("cxnT", j)])
                    xk = [("cxnT", j) for j in range(4)]
                    for which, (col0, gl_, glk) in enumerate(((0, gql, "gql"), (256, gkl, "gkl"))):
                        for i in range(2):
                            cb_ = ci % 2
                            ci += 1
                            for c in range(8):
                                S.op("pe", lambda e, cb_=cb_, c=c, col0=col0, i=i: e.matmul(
                                    cps[cb_][:], lhsT=win1[:, c, col0 + i * 128:col0 + (i + 1) * 128], rhs=xnT[:, c, :],
                                    start=(c == 0), stop=(c == 7)),
                                    reads=xk + ["win1"], writes=[("cps", cb_)])
                            S.op("act", lambda e, cb_=cb_, i=i: e.activation(out=csq[i][:], in_=cps[cb_][:], func=AF.Square),
                                 reads=[("cps", cb_)], writes=[("csq", i)])
                            S.op("act", lambda e, cb_=cb_, i=i: e.activation(out=craw[i][:], in_=cps[cb_][:], func=AF.Copy),
                                 reads=[("cps", cb_)], writes=[("craw", i)])
                        for i in range(2):
                            S.op("pe", lambda e, i=i: e.matmul(sps[:], lhsT=cbm(CB_ONES), rhs=csq[i][:], start=(i == 0), stop=(i == 1)),
                                 reads=[("csq", i), "cb"], writes=["csps"])
                        S.op("act", lambda e: e.activation(out=crs[:], in_=sps[:], func=AF.Sqrt, bias=epsc[:, 2:3], scale=1.0),
                             reads=["csps", ("epsc", 2)], writes=["crs"])
                        S.op("dve", lambda e: e.reciprocal(out=crs[:], in_=crs[:]), reads=["crs"], writes=["crs"])
                        for i in range(2):
                            S.op("dve", lambda e, i=i, which=which, gl_=gl_: e.scalar_tensor_tensor(
                                out=cn[which][:, i, :], in0=craw[i][:], scalar=gl_[:, i:i + 1], in1=crs[:], op0=ALU.mult, op1=ALU.mult),
                                reads=[("craw", i), "crs", glk], writes=[("cn", which, i)])
                    cb_ = ci % 2
                    ci += 1
                    for c in range(8):
                        S.op("pe", lambda e, cb_=cb_, c=c: e.matmul(
                            cps[cb_][0:32, :], lhsT=win1[:, c, 512:544], rhs=xnT[:, c, :], start=(c == 0), stop=(c == 7)),
                            reads=xk + ["win1"], writes=[("cps", cb_)])
                    S.op("act", lambda e, cb_=cb_: e.activation(out=krT[:], in_=cps[cb_][0:32, :], func=AF.Copy),
                         reads=[("cps", cb_)], writes=["krT"])
                    pend = []
                    for isk in range(2):
                        for h in range(16):
                            qb = hi % 2
                            hi += 1
                            if isk == 0:
                                for i in range(2):
                                    S.op("pe", lambda e, qb=qb, h=h, i=i: e.matmul(
                                        qps[qb][0:96, :], lhsT=wuq[:, i, h * 96:(h + 1) * 96], rhs=cn[0][:, i, :],
                                        start=(i == 0), stop=(i == 1)),
                                        reads=[("cn", 0, i), "wuq"], writes=[("cqps", qb)])
                            else:
                                for i in range(2):
                                    S.op("pe", lambda e, qb=qb, h=h, i=i: e.matmul(
                                        qps[qb][0:96, :], lhsT=wukx[:, i, h, :], rhs=cn[1][:, i, :],
                                        start=(i == 0), stop=False),
                                        reads=[("cn", 1, i), "wukx"], writes=[("cqps", qb)])
                                S.op("pe", lambda e, qb=qb: e.matmul(
                                    qps[qb][0:96, :], lhsT=cbm(CB_E, 32, 96), rhs=krT[:], start=False, stop=True),
                                    reads=["krT", "cb"], writes=[("cqps", qb)])
                            def post(qb=qb, isk=isk, h=h, b=b, t0=t0, hi=hi):
                                S.op("act", lambda e, qb=qb: e.activation(out=sq[qb][:], in_=qps[qb][0:96, :], func=AF.Square),
                                     reads=[("cqps", qb)], writes=[("hsq", qb)])
                                S.op("act", lambda e, qb=qb, isk=isk: e.activation(
                                    out=qg[qb][:], in_=qps[qb][0:96, :], func=AF.Copy, scale=g96[:, isk:isk + 1]),
                                    reads=[("cqps", qb), "g96"], writes=[("hqg", qb)])
                                S.op("pe", lambda e, qb=qb: e.matmul(sps[0:96, :], lhsT=cbm(CB_ONES, 96, 96), rhs=sq[qb][:], start=True, stop=True),
                                     reads=[("hsq", qb), "cb"], writes=["csps"])
                                S.op("pe", lambda e, qb=qb: e.matmul(rps[0:96, :], lhsT=cbm(CB_R96, 96, 96), rhs=qg[qb][:], start=True, stop=True),
                                     reads=[("hqg", qb), "cb"], writes=["crps"])
                                S.op("act", lambda e, qb=qb: e.activation(
                                    out=sd[qb][:], in_=sps[0:96, :], func=AF.Sqrt, bias=epsc[0:96, 3:4], scale=1.0),
                                    reads=["csps", ("epsc", 3)], writes=[("hsd", qb)])
                                S.op(ENG_T1, lambda e, qb=qb, b=b: e.tensor_tensor(
                                    out=t1[qb][:], in0=qg[qb][:], in1=cst[b][:, 0, :], op=ALU.mult),
                                    reads=[("hqg", qb), ("ccst", b)], writes=[("ht1", qb)])
                                S.op("dve", lambda e, qb=qb, b=b: e.tensor_tensor(
                                    out=t2[qb][:], in0=rps[0:96, :], in1=cst[b][:, 1, :], op=ALU.mult),
                                    reads=["crps", ("ccst", b)], writes=[("ht2", qb)])
                                S.op(ENG_ADD, lambda e, qb=qb: e.tensor_tensor(
                                    out=t1[qb][:], in0=t1[qb][:], in1=t2[qb][:], op=ALU.add),
                                    reads=[("ht1", qb), ("ht2", qb)], writes=[("ht1", qb)])
                                S.op("dve", lambda e, qb=qb: e.reciprocal(out=sd[qb][:], in_=sd[qb][:]),
                                     reads=[("hsd", qb)], writes=[("hsd", qb)])
                                fb = hi % 3
                                S.op("dve", lambda e, qb=qb, fb=fb: e.tensor_tensor(
                                    out=qf[fb][:], in0=t1[qb][:], in1=sd[qb][:], op=ALU.mult),
                                    reads=[("ht1", qb), ("hsd", qb)], writes=[("hqf", fb)])
                                dst = k1 if isk else q1
                                S.dma("sp", lambda e, fb=fb, h=h, t0=t0, dst=dst: e.dma_start(
                                    out=dst[h * 96:(h + 1) * 96, t0:t0 + 512], in_=qf[fb][:]),
                                    reads=[("hqf", fb)], chan="hqf%d" % fb)
                            if pend:
                                pend.pop()()
                            pend.append(post)
                    pend.pop()()
                    for j in range(4):
                        vb = vi % 2
                        vi += 1
                        for half in range(2):
                            for i in range(2):
                                S.op("pe", lambda e, half=half, i=i, j=j: e.matmul(
                                    vps[:], lhsT=cn[1][:, i, j * 128:(j + 1) * 128],
                                    rhs=wuv[:, i, half * 8:(half + 1) * 8, :].rearrange("p h d -> p (h d)"),
                                    start=(i == 0), stop=(i == 1)),
                                    reads=[("cn", 1, i), "wuv"], writes=["cvps"])
                            S.op("act" if half else "dve", (lambda e, vb=vb, half=half: e.activation(
                                out=vst[vb][:, half * 8:(half + 1) * 8, 0:64], in_=vps[:, :].rearrange("p (h d) -> p h d", h=8), func=AF.Copy)) if half else
                                (lambda e, vb=vb, half=half: e.tensor_copy(
                                    out=vst[vb][:, half * 8:(half + 1) * 8, 0:64], in_=vps[:, :].rearrange("p (h d) -> p h d", h=8))),
                                reads=["cvps"], writes=[("cvst", vb, half)])
                        S.dma("sp", lambda e, vb=vb, t0=t0, j=j: e.dma_start(
                            out=v1[t0 + j * 128:t0 + (j + 1) * 128, :], in_=vst[vb][:].rearrange("p h d -> p (h d)")),
                            reads=[("cvst", vb, 0), ("cvst", vb, 1), ("cvst1", vb)], chan="cvst%d" % vb)
                S.emit_pass()

        def pass_c2():
            with ExitStack() as es:
                kT = [sbt(es, "dkT%d" % i, [96, T], BF16) for i in range(2)]
                qT = [sbt(es, "dqT%d" % i, [96, T], BF16) for i in range(2)]
                vt = [sbt(es, "dvt%d" % i, [128, NT, 128], BF16) for i in range(2)]
                ost = [sbt(es, "dost%d" % i, [64, T], BF16) for i in range(2)]
                pt = [sbt(es, "dpt%d" % i, [128, 1024], BF16) for i in range(4)]
                rec = [sbt(es, "drec%d" % i, [128, 512], F32) for i in range(2)]
                sps = [pst(es, "dsps%d" % i, F32, 1024) for i in range(3)]
                ops_ = [pst(es, "dops%d" % i) for i in range(2)]
                SC = float(np.sqrt(96.0))
                NQT = T // 512
                NKP = NT // 2
                assert (HB // 128) % 2 == 0
                steps = [(h, qt, kp) for h in range(16) for qt in range(NQT) for kp in range(NKP)]
                LAG = 2

                def load_head(h):
                    b = h % 2
                    S.dma("sp", lambda e, b=b, h=h: e.dma_start(out=kT[b][:], in_=k1[h * 96:(h + 1) * 96, :]),
                          writes=[("dkT", b)], chan="dkT%d" % b)
                    S.dma("sp", lambda e, b=b, h=h: e.dma_start(out=qT[b][:], in_=q1[h * 96:(h + 1) * 96, :]),
                          writes=[("dqT", b)], chan="dqT%d" % b)
                    S.dma("sp", lambda e, b=b, h=h: e.dma_start(
                        out=vt[b][:], in_=v1[:, h * 128:(h + 1) * 128].rearrange("(m p) c -> p m c", p=128)),
                        writes=[("dvt", b)], chan="dvt%d" % b)

                def emit_s(i):
                    h, qt, kp = steps[i]
                    b = h % 2
                    sb_ = i % 3
                    pb_ = i % 4
                    cross = ((qt * 512 < HB) != (kp * 256 < HB))
                    for u in range(2):
                        kb = 2 * kp + u
                        S.op("pe", lambda e, sb_=sb_, b=b, kb=kb, qt=qt, u=u: e.matmul(
                            sps[sb_][:, u * 512:(u + 1) * 512], lhsT=kT[b][:, kb * 128:(kb + 1) * 128],
                            rhs=qT[b][:, qt * 512:(qt + 1) * 512], start=True, stop=True),
                            reads=[("dkT", b), ("dqT", b)], writes=[("dsps", sb_)])
                    bias_ap = pc[:, 1:2] if cross else epsc[:, 4:5]
                    S.op("act", lambda e, sb_=sb_, pb_=pb_, bias_ap=bias_ap: e.activation(
                        out=pt[pb_][:], in_=sps[sb_][:], func=AF.Exp, bias=bias_ap, scale=SC),
                        reads=[("dsps", sb_), "pc", ("epsc", 4)], writes=[("dpt", pb_)])

                def emit_pv(i):
                    h, qt, kp = steps[i]
                    b = h % 2
                    pb_ = i % 4
                    ob = (h * NQT + qt) % 2
                    for u in range(2):
                        kb = 2 * kp + u
                        S.op("pe", lambda e, pb_=pb_, b=b, kb=kb, ob=ob, u=u: e.matmul(
                            ops_[ob][:], lhsT=vt[b][:, kb, :], rhs=pt[pb_][:, u * 512:(u + 1) * 512],
                            start=(kb == 0), stop=(kb == NT - 1)),
                            reads=[("dpt", pb_), ("dvt", b)], writes=[("dops", ob)])
                    if kp == NKP - 1:
                        S.op("dve", lambda e, ob=ob: e.reciprocal(out=rec[ob][64:128, :], in_=ops_[ob][64:128, :]),
                             reads=[("dops", ob)], writes=[("drec", ob)])
                        S.op("dve", lambda e, ob=ob, b=b, qt=qt: e.tensor_tensor(
                            out=ost[b][:, qt * 512:(qt + 1) * 512], in0=ops_[ob][0:64, :], in1=rec[ob][64:128, :], op=ALU.mult),
                            reads=[("dops", ob), ("drec", ob)], writes=[("dost", b)])
                        if qt == NQT - 1:
                            S.dma("sp", lambda e, b=b, h=h: e.dma_start(out=cat1[h * 64:(h + 1) * 64, :], in_=ost[b][:]),
                                  reads=[("dost", b)], chan="dost%d" % b)

                load_head(0)
                for i in range(len(steps) + LAG):
                    if i < len(steps):
                        h, qt, kp = steps[i]
                        if h + 1 < 16 and kp == 0 and qt == 1:
                            load_head(h + 1)
                        emit_s(i)
                    if i >= LAG:
                        emit_pv(i - LAG)
                S.emit_pass()

        if "a1" in passes:
            pass_a1()
        if "a2" in passes:
            pass_a2()
        if "a3" in passes:
            pass_outproj("a3", x_in, cat0, 6, e_w_out, x1)
        if "b" in passes:
            pass_ffn(0, x1, x2)
        if "c1" in passes:
            pass_c1()
        if "c2" in passes:
            pass_c2()
        if "c3" in passes:
            pass_outproj("c3", x2, cat1, 8, o_w_out, x3)
        if "d" in passes:
            pass_ffn(1, x3, y_out)

        S.final_wait()
    return nc


def _rope_tables(T, L):
    pos = (np.arange(T) % L).astype(np.float32)

    def tab(Dr):
        inv = np.power(np.float32(10000.0), -np.arange(0, Dr, 2, dtype=np.float32) / np.float32(Dr)).astype(np.float32)
        ang = (pos[:, None] * inv[None, :]).astype(np.float32)
        return np.cos(ang).astype(np.float32).T, np.sin(ang).astype(np.float32).T

    c64, s64 = tab(64)
    c32, s32 = tab(32)
    cs64 = np.stack([np.tile(c64, (4, 1)), np.tile(s64, (4, 1))]).astype(np.float32)
    c96 = np.concatenate([np.ones((64, T), np.float32), c32, c32])
    s96 = np.concatenate([np.zeros((64, T), np.float32), s32, s32])
    cs96 = np.stack([c96, s96]).astype(np.float32)
    return np.ascontiguousarray(cs64), np.ascontiguousarray(cs96)


def _const_mats(is_prompt):
    m = np.zeros((NCB, 128, 128), np.float32)
    m[CB_IDENT] = np.eye(128)
    k = np.arange(128)
    m[CB_B64] = (k[:, None] // 64 == k[None, :] // 64)
    for o in range(128):
        if o % 64 < 32:
            m[CB_R64][o + 32, o] = -1.0
        else:
            m[CB_R64][o - 32, o] = 1.0
    m[CB_ONES] = 1.0
    for i in range(16):
        m[CB_R96][80 + i, 64 + i] = -1.0
        m[CB_R96][64 + i, 80 + i] = 1.0
    for i in range(32):
        m[CB_E][i, 64 + i] = 1.0
    j = k[:, None]
    i = k[None, :]
    ge = (j >= i).astype(np.float32)
    le = (j <= i).astype(np.float32)
    gef = ge * (j >= 64)
    lel = le * (j < 64)
    m[CB_GE], m[CB_LE], m[CB_GEF], m[CB_LEL] = ge, le, gef, lel
    if is_prompt:
        m[CB_GEB], m[CB_LEB] = gef, lel
        m[CB_GEAB], m[CB_LEAB] = 0.0, 0.0
    else:
        m[CB_GEB], m[CB_LEB] = ge, le
        m[CB_GEAB], m[CB_LEAB] = ge, le
    out = np.ascontiguousarray(m.transpose(1, 0, 2).reshape(128, NCB * 128))
    pcv = np.zeros((128, 4), np.float32)
    pcv[:, 0] = 0.0 if is_prompt else 1.0
    pcv[:, 1] = NEGB if is_prompt else 0.0
    return out, pcv


_WNAMES = ("e_norm", "e_w_in", "e_a_q_gain", "e_a_k_gain", "e_a_sink", "e_b_q_gain", "e_b_k_gain", "e_w_out",
           "o_norm", "o_w_in", "o_q_lora_gain", "o_w_uq", "o_kv_gain", "o_w_ukv", "o_q_gain", "o_k_gain", "o_w_out")
_FNAMES = ("f_norm", "f_w_up", "f_conv_w", "f_conv_b", "f_w_down")


def make_in_maps(streams, T, weights):
    wmap = {}
    for n in _WNAMES:
        a = np.asarray(weights[n], np.float32)
        wmap[n] = np.ascontiguousarray(a.reshape(a.shape[1:]))
    for n in _FNAMES:
        wmap[n] = np.ascontiguousarray(np.asarray(weights[n], np.float32))
    tabs = {}
    in_maps = []
    for (xs, is_p) in streams:
        if is_p not in tabs:
            cs64, cs96 = _rope_tables(T, T // 2 if is_p else T)
            cm, pcv = _const_mats(is_p)
            tabs[is_p] = (cs64, cs96, cm, pcv)
        cs64, cs96, cm, pcv = tabs[is_p]
        m = dict(wmap)
        m.update(x=np.ascontiguousarray(xs, dtype=np.float32), cs64=cs64, cs96=cs96, consts=cm, percore=pcv)
        in_maps.append(m)
    return in_maps


_PROG = {}


def kernel(**inputs):
    T = 8192
    xp = np.asarray(inputs["x_prompt"], np.float32)
    xs = np.asarray(inputs["x_sample"], np.float32)
    streams = []
    for c in range(4):
        streams.append((xp[2 * c:2 * c + 2].reshape(T, D), True))
    for c in range(4):
        streams.append((xs[c], False))
    in_maps = make_in_maps(streams, T, inputs)
    if T not in _PROG:
        _PROG[T] = build_program(T)
    res = run_bass_kernel_spmd(_PROG[T], in_maps, core_ids=list(range(8)))
    ys = [np.asarray(r["y"], np.float32) for r in res.results]
    y_prompt = np.stack([ys[c].reshape(2, T // 2, D) for c in range(4)]).reshape(8, T // 2, D)
    y_sample = np.stack(ys[4:8])
    return (y_prompt, y_sample)
```

```python
import os
import numpy as np
from contextlib import ExitStack
import concourse.bass as bass
import concourse.mybir as mybir
from concourse.bass_utils import run_bass_kernel_spmd

F32 = mybir.dt.float32
BF16 = mybir.dt.bfloat16
AF = mybir.ActivationFunctionType
ALU = mybir.AluOpType

D = 1024
EPS = 1e-6
DFF = 2816
NFC = DFF // 128
PAD = 1024
NQK = 18
NEGB = -30000.0

ENGS = ("pe", "act", "dve", "pool", "sp")
STRICT = not os.environ.get("NOSTRICT")
PENG = "dve" if os.environ.get("POOL2DVE") else "pool"
ENG_FFN_CONV = os.environ.get("ENG_FFN_CONV", "dve")
ENG_T1 = os.environ.get("ENG_T1", "pool")
ENG_ADD = os.environ.get("ENG_ADD", "dve")
ENG_MASK = os.environ.get("ENG_MASK", "dve")


class Sched:
    def __init__(self, nc, es):
        self.nc = nc
        self.es = es
        self.sem = {e: es.enter_context(nc.semaphore("sem_" + e)) for e in ("pe", "act", "dve", "pool")}
        self.sigcnt = {e: 0 for e in self.sem}
        self.chans = {}
        self.waited = {e: {} for e in ENGS}
        self.ops = []
        self.last_w = {}
        self.readers = {}
        self.bar_val = 0
        self.n_inst = 0

    def chan(self, name):
        if name not in self.chans:
            self.chans[name] = dict(sem=self.es.enter_context(self.nc.semaphore("ch_" + name)), issued=0, name=name)
        return self.chans[name]

    def _add(self, o, reads, writes):
        idx = len(self.ops)
        deps = []
        seen = set()

        def add_dep(d, raw):
            if d in seen:
                return
            seen.add(d)
            deps.append((d, raw))

        for k in reads:
            if k in self.last_w:
                add_dep(self.last_w[k], True)
        for k in writes:
            if k in self.last_w:
                lw = self.ops[self.last_w[k]]
                if not (o["dma"] and lw["dma"] and lw["chan"] is o["chan"] and lw["eng"] == o["eng"]):
                    add_dep(self.last_w[k], False)
            for r in self.readers.get(k, ()):
                add_dep(r, False)
        for k in reads:
            self.readers.setdefault(k, []).append(idx)
        for k in writes:
            self.last_w[k] = idx
            self.readers[k] = []
        need = []
        for d, raw in deps:
            od = self.ops[d]
            if od["dma"]:
                need.append(("ch", od["chan"], od["chan"]["issued"]))
            else:
                if od["eng"] == o["eng"] and not o["dma"]:
                    if o["eng"] == "pe" or (not raw and not STRICT):
                        continue
                od["sig"] = True
                need.append(("op", d))
        o["need"] = need
        self.ops.append(o)
        return idx

    def op(self, eng, fn, reads=(), writes=()):
        o = dict(eng=eng, fn=fn, dma=False, sig=False)
        return self._add(o, reads, writes)

    def dma(self, eng, fn, reads=(), writes=(), chan=None):
        ch = self.chan(chan)
        o = dict(eng=eng, fn=fn, dma=True, sig=False, chan=ch)
        idx = self._add(o, reads, writes)
        ch["issued"] += 1
        return idx

    def alias(self, key, idx):
        self.last_w[key] = idx
        self.readers[key] = []

    def emit_pass(self):
        nc = self.nc
        ops = self.ops
        last = {}
        for i, o in enumerate(ops):
            if not o["dma"]:
                last[o["eng"]] = i
        for e, i in last.items():
            ops[i]["sig"] = True
        for o in ops:
            if not o["dma"] and o["sig"]:
                self.sigcnt[o["eng"]] += 1
                o["sigval"] = self.sigcnt[o["eng"]]
        per = {e: [o for o in ops if o["eng"] == e] for e in ENGS}
        bar_prev = self.bar_val
        self.sigcnt["dve"] += 1
        bar_new = self.sigcnt["dve"]
        sched = self

        def wait(e, eng, sem, key, val):
            w = sched.waited[e]
            if w.get(key, 0) >= val:
                return
            w[key] = val
            eng.wait_ge(sem, val)
            sched.n_inst += 1

        def run_engine(e, eng):
            if bar_prev > 0:
                wait(e, eng, sched.sem["dve"], "dve", bar_prev)
            for o in per[e]:
                for n in o["need"]:
                    if n[0] == "ch":
                        wait(e, eng, n[1]["sem"], "ch_" + n[1]["name"], 16 * n[2])
                    else:
                        od = ops[n[1]]
                        wait(e, eng, sched.sem[od["eng"]], od["eng"], od["sigval"])
                ins = o["fn"](eng)
                sched.n_inst += 1
                if o["dma"]:
                    ins.then_inc(o["chan"]["sem"], 16)
                elif o["sig"]:
                    ins.then_inc(sched.sem[e], 1)
            if e == "dve":
                for e2 in ("pe", "act", "pool"):
                    if sched.sigcnt[e2] > 0:
                        wait(e, eng, sched.sem[e2], e2, sched.sigcnt[e2])
                for ch in sched.chans.values():
                    if ch["issued"] > 0:
                        wait(e, eng, ch["sem"], "ch_" + ch["name"], 16 * ch["issued"])
                eng.memset(sched.bar_tile[0:1, 0:1], 0.0).then_inc(sched.sem["dve"], 1)
            else:
                pass

        with nc.Block() as block:
            @block.tensor
            def _(eng):
                run_engine("pe", eng)

            @block.scalar
            def _(eng):
                run_engine("act", eng)

            @block.vector
            def _(eng):
                run_engine("dve", eng)

            @block.gpsimd
            def _(eng):
                run_engine("pool", eng)

            @block.sync
            def _(eng):
                run_engine("sp", eng)

        self.bar_val = bar_new
        self.ops = []
        self.last_w = {}
        self.readers = {}

    def final_wait(self):
        nc = self.nc
        sched = self
        with nc.Block() as block:
            @block.sync
            def _(eng):
                eng.wait_ge(sched.sem["dve"], sched.bar_val)

            @block.tensor
            def _(eng):
                eng.wait_ge(sched.sem["dve"], sched.bar_val)

            @block.scalar
            def _(eng):
                eng.wait_ge(sched.sem["dve"], sched.bar_val)

            @block.gpsimd
            def _(eng):
                eng.wait_ge(sched.sem["dve"], sched.bar_val)


CB_IDENT, CB_B64, CB_R64, CB_ONES, CB_R96, CB_E, CB_GE, CB_LE, CB_GEF, CB_LEL, CB_GEB, CB_LEB, CB_GEAB, CB_LEAB = range(14)
NCB = 14


def build_program(T, debug=False, passes=None):
    nc = bass.Bass("TRN2", target_bir_lowering=False)
    G = T // 512
    NT = T // 128
    HB = T // 2
    SEG = 2048
    NSEG = T // SEG
    allp = ("a1", "a2", "a3", "b", "c1", "c2", "c3", "d")
    passes = allp if passes is None else passes

    def din(name, shape):
        return nc.dram_tensor(name, list(shape), F32, kind="ExternalInput").ap()

    def dscr(name, shape, dt):
        kind = "ExternalOutput" if (debug and name in debug) else "Internal"
        return nc.dram_tensor(name, list(shape), dt, kind=kind).ap()

    x_in = din("x", [T, D])
    cs64 = din("cs64", [2, 128, T])
    cs96 = din("cs96", [2, 96, T])
    consts = din("consts", [128, NCB * 128])
    percore = din("percore", [128, 4])
    e_norm = din("e_norm", [D])
    e_w_in = din("e_w_in", [D, 3072])
    e_a_q_gain = din("e_a_q_gain", [64])
    e_a_k_gain = din("e_a_k_gain", [64])
    e_a_sink = din("e_a_sink", [8])
    e_b_q_gain = din("e_b_q_gain", [3, 64])
    e_b_k_gain = din("e_b_k_gain", [3, 64])
    e_w_out = din("e_w_out", [768, D])
    o_norm = din("o_norm", [D])
    o_w_in = din("o_w_in", [D, 544])
    o_q_lora_gain = din("o_q_lora_gain", [256])
    o_w_uq = din("o_w_uq", [256, 1536])
    o_kv_gain = din("o_kv_gain", [256])
    o_w_ukv = din("o_w_ukv", [256, 2048])
    o_q_gain = din("o_q_gain", [96])
    o_k_gain = din("o_k_gain", [96])
    o_w_out = din("o_w_out", [D, D])
    f_norm = din("f_norm", [2, D])
    f_w_up = din("f_w_up", [2, D, 2 * DFF])
    f_conv_w = din("f_conv_w", [2, 3, DFF])
    f_conv_b = din("f_conv_b", [2, DFF])
    f_w_down = din("f_w_down", [2, DFF, D])

    y_out = nc.dram_tensor("y", [T, D], F32, kind="ExternalOutput").ap()

    TP = T + 2 * PAD
    qk0 = dscr("qk0", [NQK * 128, TP], BF16)
    v0 = dscr("v0", [TP, 896], BF16)
    cat0 = dscr("cat0", [768, T], BF16)
    x1 = dscr("x1", [T, D], F32)
    x2 = dscr("x2", [T, D], F32)
    q1 = dscr("q1", [16 * 96, T], BF16)
    k1 = dscr("k1", [16 * 96, T], BF16)
    v1 = dscr("v1", [T, 16 * 128], BF16)
    cat1 = dscr("cat1", [D, T], BF16)
    x3 = dscr("x3", [T, D], F32)

    es_glob = ExitStack()
    with es_glob:
        S = Sched(nc, es_glob)
        S.bar_tile = es_glob.enter_context(nc.sbuf_tensor("bar_tile", [1, 8], F32))
        cb = es_glob.enter_context(nc.sbuf_tensor("cb", [128, NCB, 128], BF16))
        pc = es_glob.enter_context(nc.sbuf_tensor("pc", [128, 4], F32))
        epsc = es_glob.enter_context(nc.sbuf_tensor("epsc", [128, 8], F32))
        ncd = nc.allow_non_contiguous_dma(reason="tiny parameter loads")
        es_glob.enter_context(ncd)

        def cbm(i, p=128, f=128):
            return cb[0:p, i, 0:f]

        uniq = [0]

        def sbt(es, name, shape, dt):
            uniq[0] += 1
            return es.enter_context(nc.sbuf_tensor("%s_%d" % (name, uniq[0]), list(shape), dt))

        def pst(es, name, dt=F32, cols=512):
            uniq[0] += 1
            return es.enter_context(nc.psum_tensor("%s_%d" % (name, uniq[0]), [128, cols], dt))

        with ExitStack() as es:
            zt = sbt(es, "zt", [128, 8 * 896], BF16)
            S.dma("pool", lambda e: e.dma_start(out=cb[:].rearrange("p a b -> p (a b)"), in_=consts[:, :]),
                  writes=["cb"], chan="cb")
            S.dma("sp", lambda e: e.dma_start(out=pc[:], in_=percore[:, :]), writes=["pc"], chan="pc")
            S.op("pool", lambda e: e.memset(zt[:], 0.0), writes=["zt"])
            for i, v in enumerate((D * EPS, 64 * EPS, 256 * EPS, 96 * EPS, 0.0)):
                S.op("pool", lambda e, i=i, v=v: e.memset(epsc[:, i:i + 1], float(v)), writes=[("epsc", i)])
            if "a1" in passes:
                for side in range(2):
                    c0 = 0 if side == 0 else PAD + T
                    for kc in range(NQK):
                        S.dma("sp", lambda e, kc=kc, c0=c0: e.dma_start(
                            out=qk0[kc * 128:(kc + 1) * 128, c0:c0 + PAD], in_=zt[:, 0:PAD]),
                            reads=["zt"], chan="zpad")
                    S.dma("sp", lambda e, c0=c0: e.dma_start(
                        out=v0[c0:c0 + PAD, :].rearrange("(p a) c -> p (a c)", p=128), in_=zt[:, 0:8 * 896]),
                        reads=["zt"], chan="zpad")
            S.emit_pass()

        def load_gain_cols(es, name, src_vec, n, chunks, mul, chan):
            t = sbt(es, name, [128, chunks], F32)
            S.dma("sp", lambda e: e.dma_start(out=t[:], in_=src_vec.rearrange("(c p) -> p c", p=128)),
                  writes=[name], chan=chan)
            if mul != 1.0:
                S.op("pool", lambda e: e.tensor_scalar(out=t[:], in0=t[:], scalar1=float(mul), scalar2=None, op0=ALU.mult),
                     reads=[name], writes=[name])
            return t

        def norm_transpose(es_unused, bufs, gi, src_dram, t0, ntok_tiles, gcol, xnT, xnT_key, col0, keep_x=None):
            pass

        def pass_a1():
            with ExitStack() as es:
                wqk = sbt(es, "wqk", [128, 8, NQK * 128], BF16)
                wv = sbt(es, "wv", [128, 8, 896], BF16)
                colmap = []
                for c in range(4):
                    colmap.append((c, 0, c * 128, 128))
                colmap += [(4, 0, 512, 64), (4, 64, 512, 64), (5, 0, 576, 64), (5, 64, 576, 64)]
                for g in range(3):
                    base = 768 + g * 768
                    for c in range(2):
                        colmap.append((6 + g * 4 + c, 0, base + c * 128, 128))
                        colmap.append((6 + g * 4 + 2 + c, 0, base + 256 + c * 128, 128))
                w_in_v = e_w_in.rearrange("(c p) n -> p c n", p=128)
                for (dc, do, sc, n) in colmap:
                    S.dma("pool", lambda e, dc=dc, do=do, sc=sc, n=n: e.dma_start(
                        out=wqk[:, :, dc * 128 + do: dc * 128 + do + n], in_=w_in_v[:, :, sc:sc + n]),
                        writes=[("wqk", dc, do)], chan="wqk")
                S.alias("wqk", len(S.ops) - 1)
                vmap = [(0, 640, 128)] + [(128 + g * 256, 768 + g * 768 + 512, 256) for g in range(3)]
                for (do, sc, n) in vmap:
                    S.dma("pool", lambda e, do=do, sc=sc, n=n: e.dma_start(
                        out=wv[:, :, do:do + n], in_=w_in_v[:, :, sc:sc + n]), writes=[("wv", do)], chan="wv")
                S.alias("wv", len(S.ops) - 1)
                gn = load_gain_cols(es, "gn", e_norm, D, 8, 32.0, "gn")
                gq = sbt(es, "gq", [128, NQK], F32)
                gsrc = [e_a_q_gain] * 4 + [e_a_k_gain] * 2
                for g in range(3):
                    gsrc += [e_b_q_gain[g], e_b_q_gain[g], e_b_k_gain[g], e_b_k_gain[g]]
                for c, gs in enumerate(gsrc):
                    for h in range(2):
                        S.dma("sp", lambda e, c=c, gs=gs, h=h: e.dma_start(
                            out=gq[h * 64:(h + 1) * 64, c:c + 1], in_=gs.rearrange("(p o) -> p o", o=1)),
                            writes=[("gq", c, h)], chan="gq")
                S.alias("gq", len(S.ops) - 1)
                xt = [sbt(es, "xt%d" % i, [128, 4, D], F32) for i in range(2)]
                xn = sbt(es, "xn", [128, 4, D], BF16)
                junk = [sbt(es, "junk%d" % i, [128, D], BF16) for i in range(4)]
                ssq = [sbt(es, "ssq%d" % i, [128, 4], F32) for i in range(2)]
                rst = [sbt(es, "rst%d" % i, [128, 4], F32) for i in range(2)]
                xnT = [sbt(es, "xnT%d" % i, [128, 8, 512], BF16) for i in range(2)]
                cst = [sbt(es, "cst%d" % i, [128, 2, 512], F32) for i in range(2)]
                sq = [sbt(es, "sq%d" % i, [128, 512], BF16) for i in range(2)]
                qg = [sbt(es, "qg%d" % i, [128, 512], BF16) for i in range(2)]
                sd = [sbt(es, "sd%d" % i, [128, 512], F32) for i in range(2)]
                t1 = [sbt(es, "t1%d" % i, [128, 512], F32) for i in range(2)]
                t2 = [sbt(es, "t2%d" % i, [128, 512], F32) for i in range(2)]
                t3 = [sbt(es, "t3%d" % i, [128, 512], F32) for i in range(2)]
                qf = [sbt(es, "qf%d" % i, [128, 512], BF16) for i in range(3)]
                vst = [sbt(es, "vst%d" % i, [128, 896], BF16) for i in range(2)]
                ptr = [pst(es, "ptr%d" % i, BF16, 1024) for i in range(2)]
                qps = [pst(es, "qps%d" % i) for i in range(2)]
                sps = pst(es, "sps")
                rps = pst(es, "rps")
                vps = [pst(es, "vps%d" % i) for i in range(2)]
                ci = 0
                vi = 0
                for g in range(G):
                    t0 = g * 512
                    b = g % 2
                    S.dma("sp", lambda e, b=b, t0=t0: e.dma_start(
                        out=xt[b][:], in_=x_in[t0:t0 + 512, :].rearrange("(j p) d -> p j d", p=128)),
                        writes=[("xt", b)], chan="xt%d" % b)
                    S.dma("sp", lambda e, b=b, t0=t0: e.dma_start(
                        out=cst[b][:], in_=cs64[:, :, t0:t0 + 512].rearrange("a p t -> p a t")),
                        writes=[("cst", b)], chan="cst%d" % b)
                    S.op("pool", lambda e, b=b: e.memset(ssq[b][:], 0.0), writes=[("ssq", b, j) for j in range(4)])
                    for j in range(4):
                        S.op("act", lambda e, b=b, j=j: e.activation(
                            out=junk[j][:], in_=xt[b][:, j, :], func=AF.Square, accum_out=ssq[b][:, j:j + 1]),
                            reads=[("xt", b), ("ssq", b, j)], writes=[("ssq", b, j), ("junk", j)])
                    S.op("act", lambda e, b=b: e.activation(
                        out=rst[b][:], in_=ssq[b][:], func=AF.Sqrt, bias=epsc[:, 0:1], scale=1.0),
                        reads=[("ssq", b, j) for j in range(4)] + [("epsc", 0)], writes=[("rst", b)])
                    S.op("dve", lambda e, b=b: e.reciprocal(out=rst[b][:], in_=rst[b][:]),
                         reads=[("rst", b)], writes=[("rst", b)])
                    for j in range(4):
                        S.op("dve" if j % 2 else "pool", lambda e, b=b, j=j: e.tensor_scalar(
                            out=xn[:, j, :], in0=xt[b][:, j, :], scalar1=rst[b][:, j:j + 1], scalar2=None, op0=ALU.mult),
                            reads=[("xt", b), ("rst", b)], writes=[("xn", j)])
                    for c in range(8):
                        pb = c % 2
                        for j in range(4):
                            S.op("pe", lambda e, pb=pb, c=c, j=j: e.transpose(
                                ptr[pb][:, j * 128:(j + 1) * 128], xn[:, j, c * 128:(c + 1) * 128], cbm(CB_IDENT)),
                                reads=[("xn", j), "cb"], writes=[("ptr", pb)])
                        S.op("act" if c % 2 else "dve", (lambda e, pb=pb, c=c, b=b: e.activation(
                            out=xnT[b][:, c, :], in_=ptr[pb][:, 0:512], func=AF.Copy, scale=gn[:, c:c + 1])) if c % 2 else
                            (lambda e, pb=pb, c=c, b=b: e.tensor_scalar(
                                out=xnT[b][:, c, :], in0=ptr[pb][:, 0:512], scalar1=gn[:, c:c + 1], scalar2=None, op0=ALU.mult)),
                            reads=[("ptr", pb), "gn"], writes=[("xnT", b, c)])
                    xk = [("xnT", b, c) for c in range(8)]
                    pend = []
                    for fc in range(NQK):
                        qb = ci % 2
                        ci += 1
                        for c in range(8):
                            S.op("pe", lambda e, qb=qb, fc=fc, c=c, b=b: e.matmul(
                                qps[qb][:], lhsT=wqk[:, c, fc * 128:(fc + 1) * 128], rhs=xnT[b][:, c, :],
                                start=(c == 0), stop=(c == 7)),
                                reads=[("xnT", b, c), "wqk"], writes=[("qps", qb)])
                        def post(qb=qb, fc=fc, b=b, g=g, t0=t0):
                            S.op("act", lambda e, qb=qb: e.activation(out=sq[qb][:], in_=qps[qb][:], func=AF.Square),
                                 reads=[("qps", qb)], writes=[("sq", qb)])
                            S.op("act", lambda e, qb=qb, fc=fc: e.activation(
                                out=qg[qb][:], in_=qps[qb][:], func=AF.Copy, scale=gq[:, fc:fc + 1]),
                                reads=[("qps", qb), "gq"], writes=[("qg", qb)])
                            S.op("pe", lambda e, qb=qb: e.matmul(sps[:], lhsT=cbm(CB_B64), rhs=sq[qb][:], start=True, stop=True),
                                 reads=[("sq", qb), "cb"], writes=["sps"])
                            S.op("pe", lambda e, qb=qb: e.matmul(rps[:], lhsT=cbm(CB_R64), rhs=qg[qb][:], start=True, stop=True),
                                 reads=[("qg", qb), "cb"], writes=["rps"])
                            S.op("act", lambda e, qb=qb: e.activation(
                                out=sd[qb][:], in_=sps[:], func=AF.Ln, bias=epsc[:, 1:2], scale=1.0),
                                reads=["sps", ("epsc", 1)], writes=[("sd", qb)])
                            S.op(ENG_T1, lambda e, qb=qb, b=b: e.tensor_tensor(
                                out=t1[qb][:], in0=qg[qb][:], in1=cst[b][:, 0, :], op=ALU.mult),
                                reads=[("qg", qb), ("cst", b)], writes=[("t1", qb)])
                            S.op("dve", lambda e, qb=qb, b=b: e.tensor_tensor(
                                out=t2[qb][:], in0=rps[:], in1=cst[b][:, 1, :], op=ALU.mult),
                                reads=["rps", ("cst", b)], writes=[("t2", qb)])
                            S.op(ENG_ADD, lambda e, qb=qb: e.tensor_tensor(
                                out=t3[qb][:], in0=t1[qb][:], in1=t2[qb][:], op=ALU.add),
                                reads=[("t1", qb), ("t2", qb)], writes=[("t3", qb)])
                            fb = (g * NQK + fc) % 3
                            S.op("act", lambda e, qb=qb: e.activation(out=sd[qb][:], in_=sd[qb][:], func=AF.Exp, scale=-0.5),
                                 reads=[("sd", qb)], writes=[("sd", qb)])
                            S.op("dve", lambda e, qb=qb, fb=fb: e.tensor_tensor(
                                out=qf[fb][:], in0=t3[qb][:], in1=sd[qb][:], op=ALU.mult),
                                reads=[("t3", qb), ("sd", qb)], writes=[("qf", fb)])
                            S.dma("sp", lambda e, fb=fb, fc=fc, t0=t0: e.dma_start(
                                out=qk0[fc * 128:(fc + 1) * 128, PAD + t0:PAD + t0 + 512], in_=qf[fb][:]),
                                reads=[("qf", fb)], chan="qf%d" % fb)
                        if pend:
                            pend.pop()()
                        pend.append(post)
                    pend.pop()()
                    for j in range(4):
                        vb = vi % 2
                        vi += 1
                        for (c0, c1, pi) in ((0, 512, 0), (512, 896, 1)):
                            for c in range(8):
                                S.op("pe", lambda e, pi=pi, c0=c0, c1=c1, c=c, b=b, j=j: e.matmul(
                                    vps[pi][:, 0:c1 - c0], lhsT=xnT[b][:, c, j * 128:(j + 1) * 128], rhs=wv[:, c, c0:c1],
                                    start=(c == 0), stop=(c == 7)),
                                    reads=[("xnT", b, c), "wv"], writes=[("vps", pi)])
                            S.op("act" if pi else "dve", (lambda e, pi=pi, c0=c0, c1=c1, vb=vb: e.activation(
                                out=vst[vb][:, c0:c1], in_=vps[pi][:, 0:c1 - c0], func=AF.Copy)) if pi else
                                (lambda e, pi=pi, c0=c0, c1=c1, vb=vb: e.tensor_copy(out=vst[vb][:, c0:c1], in_=vps[pi][:, 0:c1 - c0])),
                                reads=[("vps", pi)], writes=[("vst", vb, pi)])
                        S.dma("sp", lambda e, vb=vb, t0=t0, j=j: e.dma_start(
                            out=v0[PAD + t0 + j * 128:PAD + t0 + (j + 1) * 128, :], in_=vst[vb][:]),
                            reads=[("vst", vb, 0), ("vst", vb, 1)], chan="vst%d" % vb)
                S.emit_pass()

        def pass_a2():
            with ExitStack() as es:
                qTA = sbt(es, "qTA", [128, 4, SEG], BF16)
                kTA = sbt(es, "kTA", [128, 2, SEG + 2 * PAD], BF16)
                vtA = sbt(es, "vtA", [128, 18, 2, 128], BF16)
                qTB = sbt(es, "qTB", [128, 2, SEG], BF16)
                kTB = sbt(es, "kTB", [128, 2, SEG + 2 * PAD], BF16)
                vtB = sbt(es, "vtB", [128, 32, 4, 128], BF16)
                acc = sbt(es, "acc", [128, 4, SEG], F32)
                recB = sbt(es, "recB", [128, SEG], F32)
                ostA = sbt(es, "ostA", [128, 4, SEG], BF16)
                ostB = sbt(es, "ostB", [128, 2, SEG], BF16)
                pt = [sbt(es, "pt%d" % i, [128, 512], BF16) for i in range(3)]
                dn = [sbt(es, "dn%d" % i, [128, 512], F32) for i in range(2)]
                snk = sbt(es, "snk", [128, 8], F32)
                spsX = [pst(es, "s2psx%d" % i) for i in range(2)]
                spsY = [pst(es, "s2psy%d" % i) for i in range(2)]
                ops_ = [pst(es, "o2ps%d" % i) for i in range(2)]
                S.op("pool", lambda e: e.memset(vtA[:, :, :, 64:128], 1.0), writes=["vtA1"])
                S.op("pool", lambda e: e.memset(vtB[:, :, :, 64:128], 1.0), writes=["vtB1"])
                S.dma("sp", lambda e: e.dma_start(out=snk[:], in_=e_a_sink.partition_broadcast(128)), writes=["snk"], chan="snk")
                S.op("act", lambda e: e.activation(out=snk[:], in_=snk[:], func=AF.Exp), reads=["snk"], writes=["snk"])
                cnt = dict(p=0, s=0, o=0, d=0)
                pend_pv = []

                def flush_pv():
                    while pend_pv:
                        pend_pv.pop(0)()

                def attend(items, mask, vt_single, o_t, first, last):
                    order = [i for i in range(len(items)) if items[i][0] == 0] + [i for i in range(len(items)) if items[i][0] != 0]
                    nx = sum(1 for it in items if it[0] == 0)
                    n = len(items)
                    si = cnt["s"] % 2
                    cnt["s"] += 1
                    pi = cnt["p"] % 3
                    cnt["p"] += 1
                    for col, idx in enumerate(order):
                        po, l, r, rk, _ = items[idx]
                        bank = spsX[si] if po == 0 else spsY[si]
                        bkey = ("spsX", si) if po == 0 else ("spsY", si)
                        cc = col if po == 0 else col - nx
                        S.op("pe", lambda e, l=l, r=r, cc=cc, bank=bank: e.matmul(
                            bank[:, cc * 128:(cc + 1) * 128], lhsT=l, rhs=r, start=True, stop=True),
                            reads=rk, writes=[bkey])
                    flush_pv()
                    if nx > 0:
                        S.op("act", lambda e, si=si, pi=pi, nx=nx: e.activation(
                            out=pt[pi][:, 0:nx * 128], in_=spsX[si][:, 0:nx * 128], func=AF.Exp, scale=8.0),
                            reads=[("spsX", si)], writes=[("pt", pi, 0)])
                    if n - nx > 0:
                        S.op("act", lambda e, si=si, pi=pi, nx=nx, n=n: e.activation(
                            out=pt[pi][:, nx * 128:n * 128], in_=spsY[si][:, 0:(n - nx) * 128], func=AF.Exp, scale=8.0),
                            reads=[("spsY", si)], writes=[("pt", pi, 1)])
                    pk = [("pt", pi, 0), ("pt", pi, 1)]
                    if mask is not None:
                        S.op(ENG_MASK, lambda e, pi=pi, n=n, mask=mask: e.tensor_tensor(
                            out=pt[pi][:, 0:n * 128].rearrange("p (a b) -> p a b", a=n),
                            in0=pt[pi][:, 0:n * 128].rearrange("p (a b) -> p a b", a=n),
                            in1=cbm(mask).unsqueeze(1).broadcast_to([128, n, 128]), op=ALU.mult),
                            reads=pk + ["cb"], writes=pk)
                    def emit_pv(vt_single=vt_single, items=items, order=order, pi=pi, o_t=o_t, n=n, first=first, last=last, pk=pk):
                        if vt_single is not None:
                            vl, vk = vt_single
                            S.op("pe", lambda e, vl=vl, pi=pi, o_t=o_t, n=n: e.matmul(
                                ops_[o_t][:, 0:n * 128], lhsT=vl, rhs=pt[pi][:, 0:n * 128], start=first, stop=last),
                                reads=pk + vk, writes=[("ops", o_t)])
                        else:
                            for col, idx in enumerate(order):
                                vl, vk = items[idx][4]
                                st_ = first and col == 0
                                sp_ = last and col == n - 1
                                S.op("pe", lambda e, vl=vl, col=col, pi=pi, o_t=o_t, st_=st_, sp_=sp_: e.matmul(
                                    ops_[o_t][:, col * 128:(col + 1) * 128], lhsT=vl, rhs=pt[pi][:, col * 128:(col + 1) * 128],
                                    start=st_, stop=sp_),
                                    reads=pk + vk, writes=[("ops", o_t)])
                    pend_pv.append(emit_pv)
                    return order

                for sg in range(NSEG):
                    tseg = sg * SEG
                    S.dma("sp", lambda e, tseg=tseg: e.dma_start(
                        out=qTA[:], in_=qk0[0:512, PAD + tseg:PAD + tseg + SEG].rearrange("(c p) t -> p c t", p=128)),
                        writes=["qTA"], chan="qTA")
                    S.dma("sp", lambda e, tseg=tseg: e.dma_start(
                        out=kTA[:], in_=qk0[512:768, tseg:tseg + SEG + 2 * PAD].rearrange("(c p) t -> p c t", p=128)),
                        writes=["kTA"], chan="kTA")
                    for h in range(2):
                        S.dma("sp", lambda e, tseg=tseg, h=h: e.dma_start(
                            out=vtA[:, :, h, 0:64],
                            in_=v0[PAD + tseg - 128:PAD + tseg - 128 + 18 * 128, h * 64:(h + 1) * 64].rearrange("(m j) d -> j m d", j=128)),
                            writes=["vtA"], chan="vtA")
                    for nn in range(SEG // 128 if not os.environ.get("SKIP_A") else 0):
                        tb = tseg + nn * 128
                        tiles = []
                        if tb > 0:
                            tiles.append((-1, CB_GEAB if tb == HB else CB_GE))
                        tiles.append((0, None))
                        if tb + 128 < T:
                            tiles.append((1, CB_LEAB if tb + 128 == HB else CB_LE))
                        for hkv in range(2):
                            ot = cnt["o"] % 2
                            cnt["o"] += 1
                            for ti, (off, mask) in enumerate(tiles):
                                kc0 = PAD + 128 * (nn + off)
                                items = []
                                for g in range(4):
                                    hq = 4 * hkv + g
                                    po = (hq % 2) * 64
                                    items.append((po, kTA[po:po + 64, hkv, kc0:kc0 + 128],
                                                  qTA[po:po + 64, hq // 2, nn * 128:(nn + 1) * 128], ["kTA", "qTA"], None))
                                order = attend(items, mask, (vtA[:, nn + 1 + off, hkv, :], ["vtA", "vtA1"]),
                                               ot, ti == 0, ti == len(tiles) - 1)
                            flush_pv()
                            di = cnt["d"] % 2
                            cnt["d"] += 1
                            for col, g in enumerate(order):
                                hq = 4 * hkv + g
                                S.op("dve", lambda e, di=di, ot=ot, col=col, hq=hq: e.tensor_scalar(
                                    out=dn[di][64:128, col * 128:(col + 1) * 128], in0=ops_[ot][64:128, col * 128:(col + 1) * 128],
                                    scalar1=snk[64:128, hq:hq + 1], scalar2=None, op0=ALU.add),
                                    reads=[("ops", ot), "snk"], writes=[("dn", di, col)])
                            S.op("dve", lambda e, di=di: e.reciprocal(out=dn[di][64:128, :], in_=dn[di][64:128, :]),
                                 reads=[("dn", di, g) for g in range(4)], writes=[("dn", di, g) for g in range(4)])
                            for col, g in enumerate(order):
                                hq = 4 * hkv + g
                                po = (hq % 2) * 64
                                S.op("dve", lambda e, di=di, ot=ot, col=col, hq=hq, po=po, nn=nn: e.tensor_tensor(
                                    out=ostA[po:po + 64, hq // 2, nn * 128:(nn + 1) * 128],
                                    in0=ops_[ot][0:64, col * 128:(col + 1) * 128], in1=dn[di][64:128, col * 128:(col + 1) * 128], op=ALU.mult),
                                    reads=[("ops", ot), ("dn", di, col)], writes=[("ostA", hq, nn)])
                    S.dma("sp", lambda e, tseg=tseg: e.dma_start(
                        out=cat0[0:512, tseg:tseg + SEG].rearrange("(c p) t -> p c t", p=128), in_=ostA[:]),
                        reads=[("ostA", hq, nn) for hq in range(8) for nn in range(SEG // 128)], chan="ostA")
                    for g, r in enumerate((1, 4, 16)):
                        if os.environ.get("SKIP_B") and str(g) in os.environ.get("SKIP_B"):
                            continue
                        nblk = SEG // (128 * r)
                        Fblocks = T // (128 * r)
                        nbnd = HB // (128 * r)
                        qrow = (6 + 4 * g) * 128
                        S.dma("sp", lambda e, tseg=tseg, qrow=qrow: e.dma_start(
                            out=qTB[:], in_=qk0[qrow:qrow + 256, PAD + tseg:PAD + tseg + SEG].rearrange("(c p) t -> p c t", p=128)),
                            writes=["qTB"], chan="qTB")
                        S.dma("sp", lambda e, tseg=tseg, qrow=qrow: e.dma_start(
                            out=kTB[:], in_=qk0[qrow + 256:qrow + 512, tseg:tseg + SEG + 2 * PAD].rearrange("(c p) t -> p c t", p=128)),
                            writes=["kTB"], chan="kTB")
                        vcol = 128 + 256 * g
                        for c in range(r):
                            base = PAD + tseg - 64 * r + c
                            nrow = (nblk + 1) * 128
                            for h in range(4):
                                S.dma("sp", lambda e, base=base, nrow=nrow, r=r, c=c, nblk=nblk, vcol=vcol, h=h: e.dma_start(
                                    out=vtB[:, c * (nblk + 1):(c + 1) * (nblk + 1), h, 0:64],
                                    in_=v0[base:base + (nrow - 1) * r + 1:r, vcol + h * 64:vcol + (h + 1) * 64].rearrange("(m j) d -> j m d", j=128)),
                                    writes=["vtB"], chan="vtB")
                        for nn in range(nblk):
                            n = nblk * sg + nn
                            for c in range(r):
                                ot = cnt["o"] % 2
                                cnt["o"] += 1
                                for which in range(2):
                                    mm = nn + which
                                    if which == 0:
                                        mask = CB_GEF if n == 0 else (CB_GEB if n == nbnd else CB_GE)
                                    else:
                                        mask = CB_LEL if n + 1 == Fblocks else (CB_LEB if n + 1 == nbnd else CB_LE)
                                    k0 = PAD - 64 * r + c + 128 * r * mm
                                    q0 = 128 * r * nn + c
                                    items = []
                                    for h in range(4):
                                        po = (h % 2) * 64
                                        items.append((po, kTB[po:po + 64, h // 2, k0:k0 + 127 * r + 1:r],
                                                      qTB[po:po + 64, h // 2, q0:q0 + 127 * r + 1:r], ["kTB", "qTB"],
                                                      (vtB[:, c * (nblk + 1) + mm, h, :], ["vtB", "vtB1"])))
                                    order = attend(items, mask, None, ot, which == 0, which == 1)
                                assert order == [0, 2, 1, 3]
                                flush_pv()
                                for half in range(2):
                                    av = acc[:, half:4:2, 128 * r * nn:128 * r * (nn + 1)].rearrange("p h (i r) -> p h r i", r=r)[:, :, c, :]
                                    ov = ops_[ot][:, half * 256:(half + 1) * 256].rearrange("p (h i) -> p h i", h=2)
                                    if g == 0:
                                        S.op("dve", lambda e, av=av, ov=ov: e.tensor_copy(out=av, in_=ov),
                                             reads=[("ops", ot)], writes=["acc"])
                                    else:
                                        S.op("dve", lambda e, av=av, ov=ov: e.tensor_tensor(out=av, in0=ov, in1=av, op=ALU.add),
                                             reads=[("ops", ot), "acc"], writes=["acc"])
                    for h in range(4):
                        po = (h % 2) * 64
                        S.op("dve", lambda e, h=h: e.reciprocal(out=recB[0:64, :], in_=acc[64:128, h, :]),
                             reads=["acc"], writes=["recB"])
                        S.op("pool", lambda e, h=h, po=po: e.tensor_tensor(
                            out=ostB[po:po + 64, h // 2, :], in0=acc[0:64, h, :], in1=recB[0:64, :], op=ALU.mult),
                            reads=["acc", "recB"], writes=[("ostB", h)])
                    S.dma("sp", lambda e, tseg=tseg: e.dma_start(
                        out=cat0[512:768, tseg:tseg + SEG].rearrange("(c p) t -> p c t", p=128), in_=ostB[:]),
                        reads=[("ostB", h) for h in range(4)], chan="ostB")
                S.emit_pass()

        def pass_outproj(tag, xin, catT, KC, w_dram, xout):
            with ExitStack() as es:
                wo = sbt(es, "wo", [128, KC, D], BF16)
                S.dma("pool", lambda e: e.dma_start(out=wo[:], in_=w_dram.rearrange("(c p) n -> p c n", p=128)),
                      writes=["wo"], chan="wo")
                ct = [sbt(es, "ct%d" % i, [128, KC, 512], BF16) for i in range(2)]
                xt = [sbt(es, "oxt%d" % i, [128, 4, D], F32) for i in range(2)]
                yps = [pst(es, "yps%d" % i) for i in range(4)]
                k = 0
                for g in range(G):
                    t0 = g * 512
                    b = g % 2
                    S.dma("sp", lambda e, b=b, t0=t0: e.dma_start(
                        out=ct[b][:], in_=catT[:, t0:t0 + 512].rearrange("(c p) t -> p c t", p=128)),
                        writes=[("ct", b)], chan="ct%d" % b)
                    S.dma("sp", lambda e, b=b, t0=t0: e.dma_start(
                        out=xt[b][:], in_=xin[t0:t0 + 512, :].rearrange("(j p) d -> p j d", p=128)),
                        writes=[("oxt", b)], chan="oxt%d" % b)
                    for j in range(4):
                        for half in range(2):
                            yb = k % 4
                            k += 1
                            for kc in range(KC):
                                S.op("pe", lambda e, yb=yb, b=b, kc=kc, j=j, half=half: e.matmul(
                                    yps[yb][:], lhsT=ct[b][:, kc, j * 128:(j + 1) * 128], rhs=wo[:, kc, half * 512:(half + 1) * 512],
                                    start=(kc == 0), stop=(kc == KC - 1)),
                                    reads=[("ct", b), "wo"], writes=[("yps", yb)])
                            S.op("dve", lambda e, yb=yb, b=b, j=j, half=half: e.tensor_tensor(
                                out=xt[b][:, j, half * 512:(half + 1) * 512], in0=yps[yb][:], in1=xt[b][:, j, half * 512:(half + 1) * 512], op=ALU.add),
                                reads=[("yps", yb), ("oxt", b)], writes=[("oxt", b)])
                    S.dma("sp", lambda e, b=b, t0=t0: e.dma_start(
                        out=xout[t0:t0 + 512, :].rearrange("(j p) d -> p j d", p=128), in_=xt[b][:]),
                        reads=[("oxt", b)], chan="oxt%d" % b)
                S.emit_pass()

        def pass_ffn(l, xin, xout):
            with ExitStack() as es:
                wup = sbt(es, "wup", [128, 8, 2 * DFF], BF16)
                wdn = sbt(es, "wdn", [128, NFC, D], BF16)
                upv = f_w_up[l].rearrange("(c p) n -> p c n", p=128)
                for q in range(8):
                    S.dma("pool", lambda e, q=q: e.dma_start(out=wup[:, :, q * 704:(q + 1) * 704], in_=upv[:, :, q * 704:(q + 1) * 704]),
                          writes=[("wup", q)], chan="wup")
                S.alias("wup", len(S.ops) - 1)
                dnv = f_w_down[l].rearrange("(c p) n -> p c n", p=128)
                for q in range(2):
                    S.dma("pool", lambda e, q=q: e.dma_start(out=wdn[:, q * 11:(q + 1) * 11, :], in_=dnv[:, q * 11:(q + 1) * 11, :]),
                          writes=[("wdn", q)], chan="wdn")
                S.alias("wdn", len(S.ops) - 1)
                gn = load_gain_cols(es, "fgn", f_norm[l], D, 8, 32.0, "fgn")
                cw = sbt(es, "cw", [128, 3, NFC], F32)
                for jj in range(3):
                    S.dma("sp", lambda e, jj=jj: e.dma_start(out=cw[:, jj, :], in_=f_conv_w[l, jj].rearrange("(c p) -> p c", p=128)),
                          writes=[("cw", jj)], chan="cw")
                S.alias("cw", len(S.ops) - 1)
                cbi = sbt(es, "cbi", [128, NFC], F32)
                S.dma("sp", lambda e: e.dma_start(out=cbi[:], in_=f_conv_b[l].rearrange("(c p) -> p c", p=128)),
                      writes=["cbi"], chan="cbi")
                xt = sbt(es, "fxt", [128, 4, D], F32)
                xh = sbt(es, "fxh", [2, D], F32)
                xn = [sbt(es, "fxn%d" % i, [128, D], BF16) for i in range(2)]
                xhn = sbt(es, "fxhn", [2, D], BF16)
                ssq = sbt(es, "fssq", [128, 8], F32)
                rst = sbt(es, "frst", [128, 8], F32)
                xnT = sbt(es, "fxnT", [128, 8, 512], BF16)
                xnTh = sbt(es, "fxnTh", [128, 8, 2], BF16)
                gsb = [sbt(es, "gsb%d" % i, [128, 514], F32) for i in range(3)]
                aa = [sbt(es, "aa%d" % i, [128, 512], F32) for i in range(3)]
                hT = sbt(es, "hT", [128, NFC, 512], BF16)
                junk = [hT[:, 0:2, :].rearrange("p a b -> p (a b)"), hT[:, 2:4, :].rearrange("p a b -> p (a b)")]
                gps = [pst(es, "gps%d" % i) for i in range(2)]
                vps = [pst(es, "fvps%d" % i) for i in range(3)]
                hpsl = [pst(es, "hps%d" % i) for i in range(1)]
                yps = [pst(es, "fyps%d" % i) for i in range(2)]
                ptr = yps[1][:, :].bitcast(BF16)
                k = 0
                fi = 0
                for g in range(G):
                    t0 = g * 512
                    has_l = (t0 > 0)
                    has_r = (t0 + 512 < T)
                    S.dma("sp", lambda e, t0=t0: e.dma_start(
                        out=xt[:], in_=xin[t0:t0 + 512, :].rearrange("(j p) d -> p j d", p=128)),
                        writes=["fxt"], chan="fxt")
                    S.op("pool", lambda e: e.memset(ssq[:], 0.0), writes=["fssq"])
                    if has_l or has_r:
                        S.op("pool", lambda e: e.memset(xh[:], 1.0), writes=["fxh"])
                        if has_l:
                            S.dma("sp", lambda e, t0=t0: e.dma_start(out=xh[0:1, :], in_=xin[t0 - 1:t0, :]), reads=["fxh"], writes=["fxh"], chan="fxh")
                        if has_r:
                            S.dma("sp", lambda e, t0=t0: e.dma_start(out=xh[1:2, :], in_=xin[t0 + 512:t0 + 513, :]), reads=["fxh"], writes=["fxh"], chan="fxh")
                        S.op("act", lambda e: e.activation(out=junk[0][0:2, :], in_=xh[:], func=AF.Square, accum_out=ssq[0:2, 4:5]),
                             reads=["fxh", "fssq"], writes=["fssq", ("hT", 0), ("hT", 1)])
                    for j in range(4):
                        S.op("act", lambda e, j=j: e.activation(
                            out=junk[j % 2], in_=xt[:, j, :], func=AF.Square, accum_out=ssq[:, j:j + 1]),
                            reads=["fxt", "fssq"], writes=["fssq", ("hT", 2 * (j % 2)), ("hT", 2 * (j % 2) + 1)])
                    S.op("act", lambda e: e.activation(out=rst[:], in_=ssq[:], func=AF.Sqrt, bias=epsc[:, 0:1], scale=1.0),
                         reads=["fssq", ("epsc", 0)], writes=["frst"])
                    S.op("dve", lambda e: e.reciprocal(out=rst[:], in_=rst[:]), reads=["frst"], writes=["frst"])
                    if has_l or has_r:
                        S.op("dve", lambda e: e.tensor_scalar(out=xhn[:], in0=xh[:], scalar1=rst[0:2, 4:5], scalar2=None, op0=ALU.mult),
                             reads=["fxh", "frst"], writes=["fxhn"])
                    for j in range(4):
                        nb_ = j % 2
                        S.op("dve" if j % 2 else "act", (lambda e, j=j, nb_=nb_: e.tensor_scalar(
                            out=xn[nb_][:], in0=xt[:, j, :], scalar1=rst[:, j:j + 1], scalar2=None, op0=ALU.mult)) if j % 2 else
                            (lambda e, j=j, nb_=nb_: e.activation(out=xn[nb_][:], in_=xt[:, j, :], func=AF.Copy, scale=rst[:, j:j + 1])),
                            reads=["fxt", "frst"], writes=[("fxn", nb_)])
                        for c in range(8):
                            S.op("pe", lambda e, c=c, j=j, nb_=nb_: e.transpose(
                                ptr[:, c * 128:(c + 1) * 128], xn[nb_][:, c * 128:(c + 1) * 128], cbm(CB_IDENT)),
                                reads=[("fxn", nb_), "cb"], writes=[("fyps", 1)])
                        S.op("dve", lambda e, j=j: e.tensor_tensor(
                            out=xnT[:, :, j * 128:(j + 1) * 128], in0=ptr[:, :].rearrange("p (c t) -> p c t", c=8),
                            in1=gn[:, :].unsqueeze(2).broadcast_to([128, 8, 128]), op=ALU.mult),
                            reads=[("fyps", 1), "fgn"], writes=[("fxnT", j)])
                    if has_l or has_r:
                        for c in range(8):
                            S.op("pe", lambda e, c=c: e.transpose(
                                ptr[:, c * 128:c * 128 + 2], xhn[0:2, c * 128:(c + 1) * 128], cbm(CB_IDENT, 2, 2)),
                                reads=["fxhn", "cb"], writes=[("fyps", 1)])
                        S.op("dve", lambda e: e.tensor_tensor(
                            out=xnTh[:], in0=ptr[:, :].rearrange("p (c t) -> p c t", c=8)[:, :, 0:2],
                            in1=gn[:, :].unsqueeze(2).broadcast_to([128, 8, 2]), op=ALU.mult),
                             reads=[("fyps", 1), "fgn"], writes=["fxnTh"])
                    xk = [("fxnT", j) for j in range(4)]
                    pend_tail = []
                    for fc in range(NFC):
                        gb = fi % 2
                        v3 = fi % 3
                        fi += 1
                        for (dst, col0, key) in ((gps[gb], fc * 128, ("gps", gb)), (vps[v3], DFF + fc * 128, ("fvps", v3))):
                            for c in range(8):
                                S.op("pe", lambda e, dst=dst, col0=col0, c=c: e.matmul(
                                    dst[:], lhsT=wup[:, c, col0:col0 + 128], rhs=xnT[:, c, :],
                                    start=(c == 0), stop=(c == 7)),
                                    reads=xk + ["wup"], writes=[key])
                            if dst is gps[gb] and (has_l or has_r):
                                for c in range(8):
                                    S.op("pe", lambda e, col0=col0, c=c, fc=fc: e.matmul(
                                        hpsl[0][:, 2 * fc:2 * fc + 2], lhsT=wup[:, c, col0:col0 + 128], rhs=xnTh[:, c, :],
                                        start=(c == 0), stop=(c == 7)),
                                        reads=["fxnTh", "wup"], writes=[("hps", 0)])
                        S.op("act", lambda e, gb=gb, v3=v3: e.activation(out=gsb[v3][:, 1:513], in_=gps[gb][:], func=AF.Copy),
                             reads=[("gps", gb)], writes=[("gsb", v3, 1)])
                        for side, has in ((0, has_l), (1, has_r)):
                            colo = 0 if side == 0 else 513
                            if not has:
                                S.op("pool", lambda e, gb=v3, colo=colo: e.memset(gsb[gb][:, colo:colo + 1], 0.0),
                                     writes=[("gsb", v3, 0 if side == 0 else 2)])
                            else:
                                tpos = t0 if side == 0 else t0 + 512
                                sc = pc[:, 0:1] if tpos == HB else 1.0
                                S.op("dve", lambda e, gb=v3, colo=colo, fc=fc, side=side, sc=sc: e.tensor_scalar(
                                    out=gsb[gb][:, colo:colo + 1], in0=hpsl[0][:, 2 * fc + side:2 * fc + side + 1], scalar1=sc, scalar2=None, op0=ALU.mult),
                                    reads=[("hps", 0), "pc"], writes=[("gsb", v3, 0 if side == 0 else 2)])
                        gk = [("gsb", v3, i) for i in range(3)]
                        S.op(ENG_FFN_CONV, lambda e, gb=v3, fc=fc: e.tensor_scalar(
                            out=aa[gb][:], in0=gsb[gb][:, 1:513], scalar1=cw[:, 1, fc:fc + 1], scalar2=None, op0=ALU.mult),
                            reads=gk + ["cw"], writes=[("aa", v3)])
                        S.op("dve", lambda e, gb=v3, fc=fc: e.scalar_tensor_tensor(
                            out=aa[gb][:], in0=gsb[gb][:, 0:512], scalar=cw[:, 0, fc:fc + 1], in1=aa[gb][:], op0=ALU.mult, op1=ALU.add),
                            reads=gk + ["cw", ("aa", v3)], writes=[("aa", v3)])
                        S.op("dve", lambda e, gb=v3, fc=fc: e.scalar_tensor_tensor(
                            out=aa[gb][:], in0=gsb[gb][:, 2:514], scalar=cw[:, 2, fc:fc + 1], in1=aa[gb][:], op0=ALU.mult, op1=ALU.add),
                            reads=gk + ["cw", ("aa", v3)], writes=[("aa", v3)])
                        def tail(v3=v3, fc=fc):
                            S.op("act", lambda e, gb=v3, fc=fc: e.activation(
                                out=aa[gb][:], in_=aa[gb][:], func=AF.Gelu_apprx_tanh, bias=cbi[:, fc:fc + 1], scale=1.0),
                                reads=[("aa", v3), "cbi"], writes=[("aa", v3)])
                            S.op("dve", lambda e, gb=v3, fc=fc: e.tensor_tensor(
                                out=hT[:, fc, :], in0=vps[gb][:], in1=aa[gb][:], op=ALU.mult),
                                reads=[("fvps", v3), ("aa", v3)], writes=[("hT", fc)])
                        if pend_tail:
                            pend_tail.pop()()
                        pend_tail.append(tail)
                    pend_tail.pop()()
                    hk = [("hT", fc) for fc in range(NFC)]
                    for j in range(4):
                        for half in range(2):
                            yb = k % 2
                            k += 1
                            for fc in range(NFC):
                                S.op("pe", lambda e, yb=yb, fc=fc, j=j, half=half: e.matmul(
                                    yps[yb][:], lhsT=hT[:, fc, j * 128:(j + 1) * 128], rhs=wdn[:, fc, half * 512:(half + 1) * 512],
                                    start=(fc == 0), stop=(fc == NFC - 1)),
                                    reads=[("hT", fc), "wdn"], writes=[("fyps", yb)])
                            S.op("dve", lambda e, yb=yb, j=j, half=half: e.tensor_tensor(
                                out=xt[:, j, half * 512:(half + 1) * 512], in0=yps[yb][:], in1=xt[:, j, half * 512:(half + 1) * 512], op=ALU.add),
                                reads=[("fyps", yb), "fxt"], writes=["fxt"])
                    S.dma("sp", lambda e, t0=t0: e.dma_start(
                        out=xout[t0:t0 + 512, :].rearrange("(j p) d -> p j d", p=128), in_=xt[:]),
                        reads=["fxt"], chan="fxt")
                S.emit_pass()

        def pass_c1():
            with ExitStack() as es:
                win1 = sbt(es, "win1", [128, 8, 544], BF16)
                S.dma("pool", lambda e: e.dma_start(out=win1[:], in_=o_w_in.rearrange("(c p) n -> p c n", p=128)),
                      writes=["win1"], chan="win1")
                wuq = sbt(es, "wuq", [128, 2, 1536], BF16)
                S.dma("pool", lambda e: e.dma_start(out=wuq[:], in_=o_w_uq.rearrange("(c p) n -> p c n", p=128)),
                      writes=["wuq"], chan="wuq")
                wukx = sbt(es, "wukx", [128, 2, 16, 96], BF16)
                wuv = sbt(es, "wuv", [128, 2, 16, 64], BF16)
                S.op("pool", lambda e: e.memset(wukx[:], 0.0), writes=["wukx"])
                ukv = o_w_ukv.rearrange("(c p) (h n) -> p c h n", p=128, n=128)
                for i in range(2):
                    S.dma("pool", lambda e, i=i: e.dma_start(out=wukx[:, i, :, 0:64], in_=ukv[:, i, :, 0:64]),
                          reads=["wukx"], writes=["wukx"], chan="wukx")
                    S.dma("pool", lambda e, i=i: e.dma_start(out=wuv[:, i, :, :], in_=ukv[:, i, :, 64:128]),
                          writes=["wuv"], chan="wuv")
                gn = load_gain_cols(es, "cgn", o_norm, D, 8, 32.0, "cgn")
                gql = load_gain_cols(es, "gql", o_q_lora_gain, 256, 2, 16.0, "gql")
                gkl = load_gain_cols(es, "gkl", o_kv_gain, 256, 2, 16.0, "gkl")
                g96 = sbt(es, "g96", [96, 2], F32)
                S.dma("sp", lambda e: e.dma_start(out=g96[:, 0:1], in_=o_q_gain.rearrange("(p o) -> p o", o=1)), writes=[("g96", 0)], chan="g96")
                S.dma("sp", lambda e: e.dma_start(out=g96[:, 1:2], in_=o_k_gain.rearrange("(p o) -> p o", o=1)), writes=[("g96", 1)], chan="g96")
                S.alias("g96", len(S.ops) - 1)
                xtl = [sbt(es, "cxt%d" % i, [128, 4, D], F32) for i in range(2)]
                xn = [sbt(es, "cxn%d" % i, [128, D], BF16) for i in range(2)]
                junk = [sbt(es, "cjunk%d" % i, [128, D], BF16) for i in range(2)]
                ssq = sbt(es, "cssq", [128, 4], F32)
                rst = sbt(es, "crst", [128, 4], F32)
                xnTl = [sbt(es, "cxnT%d" % i, [128, 8, 512], BF16) for i in range(2)]
                cst = [sbt(es, "ccst%d" % i, [96, 2, 512], F32) for i in range(2)]
                csq = [sbt(es, "csq%d" % i, [128, 512], BF16) for i in range(2)]
                craw = [sbt(es, "craw%d" % i, [128, 512], F32) for i in range(2)]
                crs = sbt(es, "crs", [128, 512], F32)
                cn = [sbt(es, "cn%d" % i, [128, 2, 512], BF16) for i in range(2)]
                krT = sbt(es, "krT", [32, 512], BF16)
                sq = [sbt(es, "hsq%d" % i, [96, 512], BF16) for i in range(2)]
                qg = [sbt(es, "hqg%d" % i, [96, 512], BF16) for i in range(2)]
                sd = [sbt(es, "hsd%d" % i, [96, 512], F32) for i in range(2)]
                t1 = [sbt(es, "ht1%d" % i, [96, 512], F32) for i in range(2)]
                t2 = [sbt(es, "ht2%d" % i, [96, 512], F32) for i in range(2)]
                qf = [sbt(es, "hqf%d" % i, [96, 512], BF16) for i in range(3)]
                vst = [sbt(es, "cvst%d" % i, [128, 16, 128], BF16) for i in range(2)]
                for i in range(2):
                    S.op("pool", lambda e, i=i: e.memset(vst[i][:, :, 64:128], 1.0), writes=[("cvst1", i)])
                ptr = pst(es, "cptr", BF16, 1024)
                cps = [pst(es, "cps%d" % i) for i in range(2)]
                qps = [pst(es, "cqps%d" % i) for i in range(2)]
                sps = pst(es, "csps")
                rps = pst(es, "crps")
                vps = pst(es, "cvps")
                ci = 0
                hi = 0
                vi = 0
                for g in range(G):
                    t0 = g * 512
                    b = g % 2
                    xt = xtl[b]
                    xnT = xnTl[b]
                    S.dma("sp", lambda e, t0=t0, xt=xt: e.dma_start(
                        out=xt[:], in_=x2[t0:t0 + 512, :].rearrange("(j p) d -> p j d", p=128)),
                        writes=[("cxt", b)], chan="cxt%d" % b)
                    S.dma("sp", lambda e, b=b, t0=t0: e.dma_start(
                        out=cst[b][:], in_=cs96[:, :, t0:t0 + 512].rearrange("a p t -> p a t")),
                        writes=[("ccst", b)], chan="ccst%d" % b)
                    S.op("pool", lambda e: e.memset(ssq[:], 0.0), writes=["cssq"])
                    for j in range(4):
                        S.op("act", lambda e, j=j, xt=xt: e.activation(
                            out=junk[j % 2][:], in_=xt[:, j, :], func=AF.Square, accum_out=ssq[:, j:j + 1]),
                            reads=[("cxt", b), "cssq"], writes=["cssq", ("cjunk", j % 2)])
                    S.op("act", lambda e: e.activation(out=rst[:], in_=ssq[:], func=AF.Sqrt, bias=epsc[:, 0:1], scale=1.0),
                         reads=["cssq", ("epsc", 0)], writes=["crst"])
                    S.op("dve", lambda e: e.reciprocal(out=rst[:], in_=rst[:]), reads=["crst"], writes=["crst"])
                    for j in range(4):
                        nb_ = j % 2
                        S.op("dve" if j % 2 else "pool", lambda e, j=j, nb_=nb_, xt=xt: e.tensor_scalar(
                            out=xn[nb_][:], in0=xt[:, j, :], scalar1=rst[:, j:j + 1], scalar2=None, op0=ALU.mult),
                            reads=[("cxt", b), "crst"], writes=[("cxn", nb_)])
                        for c in range(8):
                            S.op("pe", lambda e, c=c, nb_=nb_: e.transpose(
                                ptr[:, c * 128:(c + 1) * 128], xn[nb_][:, c * 128:(c + 1) * 128], cbm(CB_IDENT)),
                                reads=[("cxn", nb_), "cb"], writes=["cptr"])
                        S.op("dve", lambda e, j=j, xnT=xnT: e.tensor_tensor(
                            out=xnT[:, :, j * 128:(j + 1) * 128], in0=ptr[:, :].rearrange("p (c t) -> p c t", c=8),
                            in1=gn[:, :].unsqueeze(2).broadcast_to([128, 8, 128]), op=ALU.mult),
                            reads=["cptr", "cgn"], writes=[("cxnT", b, j)])
                    xk = [("cxnT", b, j) for j in range(4)]
                    for which, (col0, gl_, glk) in enumerate(((0, gql, "gql"), (256, gkl, "gkl"))):
                        for i in range(2):
                            cb_ = ci % 2
                            ci += 1
                            for c in range(8):
                                S.op("pe", lambda e, cb_=cb_, c=c, col0=col0, i=i, xnT=xnT: e.matmul(
                                    cps[cb_][:], lhsT=win1[:, c, col0 + i * 128:col0 + (i + 1) * 128], rhs=xnT[:, c, :],
                                    start=(c == 0), stop=(c == 7)),
                                    reads=xk + ["win1"], writes=[("cps", cb_)])
                            S.op("act", lambda e, cb_=cb_, i=i: e.activation(out=csq[i][:], in_=cps[cb_][:], func=AF.Square),
                                 reads=[("cps", cb_)], writes=[("csq", i)])
                            S.op("act", lambda e, cb_=cb_, i=i: e.activation(out=craw[i][:], in_=cps[cb_][:], func=AF.Copy),
                                 reads=[("cps", cb_)], writes=[("craw", i)])
                        for i in range(2):
                            S.op("pe", lambda e, i=i: e.matmul(sps[:], lhsT=cbm(CB_ONES), rhs=csq[i][:], start=(i == 0), stop=(i == 1)),
                                 reads=[("csq", i), "cb"], writes=["csps"])
                        S.op("act", lambda e: e.activation(out=crs[:], in_=sps[:], func=AF.Ln, bias=epsc[:, 2:3], scale=1.0),
                             reads=["csps", ("epsc", 2)], writes=["crs"])
                        S.op("act", lambda e: e.activation(out=crs[:], in_=crs[:], func=AF.Exp, scale=-0.5), reads=["crs"], writes=["crs"])
                        for i in range(2):
                            S.op("dve", lambda e, i=i, which=which, gl_=gl_: e.scalar_tensor_tensor(
                                out=cn[which][:, i, :], in0=craw[i][:], scalar=gl_[:, i:i + 1], in1=crs[:], op0=ALU.mult, op1=ALU.mult),
                                reads=[("craw", i), "crs", glk], writes=[("cn", which, i)])
                    cb_ = ci % 2
                    ci += 1
                    for c in range(8):
                        S.op("pe", lambda e, cb_=cb_, c=c, xnT=xnT: e.matmul(
                            cps[cb_][0:32, :], lhsT=win1[:, c, 512:544], rhs=xnT[:, c, :], start=(c == 0), stop=(c == 7)),
                            reads=xk + ["win1"], writes=[("cps", cb_)])
                    S.op("act", lambda e, cb_=cb_: e.activation(out=krT[:], in_=cps[cb_][0:32, :], func=AF.Copy),
                         reads=[("cps", cb_)], writes=["krT"])
                    pend = []
                    for isk in range(2):
                        for h in range(16):
                            qb = hi % 2
                            hi += 1
                            if isk == 0:
                                for i in range(2):
                                    S.op("pe", lambda e, qb=qb, h=h, i=i: e.matmul(
                                        qps[qb][0:96, :], lhsT=wuq[:, i, h * 96:(h + 1) * 96], rhs=cn[0][:, i, :],
                                        start=(i == 0), stop=(i == 1)),
                                        reads=[("cn", 0, i), "wuq"], writes=[("cqps", qb)])
                            else:
                                for i in range(2):
                                    S.op("pe", lambda e, qb=qb, h=h, i=i: e.matmul(
                                        qps[qb][0:96, :], lhsT=wukx[:, i, h, :], rhs=cn[1][:, i, :],
                                        start=(i == 0), stop=False),
                                        reads=[("cn", 1, i), "wukx"], writes=[("cqps", qb)])
                                S.op("pe", lambda e, qb=qb: e.matmul(
                                    qps[qb][0:96, :], lhsT=cbm(CB_E, 32, 96), rhs=krT[:], start=False, stop=True),
                                    reads=["krT", "cb"], writes=[("cqps", qb)])
                            def post(qb=qb, isk=isk, h=h, b=b, t0=t0, hi=hi):
                                S.op("act", lambda e, qb=qb: e.activation(out=sq[qb][:], in_=qps[qb][0:96, :], func=AF.Square),
                                     reads=[("cqps", qb)], writes=[("hsq", qb)])
                                S.op("act", lambda e, qb=qb, isk=isk: e.activation(
                                    out=qg[qb][:], in_=qps[qb][0:96, :], func=AF.Copy, scale=g96[:, isk:isk + 1]),
                                    reads=[("cqps", qb), "g96"], writes=[("hqg", qb)])
                                S.op("pe", lambda e, qb=qb: e.matmul(sps[0:96, :], lhsT=cbm(CB_ONES, 96, 96), rhs=sq[qb][:], start=True, stop=True),
                                     reads=[("hsq", qb), "cb"], writes=["csps"])
                                S.op("pe", lambda e, qb=qb: e.matmul(rps[0:96, :], lhsT=cbm(CB_R96, 96, 96), rhs=qg[qb][:], start=True, stop=True),
                                     reads=[("hqg", qb), "cb"], writes=["crps"])
                                S.op("act", lambda e, qb=qb: e.activation(
                                    out=sd[qb][:], in_=sps[0:96, :], func=AF.Ln, bias=epsc[0:96, 3:4], scale=1.0),
                                    reads=["csps", ("epsc", 3)], writes=[("hsd", qb)])
                                S.op(ENG_T1, lambda e, qb=qb, b=b: e.tensor_tensor(
                                    out=t1[qb][:], in0=qg[qb][:], in1=cst[b][:, 0, :], op=ALU.mult),
                                    reads=[("hqg", qb), ("ccst", b)], writes=[("ht1", qb)])
                                S.op("dve", lambda e, qb=qb, b=b: e.tensor_tensor(
                                    out=t2[qb][:], in0=rps[0:96, :], in1=cst[b][:, 1, :], op=ALU.mult),
                                    reads=["crps", ("ccst", b)], writes=[("ht2", qb)])
                                S.op(ENG_ADD, lambda e, qb=qb: e.tensor_tensor(
                                    out=t1[qb][:], in0=t1[qb][:], in1=t2[qb][:], op=ALU.add),
                                    reads=[("ht1", qb), ("ht2", qb)], writes=[("ht1", qb)])
                                S.op("act", lambda e, qb=qb: e.activation(out=sd[qb][:], in_=sd[qb][:], func=AF.Exp, scale=-0.5),
                                     reads=[("hsd", qb)], writes=[("hsd", qb)])
                                fb = hi % 3
                                S.op("dve", lambda e, qb=qb, fb=fb: e.tensor_tensor(
                                    out=qf[fb][:], in0=t1[qb][:], in1=sd[qb][:], op=ALU.mult),
                                    reads=[("ht1", qb), ("hsd", qb)], writes=[("hqf", fb)])
                                dst = k1 if isk else q1
                                S.dma("sp", lambda e, fb=fb, h=h, t0=t0, dst=dst: e.dma_start(
                                    out=dst[h * 96:(h + 1) * 96, t0:t0 + 512], in_=qf[fb][:]),
                                    reads=[("hqf", fb)], chan="hqf%d" % fb)
                            if pend:
                                pend.pop()()
                            pend.append(post)
                    pend.pop()()
                    for j in range(4):
                        vb = vi % 2
                        vi += 1
                        for half in range(2):
                            for i in range(2):
                                S.op("pe", lambda e, half=half, i=i, j=j: e.matmul(
                                    vps[:], lhsT=cn[1][:, i, j * 128:(j + 1) * 128],
                                    rhs=wuv[:, i, half * 8:(half + 1) * 8, :].rearrange("p h d -> p (h d)"),
                                    start=(i == 0), stop=(i == 1)),
                                    reads=[("cn", 1, i), "wuv"], writes=["cvps"])
                            S.op("act" if half else "dve", (lambda e, vb=vb, half=half: e.activation(
                                out=vst[vb][:, half * 8:(half + 1) * 8, 0:64], in_=vps[:, :].rearrange("p (h d) -> p h d", h=8), func=AF.Copy)) if half else
                                (lambda e, vb=vb, half=half: e.tensor_copy(
                                    out=vst[vb][:, half * 8:(half + 1) * 8, 0:64], in_=vps[:, :].rearrange("p (h d) -> p h d", h=8))),
                                reads=["cvps"], writes=[("cvst", vb, half)])
                        S.dma("sp", lambda e, vb=vb, t0=t0, j=j: e.dma_start(
                            out=v1[t0 + j * 128:t0 + (j + 1) * 128, :], in_=vst[vb][:].rearrange("p h d -> p (h d)")),
                            reads=[("cvst", vb, 0), ("cvst", vb, 1), ("cvst1", vb)], chan="cvst%d" % vb)
                S.emit_pass()

        def pass_c2():
            with ExitStack() as es:
                kT = [sbt(es, "dkT%d" % i, [96, T], BF16) for i in range(2)]
                qT = [sbt(es, "dqT%d" % i, [96, T], BF16) for i in range(2)]
                vt = [sbt(es, "dvt%d" % i, [128, NT, 128], BF16) for i in range(2)]
                ost = [sbt(es, "dost%d" % i, [64, T], BF16) for i in range(2)]
                pt = [sbt(es, "dpt%d" % i, [128, 1024], BF16) for i in range(4)]
                rec = [sbt(es, "drec%d" % i, [128, 512], F32) for i in range(2)]
                sps = [pst(es, "dsps%d" % i, F32, 1024) for i in range(3)]
                ops_ = [pst(es, "dops%d" % i) for i in range(2)]
                SC = float(np.sqrt(96.0))
                NQT = T // 512
                NKP = NT // 2
                assert (HB // 128) % 2 == 0
                steps = [(h, qt, kp) for h in range(16) for qt in range(NQT) for kp in range(NKP)]
                LAG = 2

                def load_head(h):
                    b = h % 2
                    S.dma("sp", lambda e, b=b, h=h: e.dma_start(out=kT[b][:], in_=k1[h * 96:(h + 1) * 96, :]),
                          writes=[("dkT", b)], chan="dkT%d" % b)
                    S.dma("sp", lambda e, b=b, h=h: e.dma_start(out=qT[b][:], in_=q1[h * 96:(h + 1) * 96, :]),
                          writes=[("dqT", b)], chan="dqT%d" % b)
                    S.dma("sp", lambda e, b=b, h=h: e.dma_start(
                        out=vt[b][:], in_=v1[:, h * 128:(h + 1) * 128].rearrange("(m p) c -> p m c", p=128)),
                        writes=[("dvt", b)], chan="dvt%d" % b)

                def emit_s(i):
                    h, qt, kp = steps[i]
                    b = h % 2
                    sb_ = i % 3
                    pb_ = i % 4
                    cross = ((qt * 512 < HB) != (kp * 256 < HB))
                    for u in range(2):
                        kb = 2 * kp + u
                        S.op("pe", lambda e, sb_=sb_, b=b, kb=kb, qt=qt, u=u: e.matmul(
                            sps[sb_][:, u * 512:(u + 1) * 512], lhsT=kT[b][:, kb * 128:(kb + 1) * 128],
                            rhs=qT[b][:, qt * 512:(qt + 1) * 512], start=True, stop=True),
                            reads=[("dkT", b), ("dqT", b)], writes=[("dsps", sb_)])
                    bias_ap = pc[:, 1:2] if cross else epsc[:, 4:5]
                    S.op("act", lambda e, sb_=sb_, pb_=pb_, bias_ap=bias_ap: e.activation(
                        out=pt[pb_][:], in_=sps[sb_][:], func=AF.Exp, bias=bias_ap, scale=SC),
                        reads=[("dsps", sb_), "pc", ("epsc", 4)], writes=[("dpt", pb_)])

                def emit_pv(i):
                    h, qt, kp = steps[i]
                    b = h % 2
                    pb_ = i % 4
                    ob = (h * NQT + qt) % 2
                    for u in range(2):
                        kb = 2 * kp + u
                        S.op("pe", lambda e, pb_=pb_, b=b, kb=kb, ob=ob, u=u: e.matmul(
                            ops_[ob][:], lhsT=vt[b][:, kb, :], rhs=pt[pb_][:, u * 512:(u + 1) * 512],
                            start=(kb == 0), stop=(kb == NT - 1)),
                            reads=[("dpt", pb_), ("dvt", b)], writes=[("dops", ob)])
                    if kp == NKP - 1:
                        S.op("dve", lambda e, ob=ob: e.reciprocal(out=rec[ob][64:128, :], in_=ops_[ob][64:128, :]),
                             reads=[("dops", ob)], writes=[("drec", ob)])
                        S.op("dve", lambda e, ob=ob, b=b, qt=qt: e.tensor_tensor(
                            out=ost[b][:, qt * 512:(qt + 1) * 512], in0=ops_[ob][0:64, :], in1=rec[ob][64:128, :], op=ALU.mult),
                            reads=[("dops", ob), ("drec", ob)], writes=[("dost", b)])
                        if qt == NQT - 1:
                            S.dma("sp", lambda e, b=b, h=h: e.dma_start(out=cat1[h * 64:(h + 1) * 64, :], in_=ost[b][:]),
                                  reads=[("dost", b)], chan="dost%d" % b)

                load_head(0)
                for i in range(len(steps) + LAG):
                    if i < len(steps):
                        h, qt, kp = steps[i]
                        if h + 1 < 16 and kp == 0 and qt == 1:
                            load_head(h + 1)
                        emit_s(i)
                    if i >= LAG:
                        emit_pv(i - LAG)
                S.emit_pass()

        if "a1" in passes:
            pass_a1()
        if "a2" in passes:
            pass_a2()
        if "a3" in passes:
            pass_outproj("a3", x_in, cat0, 6, e_w_out, x1)
        if "b" in passes:
            pass_ffn(0, x1, x2)
        if "c1" in passes:
            pass_c1()
        if "c2" in passes:
            pass_c2()
        if "c3" in passes:
            pass_outproj("c3", x2, cat1, 8, o_w_out, x3)
        if "d" in passes:
            pass_ffn(1, x3, y_out)

        S.final_wait()
    return nc


def _rope_tables(T, L):
    pos = (np.arange(T) % L).astype(np.float32)

    def tab(Dr):
        inv = np.power(np.float32(10000.0), -np.arange(0, Dr, 2, dtype=np.float32) / np.float32(Dr)).astype(np.float32)
        ang = (pos[:, None] * inv[None, :]).astype(np.float32)
        return np.cos(ang).astype(np.float32).T, np.sin(ang).astype(np.float32).T

    c64, s64 = tab(64)
    c32, s32 = tab(32)
    cs64 = np.stack([np.tile(c64, (4, 1)), np.tile(s64, (4, 1))]).astype(np.float32)
    c96 = np.concatenate([np.ones((64, T), np.float32), c32, c32])
    s96 = np.concatenate([np.zeros((64, T), np.float32), s32, s32])
    cs96 = np.stack([c96, s96]).astype(np.float32)
    return np.ascontiguousarray(cs64), np.ascontiguousarray(cs96)


def _const_mats(is_prompt):
    m = np.zeros((NCB, 128, 128), np.float32)
    m[CB_IDENT] = np.eye(128)
    k = np.arange(128)
    m[CB_B64] = (k[:, None] // 64 == k[None, :] // 64)
    for o in range(128):
        if o % 64 < 32:
            m[CB_R64][o + 32, o] = -1.0
        else:
            m[CB_R64][o - 32, o] = 1.0
    m[CB_ONES] = 1.0
    for i in range(16):
        m[CB_R96][80 + i, 64 + i] = -1.0
        m[CB_R96][64 + i, 80 + i] = 1.0
    for i in range(32):
        m[CB_E][i, 64 + i] = 1.0
    j = k[:, None]
    i = k[None, :]
    ge = (j >= i).astype(np.float32)
    le = (j <= i).astype(np.float32)
    gef = ge * (j >= 64)
    lel = le * (j < 64)
    m[CB_GE], m[CB_LE], m[CB_GEF], m[CB_LEL] = ge, le, gef, lel
    if is_prompt:
        m[CB_GEB], m[CB_LEB] = gef, lel
        m[CB_GEAB], m[CB_LEAB] = 0.0, 0.0
    else:
        m[CB_GEB], m[CB_LEB] = ge, le
        m[CB_GEAB], m[CB_LEAB] = ge, le
    out = np.ascontiguousarray(m.transpose(1, 0, 2).reshape(128, NCB * 128))
    pcv = np.zeros((128, 4), np.float32)
    pcv[:, 0] = 0.0 if is_prompt else 1.0
    pcv[:, 1] = NEGB if is_prompt else 0.0
    return out, pcv


_WNAMES = ("e_norm", "e_w_in", "e_a_q_gain", "e_a_k_gain", "e_a_sink", "e_b_q_gain", "e_b_k_gain", "e_w_out",
           "o_norm", "o_w_in", "o_q_lora_gain", "o_w_uq", "o_kv_gain", "o_w_ukv", "o_q_gain", "o_k_gain", "o_w_out")
_FNAMES = ("f_norm", "f_w_up", "f_conv_w", "f_conv_b", "f_w_down")


def make_in_maps(streams, T, weights):
    wmap = {}
    for n in _WNAMES:
        a = np.asarray(weights[n], np.float32)
        wmap[n] = np.ascontiguousarray(a.reshape(a.shape[1:]))
    for n in _FNAMES:
        wmap[n] = np.ascontiguousarray(np.asarray(weights[n], np.float32))
    tabs = {}
    in_maps = []
    for (xs, is_p) in streams:
        if is_p not in tabs:
            cs64, cs96 = _rope_tables(T, T // 2 if is_p else T)
            cm, pcv = _const_mats(is_p)
            tabs[is_p] = (cs64, cs96, cm, pcv)
        cs64, cs96, cm, pcv = tabs[is_p]
        m = dict(wmap)
        m.update(x=np.ascontiguousarray(xs, dtype=np.float32), cs64=cs64, cs96=cs96, consts=cm, percore=pcv)
        in_maps.append(m)
    return in_maps


_PROG = {}


def kernel(**inputs):
    T = 8192
    xp = np.asarray(inputs["x_prompt"], np.float32)
    xs = np.asarray(inputs["x_sample"], np.float32)
    streams = []
    for c in range(4):
        streams.append((xp[2 * c:2 * c + 2].reshape(T, D), True))
    for c in range(4):
        streams.append((xs[c], False))
    in_maps = make_in_maps(streams, T, inputs)
    if T not in _PROG:
        _PROG[T] = build_program(T)
    res = run_bass_kernel_spmd(_PROG[T], in_maps, core_ids=list(range(8)))
    ys = [np.asarray(r["y"], np.float32) for r in res.results]
    y_prompt = np.stack([ys[c].reshape(2, T // 2, D) for c in range(4)]).reshape(8, T // 2, D)
    y_sample = np.stack(ys[4:8])
    return (y_prompt, y_sample)
```
